# Optimizing a Trainium2 kernel written in Bass

```python
import math
import jax, jax.numpy as jnp
from jax import lax
import numpy as np

D_MODEL = 2048
BATCH = 8
SEQ = 4096
DEPTH = 1
DEC_BATCH = 4
DEC_SEQ = 8192
PAST_LEN = 128

GRID_W = 64
MIX_WIDTH = D_MODEL
HY_WIDTH = MIX_WIDTH // 2
NAT_WIDTH = MIX_WIDTH - HY_WIDTH
HY_ORDER = 2
HY_SHORT = 3
HY_EMB = 33
HY_FILTER_HIDDEN = 64
HY_FAST_DECAY = 0.3
HY_SLOW_DECAY = 1.5
HY_DECAY_TARGET = 1e-2
NAT_HEADS = 16
NAT_HEAD_DIM = NAT_WIDTH // NAT_HEADS
NAT_KH_MAX = 8
NAT_KW = 16
D_FF = 4 * D_MODEL
NORM_EPS = 1e-5

kernel_name = "hymba_hyena_natten_encoder"


def rms_norm(x, g):
    xf = x.astype(jnp.float32)
    y = xf * lax.rsqrt(jnp.mean(xf * xf, axis=-1, keepdims=True) + NORM_EPS)
    return (y * g.astype(jnp.float32)).astype(x.dtype)


def short_conv(u, w, b):
    up = jnp.pad(u, ((0, 0), (1, 1), (0, 0)))
    return up[:, :-2] * w[0] + up[:, 1:-1] * w[1] + up[:, 2:] * w[2] + b


def hyena_filters(L, w1, b1, w2, b2, w3, b3, freq, w4):
    f32 = jnp.float32
    t = jnp.linspace(0.0, 1.0, L, dtype=f32)[:, None]
    bands = (HY_EMB - 1) // 2
    w_ang = (2.0 * math.pi / L) * jnp.arange(L, dtype=f32)
    f = jnp.linspace(1e-4, bands - 1, bands, dtype=f32)
    ang = w_ang[:, None] * f[None, :]
    z = jnp.concatenate([t, jnp.cos(ang), -jnp.sin(ang)], axis=-1)
    fr = freq.astype(f32)
    h = jnp.sin(fr * (z @ w1.astype(f32) + b1.astype(f32)))
    h = jnp.sin(fr * (h @ w2.astype(f32) + b2.astype(f32)))
    h = jnp.sin(fr * (h @ w3.astype(f32) + b3.astype(f32)))
    h = (h @ w4.astype(f32)).reshape(L, HY_ORDER, 2, HY_WIDTH)
    max_decay = math.log(HY_DECAY_TARGET) / HY_FAST_DECAY
    min_decay = math.log(HY_DECAY_TARGET) / HY_SLOW_DECAY
    deltas = jnp.linspace(min_decay, max_decay, HY_WIDTH, dtype=f32)
    decay = jnp.exp(-t * jnp.abs(deltas)[None, :])
    h = h * decay[:, None, None, :]
    h_fwd = h[:, :, 0]
    h_bwd = h[:, :, 1]
    k = jnp.concatenate([h_fwd, jnp.zeros((1, HY_ORDER, HY_WIDTH), f32), h_bwd[:0:-1]], axis=0)
    return jnp.fft.rfft(k, axis=0)


def hyena_mixer(u, k_hat, bias):
    L = u.shape[1]
    v, x1, x2 = jnp.split(u.astype(jnp.float32), 3, axis=-1)
    bias = bias.astype(jnp.float32)
    z = v
    for o, gate in enumerate((x1, x2)):
        z_hat = jnp.fft.rfft(z, n=2 * L, axis=1)
        conv = jnp.fft.irfft(z_hat * k_hat[:, o][None], n=2 * L, axis=1)[:, :L]
        z = gate * (conv + z * bias[o])
    return z.astype(u.dtype)


def neighborhood_attention(q, k, v, rpb):
    B, L, H, hd = q.shape
    rows = L // GRID_W
    kh = min(NAT_KH_MAX, rows)
    qg = q.reshape(B, rows, GRID_W, H, hd)
    kg = k.reshape(B, rows, GRID_W, H, hd)
    vg = v.reshape(B, rows, GRID_W, H, hd)
    cols = jnp.arange(GRID_W)
    col_start = jnp.clip(cols - NAT_KW // 2, 0, GRID_W - NAT_KW)
    col_idx = col_start[:, None] + jnp.arange(NAT_KW)[None, :]
    col_off = col_idx - cols[:, None] + (NAT_KW - 1)
    scale = hd ** -0.5

    def row_block(i):
        rs = jnp.clip(i - kh // 2, 0, rows - kh)
        q_row = lax.dynamic_index_in_dim(qg, i, axis=1, keepdims=False)
        k_rows = lax.dynamic_slice_in_dim(kg, rs, kh, axis=1)
        v_rows = lax.dynamic_slice_in_dim(vg, rs, kh, axis=1)
        k_win = k_rows[:, :, col_idx]
        v_win = v_rows[:, :, col_idx]
        s = jnp.einsum('bjhd,bajkhd->bhjak', q_row, k_win).astype(jnp.float32) * scale
        row_off = rs + jnp.arange(kh) - i + (NAT_KH_MAX - 1)
        bias = rpb[:, row_off][:, :, col_off]
        s = s + jnp.transpose(bias, (0, 2, 1, 3)).astype(jnp.float32)[None]
        p = jax.nn.softmax(s.reshape(B, H, GRID_W, kh * NAT_KW), axis=-1)
        p = p.reshape(B, H, GRID_W, kh, NAT_KW).astype(v.dtype)
        return jnp.einsum('bhjak,bajkhd->bjhd', p, v_win)

    out = lax.map(row_block, jnp.arange(rows))
    return jnp.moveaxis(out, 0, 1).reshape(B, L, H * hd)


def encoder_trunk(x, norm_mix_g, w_in, hy_conv_w, hy_conv_b, hy_pe_w1, hy_pe_b1, hy_pe_w2,
                  hy_pe_b2, hy_pe_w3, hy_pe_b3, hy_pe_freq, hy_pe_w4, hy_bias, nat_rpb,
                  gnorm_hy, gnorm_nat, w_out, norm_mlp_g, w_up, w_down, norm_f_g):
    B, L, _ = x.shape
    h = x
    for l in range(DEPTH):
        k_hat = hyena_filters(L, hy_pe_w1[l], hy_pe_b1[l], hy_pe_w2[l], hy_pe_b2[l],
                              hy_pe_w3[l], hy_pe_b3[l], hy_pe_freq[l], hy_pe_w4[l])
        a = rms_norm(h, norm_mix_g[l])
        proj = a @ w_in[l]
        u_hy = short_conv(proj[..., :3 * HY_WIDTH], hy_conv_w[l], hy_conv_b[l])
        y_hy = hyena_mixer(u_hy, k_hat, hy_bias[l])
        q, k, v = jnp.split(proj[..., 3 * HY_WIDTH:], 3, axis=-1)
        q = q.reshape(B, L, NAT_HEADS, NAT_HEAD_DIM)
        k = k.reshape(B, L, NAT_HEADS, NAT_HEAD_DIM)
        v = v.reshape(B, L, NAT_HEADS, NAT_HEAD_DIM)
        y_nat = neighborhood_attention(q, k, v, nat_rpb[l])
        mix = jnp.concatenate([rms_norm(y_hy, gnorm_hy[l]), rms_norm(y_nat, gnorm_nat[l])], axis=-1)
        h = h + mix @ w_out[l]
        m = rms_norm(h, norm_mlp_g[l])
        h = h + jnp.square(jax.nn.relu(m @ w_up[l])) @ w_down[l]
    return rms_norm(h, norm_f_g)


def setup_inputs(seed: int = 0) -> dict:
    key = jax.random.key(seed)
    ks = jax.random.split(key, 24)
    f32 = jnp.float32
    nrm = lambda k, shape, s: jax.random.normal(k, shape, f32) * s
    H2 = HY_FILTER_HIDDEN
    return {
        "x_prompt": jax.random.normal(ks[0], (BATCH, SEQ, D_MODEL), f32),
        "x_sample": jax.random.normal(ks[1], (DEC_BATCH, DEC_SEQ, D_MODEL), f32),
        "norm_mix_g": 1.0 + nrm(ks[2], (DEPTH, D_MODEL), 0.02),
        "w_in": nrm(ks[3], (DEPTH, D_MODEL, 6 * HY_WIDTH), D_MODEL ** -0.5),
        "hy_conv_w": nrm(ks[4], (DEPTH, HY_SHORT, 3 * HY_WIDTH), HY_SHORT ** -0.5),
        "hy_conv_b": nrm(ks[5], (DEPTH, 3 * HY_WIDTH), 0.02),
        "hy_pe_w1": nrm(ks[6], (DEPTH, HY_EMB, H2), HY_EMB ** -0.5),
        "hy_pe_b1": nrm(ks[7], (DEPTH, H2), 0.02),
        "hy_pe_w2": nrm(ks[8], (DEPTH, H2, H2), H2 ** -0.5),
        "hy_pe_b2": nrm(ks[9], (DEPTH, H2), 0.02),
        "hy_pe_w3": nrm(ks[10], (DEPTH, H2, H2), H2 ** -0.5),
        "hy_pe_b3": nrm(ks[11], (DEPTH, H2), 0.02),
        "hy_pe_freq": 1.0 + nrm(ks[12], (DEPTH, H2), 0.01),
        "hy_pe_w4": nrm(ks[13], (DEPTH, H2, HY_ORDER * 2 * HY_WIDTH), 0.1 * H2 ** -0.5),
        "hy_bias": nrm(ks[14], (DEPTH, HY_ORDER, HY_WIDTH), 1.0),
        "nat_rpb": nrm(ks[15], (DEPTH, NAT_HEADS, 2 * NAT_KH_MAX - 1, 2 * NAT_KW - 1), 0.02),
        "gnorm_hy": 1.0 + nrm(ks[16], (DEPTH, HY_WIDTH), 0.02),
        "gnorm_nat": 1.0 + nrm(ks[17], (DEPTH, NAT_WIDTH), 0.02),
        "w_out": nrm(ks[18], (DEPTH, MIX_WIDTH, D_MODEL), MIX_WIDTH ** -0.5),
        "norm_mlp_g": 1.0 + nrm(ks[19], (DEPTH, D_MODEL), 0.02),
        "w_up": nrm(ks[20], (DEPTH, D_MODEL, D_FF), D_MODEL ** -0.5),
        "w_down": nrm(ks[21], (DEPTH, D_FF, D_MODEL), D_FF ** -0.5),
        "norm_f_g": 1.0 + nrm(ks[22], (D_MODEL,), 0.02),
    }


def reference(x_prompt, x_sample, norm_mix_g, w_in, hy_conv_w, hy_conv_b, hy_pe_w1, hy_pe_b1,
              hy_pe_w2, hy_pe_b2, hy_pe_w3, hy_pe_b3, hy_pe_freq, hy_pe_w4, hy_bias, nat_rpb,
              gnorm_hy, gnorm_nat, w_out, norm_mlp_g, w_up, w_down, norm_f_g):
    weights = (norm_mix_g, w_in, hy_conv_w, hy_conv_b, hy_pe_w1, hy_pe_b1, hy_pe_w2, hy_pe_b2,
               hy_pe_w3, hy_pe_b3, hy_pe_freq, hy_pe_w4, hy_bias, nat_rpb, gnorm_hy, gnorm_nat,
               w_out, norm_mlp_g, w_up, w_down, norm_f_g)
    y_prompt = encoder_trunk(x_prompt, *weights)
    y_sample = encoder_trunk(x_sample, *weights)
    return (y_prompt, y_sample)
```

```python
import contextlib
import math
import numpy as np
import ml_dtypes
import concourse.bass as bass
import concourse.mybir as mybir
from concourse.bass_utils import run_bass_kernel_spmd

F32 = mybir.dt.float32
BF16 = mybir.dt.bfloat16
I32 = mybir.dt.int32
ALU = mybir.AluOpType
AF = mybir.ActivationFunctionType
NPBF = ml_dtypes.bfloat16

COMPUTE = ("tensor", "vector", "scalar", "gpsimd")
DMAQ = ("sync", "gpq", "actq")
NDMASEM = 8
NEG = -30000.0
EPS = 1e-5


class _Op:
    __slots__ = ("eng", "fn", "deps", "idx", "waited", "semslot", "semval")

    def __init__(self, eng, fn):
        self.eng = eng
        self.fn = fn
        self.deps = set()
        self.waited = False


class Prog:
    def __init__(self, nc, stack):
        self.nc = nc
        self.sems = {}
        for e in COMPUTE:
            self.sems[e] = stack.enter_context(nc.semaphore("s_" + e))
        for q in DMAQ:
            self.sems[q] = [stack.enter_context(nc.semaphore("s_%s%d" % (q, i))) for i in range(NDMASEM)]
        self.count = {e: 0 for e in COMPUTE}
        self.dcount = {q: [0] * NDMASEM for q in DMAQ}
        self.dnext = {q: 0 for q in DMAQ}
        self.reset_phase()

    def reset_phase(self):
        self.ops = []
        self.lastw = {}
        self.rd_c = {}
        self.rd_d = {}

    def op(self, eng, fn, reads=(), writes=()):
        o = _Op(eng, fn)
        o.idx = len(self.ops)
        deps = o.deps
        for k in reads:
            w = self.lastw.get(k)
            if w is not None:
                deps.add(w)
        for k in writes:
            w = self.lastw.get(k)
            if w is not None:
                deps.add(w)
            rc = self.rd_c.get(k)
            if rc:
                deps.update(rc.values())
            rd = self.rd_d.get(k)
            if rd:
                deps.update(rd)
        if eng in COMPUTE:
            for k in reads:
                self.rd_c.setdefault(k, {})[eng] = o.idx
        else:
            for k in reads:
                self.rd_d.setdefault(k, []).append(o.idx)
        for k in writes:
            self.lastw[k] = o.idx
            self.rd_c[k] = {}
            self.rd_d[k] = []
        deps.discard(o.idx)
        self.ops.append(o)
        return o

    def emit(self):
        nc = self.nc
        ops = self.ops
        phys = {"tensor": "tensor", "vector": "vector", "scalar": "scalar", "gpsimd": "gpsimd",
                "sync": "sync", "gpq": "gpsimd", "actq": "scalar"}
        for o in ops:
            best = {}
            dl = []
            for d in o.deps:
                od = ops[d]
                if od.eng in COMPUTE:
                    if od.eng == "tensor" and o.eng == "tensor":
                        continue
                    if od.eng not in best or best[od.eng] < d:
                        best[od.eng] = d
                else:
                    dl.append(d)
            o.deps = set(best.values()) | set(dl)
            for d in o.deps:
                ops[d].waited = True
        lastc = {}
        for o in ops:
            if o.eng in COMPUTE:
                lastc[o.eng] = o
        for o in lastc.values():
            o.waited = True
        for o in ops:
            if o.eng in COMPUTE:
                if o.waited:
                    self.count[o.eng] += 1
                    o.semval = self.count[o.eng]
            else:
                slot = self.dnext[o.eng] % NDMASEM
                self.dnext[o.eng] += 1
                self.dcount[o.eng][slot] += 16
                o.semslot = slot
                o.semval = self.dcount[o.eng][slot]
        streams = {"tensor": [], "vector": [], "scalar": [], "gpsimd": [], "sync": []}
        for o in ops:
            streams[phys[o.eng]].append(o)
        sems = self.sems

        def run(e, lst):
            seen = {}
            for o in lst:
                waits = []
                if o.eng in DMAQ and o.semval > 16:
                    waits.append((sems[o.eng][o.semslot], o.semval - 16))
                for d in sorted(o.deps):
                    od = ops[d]
                    if od.eng in COMPUTE:
                        waits.append((sems[od.eng], od.semval))
                    else:
                        waits.append((sems[od.eng][od.semslot], od.semval))
                for (s, v) in waits:
                    key = id(s)
                    if seen.get(key, 0) >= v:
                        continue
                    seen[key] = v
                    e.wait_ge(s, v)
                ins = o.fn(e)
                if o.eng in COMPUTE:
                    if o.waited:
                        ins.then_inc(sems[o.eng], 1)
                else:
                    ins.then_inc(sems[o.eng][o.semslot], 16)
            for c in COMPUTE:
                if self.count[c] > seen.get(id(sems[c]), 0):
                    e.wait_ge(sems[c], self.count[c])
            for q in DMAQ:
                for i in range(NDMASEM):
                    if self.dcount[q][i] > seen.get(id(sems[q][i]), 0):
                        e.wait_ge(sems[q][i], self.dcount[q][i])

        with nc.Block() as block:
            @block.tensor
            def _(e):
                run(e, streams["tensor"])

            @block.vector
            def _(e):
                run(e, streams["vector"])

            @block.scalar
            def _(e):
                run(e, streams["scalar"])

            @block.gpsimd
            def _(e):
                run(e, streams["gpsimd"])

            @block.sync
            def _(e):
                run(e, streams["sync"])
        self.reset_phase()


def dma(P, q, out, in_, reads, writes, slow=False):
    if slow:
        return P.op(q, lambda e: e.dma_start(out=out, in_=in_, allow_slow_non_contiguous=True), reads, writes)
    return P.op(q, lambda e: e.dma_start(out=out, in_=in_), reads, writes)


def mm(P, out, lhsT, rhs, start, stop, reads, writes):
    return P.op("tensor", lambda e: e.matmul(out, lhsT=lhsT, rhs=rhs, start=start, stop=stop), reads, writes)


def tr(P, out, in_, ident, reads, writes):
    return P.op("tensor", lambda e: e.transpose(out, in_, ident), reads, writes)


def act(P, out, in_, func, reads, writes, bias=None, scale=None, accum_out=None):
    kw = {}
    if bias is not None:
        kw["bias"] = bias
    if scale is not None:
        kw["scale"] = scale
    if accum_out is not None:
        kw["accum_out"] = accum_out
    return P.op("scalar", lambda e: e.activation(out=out, in_=in_, func=func, **kw), reads, writes)


def ts(P, eng, out, in0, s1, s2, op0, op1, reads, writes):
    if s2 is None:
        return P.op(eng, lambda e: e.tensor_scalar(out=out, in0=in0, scalar1=s1, scalar2=None, op0=op0), reads, writes)
    return P.op(eng, lambda e: e.tensor_scalar(out=out, in0=in0, scalar1=s1, scalar2=s2, op0=op0, op1=op1), reads, writes)


def tt(P, eng, out, in0, in1, op, reads, writes):
    return P.op(eng, lambda e: e.tensor_tensor(out=out, in0=in0, in1=in1, op=op), reads, writes)


def stt(P, eng, out, in0, scalar, in1, op0, op1, reads, writes):
    eng = "vector"
    return P.op(eng, lambda e: e.scalar_tensor_tensor(out=out, in0=in0, scalar=scalar, in1=in1, op0=op0, op1=op1),
                reads, writes)


def cp(P, eng, out, in_, reads, writes):
    if eng == "scalar":
        return P.op(eng, lambda e: e.activation(out=out, in_=in_, func=AF.Copy), reads, writes)
    return P.op(eng, lambda e: e.tensor_copy(out=out, in_=in_), reads, writes)


def mset(P, eng, ap, val, writes):
    return P.op(eng, lambda e: e.memset(ap, val), (), writes)


class Cfg:
    def __init__(self, D=2048, CH=1024, NH=16, DFF=8192):
        self.D = D
        self.CH = CH
        self.NH = NH
        self.CN = NH * 64
        self.DFF = DFF
        self.T = 8192
        self.KD = D // 128
        self.NCOL = 3 * CH + 3 * self.CN
        self.CB = min(512, CH)
        self.NCB = CH // self.CB
        self.DC = min(512, D)
        self.KF = DFF // 128
        self.KG = min(16, self.KF)
        self.MIXC = CH + self.CN
        self.KM = self.MIXC // 128


FULL = Cfg()

NAT_TYPES = ["int", "m0", "m1", "m30", "m31", "m32", "m33", "m62", "m63"]


def nat_block(m):
    if m == 0:
        return "m0", 0, 8
    if m == 1:
        return "m1", 0, 8
    if m == 62:
        return "m62", 120, 8
    if m == 63:
        return "m63", 120, 8
    if m in (31, 32, 33):
        return "m%d" % m, 2 * m - 6, 14
    if m == 30:
        return "m30", 56, 9
    return "int", 2 * m - 4, 9


def _n1p(ctype):
    n1 = np.arange(64)
    if ctype == 0:
        return n1
    return np.where(n1 < 32, n1, n1 + 32)


def fft_tables(ctype):
    N = 16384
    n2 = np.arange(128)[:, None, None]
    k1 = (np.arange(64) + 0.5)[None, None, :]
    out = {}
    for name, pl in (("nat", _n1p(0)), ("dat", _n1p(ctype))):
        n1p = pl[None, :, None]
        th = 2.0 * np.pi * np.mod((128 * n1p + n2) * k1, N) / N
        e = np.stack([np.cos(th), -np.sin(th)], axis=1)
        out["E" + name] = np.ascontiguousarray(e.transpose(2, 0, 1, 3)).reshape(64, 128 * 128)
        if name == "dat":
            g = np.stack([np.cos(th) * 2.0 / N, -np.sin(th) * 2.0 / N], axis=1)
            out["G"] = np.ascontiguousarray(g.transpose(1, 3, 0, 2)).reshape(128, 128 * 64)
    a = np.arange(128)
    ph = 2.0 * np.pi * (np.outer(a, a) % 128) / 128.0
    wre, wim = np.cos(ph), -np.sin(ph)
    out["W"] = np.concatenate([wre, wim, -wim, -wre], axis=1)
    return {k: v.astype(NPBF) for k, v in out.items()}


def filter_consts(ctype):
    T = 8192
    L = T if ctype == 0 else 4096
    f32 = np.float32
    j = np.arange(T)
    valid = j < L
    jj = np.where(valid, j, 0)
    t = (jj.astype(np.float64) / (L - 1)).astype(f32)
    bands = 16
    w_ang = (f32(2.0 * math.pi / L) * jj.astype(f32)).astype(f32)
    f = np.linspace(1e-4, bands - 1, bands, dtype=f32)
    ang = (w_ang[:, None] * f[None, :]).astype(f32)
    z = np.concatenate([t[:, None], np.cos(ang), -np.sin(ang)], axis=-1).astype(f32)
    zT = np.ascontiguousarray(z.T)
    tcol = np.ascontiguousarray(t.reshape(64, 128))
    mf = valid.astype(f32).reshape(64, 128)
    mb = mf.copy()
    mb[0, 0] = 0.0
    return {"zT": zT, "tcol": tcol, "mf": np.ascontiguousarray(mf), "mb": np.ascontiguousarray(mb)}


def nat_masks(ctype):
    def window(i):
        if ctype == 0:
            return int(np.clip(i - 4, 0, 120))
        base = 0 if i < 64 else 64
        return base + int(np.clip(i - base - 4, 0, 56))
    cols = np.arange(64)
    cs = np.clip(cols - 8, 0, 48)
    colok = (cols[None, :] >= cs[:, None]) & (cols[None, :] < cs[:, None] + 16)
    reps = {"int": 10, "m0": 0, "m1": 1, "m30": 30, "m31": 31, "m32": 32, "m33": 33, "m62": 62, "m63": 63}
    out = np.full((len(NAT_TYPES), 128, 896), NEG, np.float32)
    for ti, tn in enumerate(NAT_TYPES):
        m = reps[tn]
        _, ks, nr = nat_block(m)
        for ri in range(2):
            i = 2 * m + ri
            rs = window(i)
            for a in range(nr):
                r = ks + a
                if rs <= r < rs + 8:
                    blk = np.where(colok, 0.0, NEG)
                    out[ti, ri * 64:(ri + 1) * 64, a * 64:(a + 1) * 64] = blk
    return out.astype(NPBF)


def nat_tt_index():
    p = np.arange(128)
    ri, j = p // 64, p % 64
    B = np.arange(15)
    kc = np.arange(64)
    ro = B[None, :, None] - ri[:, None, None]
    co = 15 + kc[None, None, :] - j[:, None, None]
    ok = (ro >= 0) & (ro <= 14) & (co >= 0) & (co <= 30)
    idx = np.clip(ro, 0, 14) * 31 + np.clip(co, 0, 30)
    return idx.reshape(128, 960), ok.reshape(128, 960)


def build_program(cfg, phases=("w", "p1", "hy", "nat", "p3"), debug=False):
    c = cfg
    D, CH, NH, CN, DFF, T, KD, NCOL, CB, NCB = c.D, c.CH, c.NH, c.CN, c.DFF, c.T, c.KD, c.NCOL, c.CB, c.NCB
    nc = bass.Bass("TRN2", target_bir_lowering=False)

    def din(name, shape, dt=F32):
        return nc.dram_tensor(name, list(shape), dt, kind="ExternalInput").ap()

    okind = "ExternalOutput" if debug else "Internal"

    def dscr(name, shape, dt=BF16):
        if debug and name in debug:
            return nc.dram_tensor(name, list(shape), dt, kind="ExternalOutput").ap()
        return nc.dram_tensor(name, list(shape), dt).ap()

    x = din("x", [T, D])
    norm_mix_g = din("norm_mix_g", [D])
    w_in = din("w_in", [D, NCOL])
    hy_conv_w = din("hy_conv_w", [3, 3 * CH])
    hy_conv_b = din("hy_conv_b", [3 * CH])
    pe_w1 = din("pe_w1", [33, 64])
    pe_b1 = din("pe_b1", [64])
    pe_w2 = din("pe_w2", [64, 64])
    pe_b2 = din("pe_b2", [64])
    pe_w3 = din("pe_w3", [64, 64])
    pe_b3 = din("pe_b3", [64])
    pe_freq = din("pe_freq", [64])
    pe_w4 = din("pe_w4", [64, 4 * CH])
    hy_bias = din("hy_bias", [2, CH])
    nat_tt = din("nat_tt", [NH, 128, 960])
    gnorm = din("gnorm", [c.MIXC])
    w_out = din("w_out", [c.MIXC, D])
    norm_mlp_g = din("norm_mlp_g", [D])
    w_up = din("w_up", [D, DFF])
    w_down = din("w_down", [DFF, D])
    norm_f_g = din("norm_f_g", [D])
    c_ident = din("c_ident", [128, 128], BF16)
    c_Enat = din("c_Enat", [64, 128 * 128], BF16)
    c_Edat = din("c_Edat", [64, 128 * 128], BF16)
    c_G = din("c_G", [128, 128 * 64], BF16)
    c_W = din("c_W", [128, 512], BF16)
    c_zT = din("c_zT", [33, T])
    c_tcol = din("c_tcol", [64, 128])
    c_mf = din("c_mf", [64, 128])
    c_mb = din("c_mb", [64, 128])
    c_nad = din("c_nad", [CH])
    c_mask = din("c_mask", [len(NAT_TYPES), 128, 896], BF16)
    c_flag = din("c_flag", [1])
    y = nc.dram_tensor("y", [T, D], F32, kind="ExternalOutput").ap()
    Wb_in = dscr("Wb_in", [D, NCOL])
    Wb_out = dscr("Wb_out", [c.MIXC, D])
    Wb_up = dscr("Wb_up", [D, DFF])
    Wb_down = dscr("Wb_down", [DFF, D])
    u_hy = dscr("u_hy", [T + 2, 3 * CH])
    qT = dscr("qT", [CN, T])
    kT = dscr("kT", [CN, T])
    vn1 = dscr("vn1", [T, NH, 65])
    S1 = dscr("S1", [4, 2, 64, 128, CB])
    S2 = dscr("S2", [2, 128, 64, CB])
    KH = dscr("KH", [2, NCB, 128, 64, 2, CB])
    z1d = dscr("z1d", [T, CH])
    y_hy = dscr("y_hy", [T, CH])
    y_nat = dscr("y_nat", [T, CN])

    with contextlib.ExitStack() as top:
        P = Prog(nc, top)
        if "w" in phases:
            _phase_w(nc, P, c, locals())
        if "p1" in phases:
            _phase_p1(nc, P, c, locals())
        if "hy" in phases:
            _phase_hy(nc, P, c, locals())
        if "nat" in phases:
            _phase_nat(nc, P, c, locals())
        if "p3" in phases:
            _phase_p3(nc, P, c, locals())
    return nc


def _colvec_load(P, q, tile, key, src, n):
    v = src.rearrange("(k p) -> p k", p=128)
    dma(P, q, tile[:, 0:n], v, (), [key], slow=True)


def _phase_w(nc, P, c, A):
    D, CH, CN, DFF, NCOL = c.D, c.CH, c.CN, c.DFF, c.NCOL
    with contextlib.ExitStack() as st:
        S = lambda n, sh, dt: st.enter_context(nc.sbuf_tensor("w_" + n, sh, dt))
        gm = S("gm", [128, c.KD], F32)
        gq = S("gq", [128, c.KD], F32)
        gl = S("gl", [128, c.KD], F32)
        gh = S("gh", [128, c.KM], F32)
        one = S("one", [128, 1], F32)
        _colvec_load(P, "sync", gm, "gm", A["norm_mix_g"], c.KD)
        _colvec_load(P, "sync", gl, "gl", A["norm_mlp_g"], c.KD)
        _colvec_load(P, "sync", gh, "gh", A["gnorm"], c.KM)
        mset(P, "vector", one[:], 1.0, ["one"])
        ts(P, "vector", gq[:], gm[:], 0.125, None, ALU.mult, None, ["gm"], ["gq"])
        CW = 2048
        NWB = 6
        stg = [S("wst%d" % i, [128, CW], F32) for i in range(NWB)]
        stb = [S("wsb%d" % i, [128, CW], BF16) for i in range(NWB)]
        jobs = []
        q0, q1 = 3 * CH, 3 * CH + CN
        for r in range(D // 128):
            for (c0, c1, sc, sk) in ((0, q0, gm, "gm"), (q0, q1, gq, "gq"), (q1, NCOL, gm, "gm")):
                for cc in range(c0, c1, CW):
                    jobs.append((A["w_in"], A["Wb_in"], r, cc, min(CW, c1 - cc), sc, sk, r))
        engs = ["scalar", "vector"]
        for i, (src, dst, r, cc, w, sc, sk, si) in enumerate(jobs):
            b = i % NWB
            dma(P, "sync", stg[b][:, 0:w], src[r * 128:(r + 1) * 128, cc:cc + w], (), ["wst%d" % b])
            eng = engs[i % 2]
            if eng == "scalar":
                act(P, stb[b][:, 0:w], stg[b][:, 0:w], AF.Copy, ["wst%d" % b, sk], ["wsb%d" % b], scale=sc[:, si:si + 1])
            else:
                ts(P, eng, stb[b][:, 0:w], stg[b][:, 0:w], sc[:, si:si + 1], None, ALU.mult, None,
                   ["wst%d" % b, sk], ["wsb%d" % b])
            dma(P, "gpq", dst[r * 128:(r + 1) * 128, cc:cc + w], stb[b][:, 0:w], ["wsb%d" % b], [dst.tensor.name])
        P.emit()


def _phase_p1(nc, P, c, A):
    D, CH, CN, NH, T, KD, NCOL = c.D, c.CH, c.CN, c.NH, c.T, c.KD, c.NCOL
    TT1 = 1024
    NB = TT1 // 128
    NCC = 3 * CH // 128
    WG = 512
    x, Wb_in, u_hy, qT, kT, vn1 = A["x"], A["Wb_in"], A["u_hy"], A["qT"], A["kT"], A["vn1"]
    with contextlib.ExitStack() as st:
        S = lambda n, sh, dt: st.enter_context(nc.sbuf_tensor("p1_" + n, sh, dt))
        PS = lambda n, sh, dt: st.enter_context(nc.psum_tensor("p1_" + n, sh, dt))
        ident = S("ident", [128, 128], BF16)
        dma(P, "sync", ident[:], A["c_ident"], (), ["ident"])
        cw = S("cw", [128, NCC, 3], F32)
        cb = S("cb", [128, NCC], F32)
        for k in range(3):
            dma(P, "sync", cw[:, :, k], A["hy_conv_w"][k].rearrange("(k p) -> p k", p=128), (), ["cw"], slow=True)
        _colvec_load(P, "sync", cb, "cb", A["hy_conv_b"], NCC)
        flag = S("flag", [128, 1], F32)
        dma(P, "sync", flag[:], bass.AP(tensor=A["c_flag"].tensor, offset=0, ap=[[0, 128], [1, 1]]), (), ["flag"])
        nfw = S("nfw", [128, NCC, 3], F32)
        ts(P, "vector", nfw[:].rearrange("p a b -> p (a b)"), cw[:].rearrange("p a b -> p (a b)"), flag[:, 0:1], -1.0,
           ALU.mult, ALU.mult, ["cw", "flag"], ["nfw"])
        carry = S("carry", [128, NCC, 2], F32)
        mset(P, "vector", carry[:].rearrange("p a b -> p (a b)"), 0.0, ["carry"])
        xs = [S("xs%d" % i, [128, D], F32) for i in range(2)]
        ab = [S("ab%d" % i, [128, D], BF16) for i in range(2)]
        junk = S("junk", [128, D], F32)
        ss = S("ss", [128, 2 * NB], F32)
        aT = [S("aT%d" % i, [128, KD, TT1], BF16) for i in range(2)]
        wbuf = [S("wbuf%d" % i, [128, KD, WG], BF16) for i in range(2)]
        R = [S("R%d" % i, [128, 516], F32) for i in range(2)]
        o1 = [S("o1_%d" % i, [128, 512], F32) for i in range(2)]
        ob = [S("ob%d" % i, [128, 512], BF16) for i in range(2)]
        hst = [S("hst%d" % i, [128, 4, 512], BF16) for i in range(2)]
        qst = [S("qst%d" % i, [128, TT1], BF16) for i in range(2)]
        vst = [S("vst%d" % i, [128, 8, 65], BF16) for i in range(2)]
        for i in range(2):
            mset(P, "vector", vst[i][:].rearrange("p a b -> p (a b)"), 1.0, ["vst%d" % i])
        zrow = S("zrow", [1, 3 * CH], BF16)
        mset(P, "vector", zrow[:], 0.0, ["zrow"])
        dma(P, "gpq", u_hy[0:1, :], zrow[:], ["zrow"], ["u_hy_pad"])
        dma(P, "gpq", u_hy[T + 1:T + 2, :], zrow[:], ["zrow"], ["u_hy_pad"])
        psA = [PS("psA%d" % i, [128, 4, 128], BF16) for i in range(2)]
        psM = [PS("psM%d" % i, [128, 512], F32) for i in range(4)]
        psH = [PS("psH%d" % i, [128, 8, 128], BF16) for i in range(2)]
        nwg = NCOL // WG
        wcount = [0]

        def load_w(g):
            b = wcount[0] % 2
            wcount[0] += 1
            src = Wb_in[:, g * WG:(g + 1) * WG].rearrange("(k p) c -> p k c", p=128)
            dma(P, "sync", wbuf[b][:], src, ["Wb_in"], ["wbuf%d" % b])
            return b

        evi = [0]

        def ev_eng():
            evi[0] += 1
            return "scalar" if evi[0] % 2 else "vector"

        mcount = [0]
        hcount = [0]
        NTL = T // TT1
        DFF = c.DFF
        bgl = S("bgl", [128, KD], F32)
        bgh = S("bgh", [128, c.KM], F32)
        bone = S("bone", [128, 1], F32)
        _colvec_load(P, "sync", bgl, "bgl", A["norm_mlp_g"], KD)
        _colvec_load(P, "sync", bgh, "bgh", A["gnorm"], c.KM)
        mset(P, "vector", bone[:], 1.0, ["bone"])
        BCW = 1024
        NBG = 4
        bst = [S("bst%d" % i, [128, BCW], F32) for i in range(NBG)]
        bsb = [S("bsb%d" % i, [128, BCW], BF16) for i in range(NBG)]
        bjobs = []
        for r in range(c.MIXC // 128):
            for cc in range(0, D, BCW):
                bjobs.append((A["w_out"], A["Wb_out"], r, cc, min(BCW, D - cc), bgh, "bgh", r))
        for r in range(D // 128):
            for cc in range(0, DFF, BCW):
                bjobs.append((A["w_up"], A["Wb_up"], r, cc, min(BCW, DFF - cc), bgl, "bgl", r))
        for r in range(DFF // 128):
            for cc in range(0, D, BCW):
                bjobs.append((A["w_down"], A["Wb_down"], r, cc, min(BCW, D - cc), bone, "bone", 0))
        bgi = [0]

        def bg(n):
            for _ in range(n):
                i = bgi[0]
                if i >= len(bjobs):
                    return
                bgi[0] += 1
                src, dst, r, cc, w, sc, sk, si = bjobs[i]
                b = i % NBG
                dma(P, "actq", bst[b][:, 0:w], src[r * 128:(r + 1) * 128, cc:cc + w], (), ["bst%d" % b])
                if i % 2:
                    act(P, bsb[b][:, 0:w], bst[b][:, 0:w], AF.Copy, ["bst%d" % b, sk], ["bsb%d" % b], scale=sc[:, si:si + 1])
                else:
                    ts(P, "vector", bsb[b][:, 0:w], bst[b][:, 0:w], sc[:, si:si + 1], None, ALU.mult, None, ["bst%d" % b, sk], ["bsb%d" % b])
                dma(P, "gpq", dst[r * 128:(r + 1) * 128, cc:cc + w], bsb[b][:, 0:w], ["bsb%d" % b], ["bg_" + dst.tensor.name])
        bg_per = -(-len(bjobs) // (NTL * (NCOL // WG)))

        def prepA(ti, b):
            t0 = ti * TT1
            xb = b % 2
            dma(P, "sync", xs[xb][:], x[t0 + b * 128:t0 + (b + 1) * 128, :], (), ["xs%d" % xb])
            act(P, junk[:], xs[xb][:], AF.Square, ["xs%d" % xb], ["junk", "ss%d" % b], accum_out=ss[:, b:b + 1])
            act(P, ss[:, NB + b:NB + b + 1], ss[:, b:b + 1], AF.Sqrt, ["ss%d" % b], ["sr%d" % b], bias=EPS, scale=1.0 / D)
            P.op("vector", (lambda o, i: (lambda e: e.reciprocal(out=o, in_=i)))(ss[:, NB + b:NB + b + 1], ss[:, NB + b:NB + b + 1]),
                 ["sr%d" % b], ["sr%d" % b])
            if b % 2:
                ts(P, "vector", ab[xb][:], xs[xb][:], ss[:, NB + b:NB + b + 1], None, ALU.mult, None,
                   ["xs%d" % xb, "sr%d" % b], ["ab%d" % xb])
            else:
                act(P, ab[xb][:], xs[xb][:], AF.Copy, ["xs%d" % xb, "sr%d" % b], ["ab%d" % xb], scale=ss[:, NB + b:NB + b + 1])

        def prepB(ti, b):
            xb = b % 2
            at = aT[ti % 2]
            for k0 in range(0, KD, 4):
                pa = (k0 // 4) % 2
                nk = min(4, KD - k0)
                for kk in range(nk):
                    tr(P, psA[pa][:, kk, :], ab[xb][:, (k0 + kk) * 128:(k0 + kk + 1) * 128], ident[:],
                       ["ab%d" % xb, "ident"], ["psA%d" % pa])
                cp(P, ev_eng(), at[:, k0:k0 + nk, b * 128:(b + 1) * 128], psA[pa][:, 0:nk, :], ["psA%d" % pa], ["aT%d" % (ti % 2)])

        for b in range(NB):
            prepA(0, b)
            prepB(0, b)
        pending = []

        def flush_pending():
            while pending:
                pending.pop(0)()

        for ti in range(NTL):
            t0 = ti * TT1
            at = aT[ti % 2]
            atk = "aT%d" % (ti % 2)
            sched = {}
            if ti + 1 < NTL:
                for st_ in range(NB + 1):
                    sched.setdefault((st_ * nwg) // (NB + 1), []).append(st_)
            for g in range(nwg):
                wb = load_w(g)
                wk = "wbuf%d" % wb
                col0 = g * WG
                bg(bg_per)
                for st_ in sched.get(g, ()):
                    if st_ >= 1:
                        prepB(ti + 1, st_ - 1)
                    if st_ < NB:
                        prepA(ti + 1, st_)
                if col0 < 3 * CH:
                    for half in range(TT1 // 512):
                        s = ti * (TT1 // 512) + half
                        hb = hcount[0] % 2
                        hcount[0] += 1
                        for j in range(WG // 128):
                            cc = col0 // 128 + j
                            pm = mcount[0] % 4
                            mcount[0] += 1
                            for k in range(KD):
                                mm(P, psM[pm][:], wbuf[wb][:, k, j * 128:(j + 1) * 128], at[:, k, half * 512:(half + 1) * 512],
                                   k == 0, k == KD - 1, [wk, atk], ["psM%d" % pm])
                            rb = cc % 2
                            rk = "R%d" % rb
                            cp(P, "scalar", R[rb][:, 2:514], psM[pm][:], ["psM%d" % pm], [rk])
                            cp(P, "gpsimd", R[rb][:, 0:2], carry[:, cc, :], ["carry%d" % cc], [rk])
                            act(P, o1[rb][:], R[rb][:, 0:512], AF.Identity, [rk, "cw", "cb"], ["o1_%d" % rb],
                                bias=cb[:, cc:cc + 1], scale=cw[:, cc, 0:1])
                            stt(P, "vector", o1[rb][:], R[rb][:, 1:513], cw[:, cc, 1:2], o1[rb][:], ALU.mult, ALU.add,
                                [rk, "cw", "o1_%d" % rb], ["o1_%d" % rb])
                            stt(P, "vector", ob[rb][:], R[rb][:, 2:514], cw[:, cc, 2:3], o1[rb][:], ALU.mult, ALU.add,
                                [rk, "cw", "o1_%d" % rb], ["ob%d" % rb])
                            cp(P, "gpsimd", carry[:, cc, :], R[rb][:, 512:514], [rk], ["carry%d" % cc])
                            if s == 8:
                                stt(P, "vector", ob[rb][:, 0:1], R[rb][:, 2:3], nfw[:, cc, 2:3], ob[rb][:, 0:1], ALU.mult, ALU.add,
                                    [rk, "nfw", "ob%d" % rb], ["ob%d" % rb])
                                stt(P, "vector", ob[rb][:, 1:2], R[rb][:, 1:2], nfw[:, cc, 0:1], ob[rb][:, 1:2], ALU.mult, ALU.add,
                                    [rk, "nfw", "ob%d" % rb], ["ob%d" % rb])
                            flush_pending()

                            def post(j=j, rb=rb, hb=hb, s=s, col0=col0):
                                for q in range(4):
                                    tr(P, psH[q // 2][:, (q % 2) * 4 + j, :], ob[rb][:, q * 128:(q + 1) * 128], ident[:], ["ob%d" % rb, "ident"],
                                       ["psH%d" % (q // 2)])
                                if j == WG // 128 - 1:
                                    for q in range(4):
                                        cp(P, ev_eng(), hst[hb][:, q, :], psH[q // 2][:, (q % 2) * 4:(q % 2) * 4 + 4, :].rearrange("p a b -> p (a b)"),
                                           ["psH%d" % (q // 2)], ["hst%d" % hb])
                                    r0 = 512 * s
                                    dst = u_hy[r0:r0 + 512, col0:col0 + WG].rearrange("(q p) c -> p q c", p=128)
                                    dma(P, "gpq", dst, hst[hb][:], ["hst%d" % hb], ["u_hy%d" % hb])
                            pending.append(post)
                elif col0 < 3 * CH + 2 * CN:
                    dstT = qT if col0 < 3 * CH + CN else kT
                    cbase = col0 - (3 * CH if col0 < 3 * CH + CN else 3 * CH + CN)
                    for j in range(WG // 128):
                        qb = j % 2
                        for half in range(TT1 // 512):
                            pm = mcount[0] % 4
                            mcount[0] += 1
                            for k in range(KD):
                                mm(P, psM[pm][:], wbuf[wb][:, k, j * 128:(j + 1) * 128], at[:, k, half * 512:(half + 1) * 512],
                                   k == 0, k == KD - 1, [wk, atk], ["psM%d" % pm])
                            flush_pending()
                            cp(P, ev_eng(), qst[qb][:, half * 512:(half + 1) * 512], psM[pm][:], ["psM%d" % pm], ["qst%d" % qb])
                        dma(P, "gpq", dstT[cbase + j * 128:cbase + (j + 1) * 128, t0:t0 + TT1], qst[qb][:], ["qst%d" % qb], ["qkT%d" % qb])
                else:
                    h0 = (col0 - 3 * CH - 2 * CN) // 64
                    for b in range(NB):
                        pm = mcount[0] % 4
                        mcount[0] += 1
                        vb = b % 2
                        for k in range(KD):
                            mm(P, psM[pm][:], at[:, k, b * 128:(b + 1) * 128], wbuf[wb][:, k, :], k == 0, k == KD - 1,
                               [wk, atk], ["psM%d" % pm])
                        cp(P, ev_eng(), vst[vb][:, :, 0:64], psM[pm][:].rearrange("p (h d) -> p h d", d=64), ["psM%d" % pm], ["vst%d" % vb])
                        dma(P, "gpq", vn1[t0 + b * 128:t0 + (b + 1) * 128, h0:h0 + 8, :], vst[vb][:], ["vst%d" % vb], ["vn1_%d" % vb])
        flush_pending()
        bg(len(bjobs))
        fl = S("fl", [128, NCC, 2], F32)
        flb = S("flb", [128, NCC], BF16)
        frow = S("frow", [1, NCC * 128], BF16)
        keys = ["carry%d" % i for i in range(NCC)]
        tt(P, "vector", fl[:], carry[:], cw[:, :, 0:2], ALU.mult, keys + ["cw", "fl"], ["fl"])
        tt(P, "vector", fl[:, :, 0], fl[:, :, 0], fl[:, :, 1], ALU.add, ["fl"], ["fl"])
        tt(P, "vector", flb[:], fl[:, :, 0], cb[:], ALU.add, ["fl", "cb"], ["flb"])
        for c0 in range(0, NCC, 4):
            pa = (c0 // 4) % 2
            for kk in range(4):
                tr(P, psA[pa][0:1, kk, :], flb[:, c0 + kk:c0 + kk + 1], ident[:], ["flb", "ident"], ["psA%d" % pa])
            cp(P, "vector", frow[:, c0 * 128:(c0 + 4) * 128], psA[pa][0:1, :, :].rearrange("p a b -> p (a b)"), ["psA%d" % pa], ["frow"])
        dma(P, "gpq", u_hy[T:T + 1, :], frow[:], ["frow"], ["u_hy"])
        P.emit()


def _phase_hy(nc, P, c, A):
    CH, T, CB, NCB = c.CH, c.T, c.CB, c.NCB
    u_hy, S1, S2, KH, y_hy, z1d = A["u_hy"], A["S1"], A["S2"], A["KH"], A["y_hy"], A["z1d"]
    NG = 4
    KGRP = 2
    LOOK = 2
    with contextlib.ExitStack() as st0:
        S0 = lambda n, sh, dt: st0.enter_context(nc.sbuf_tensor("hy_" + n, sh, dt))
        h3b = S0("h3b", [64, T], BF16)
        w4b = S0("w4b", [64, 4 * CH], BF16)
        with contextlib.ExitStack() as st:
            S = lambda n, sh, dt: st.enter_context(nc.sbuf_tensor("hy_" + n, sh, dt))
            PS = lambda n, sh, dt: st.enter_context(nc.psum_tensor("hy_" + n, sh, dt))
            zT = S("zT", [33, T], F32)
            dma(P, "sync", zT[:], A["c_zT"], (), ["zT"])
            w1 = S("w1", [33, 64], F32)
            w2 = S("w2", [64, 64], F32)
            w3 = S("w3", [64, 64], F32)
            dma(P, "sync", w1[:], A["pe_w1"], (), ["w1"])
            dma(P, "sync", w2[:], A["pe_w2"], (), ["w2"])
            dma(P, "sync", w3[:], A["pe_w3"], (), ["w3"])
            pv = S("pv", [64, 8], F32)
            for i, nm in enumerate(["pe_freq", "pe_b1", "pe_b2", "pe_b3"]):
                dma(P, "sync", pv[:, i:i + 1], A[nm].rearrange("(p o) -> p o", o=1), (), ["pv"], slow=True)
            ts(P, "vector", pv[:, 4:5], pv[:, 0:1], 1.0 / (2 * math.pi), None, ALU.mult, None, ["pv"], ["pv"])
            for i in range(3):
                tt(P, "vector", pv[:, 5 + i:6 + i], pv[:, 4:5], pv[:, 1 + i:2 + i], ALU.mult, ["pv"], ["pv"])
            hA = S("hA", [64, T], F32)
            hB = S("hB", [64, T], F32)
            w4f = S("w4f", [64, 4 * CH], F32)
            dma(P, "sync", w4f[:], A["pe_w4"], (), ["w4f"])
            cp(P, "vector", w4b[:], w4f[:], ["w4f"], ["w4b"])
            uu = [S("uu%d" % i, [64, 512], F32) for i in range(2)]
            ui = [S("ui%d" % i, [64, 512], I32) for i in range(2)]
            uf = [S("uf%d" % i, [64, 512], F32) for i in range(2)]
            psF = [PS("psF%d" % i, [64, 512], F32) for i in range(2)]
            layers = [(w1, "w1", zT, hA, 33), (w2, "w2", hA, hB, 64), (w3, "w3", hB, None, 64)]
            n = 0
            for li, (wl, wk, src, dst, K) in enumerate(layers):
                for ch in range(T // 512):
                    b = n % 2
                    n += 1
                    sl = slice(ch * 512, (ch + 1) * 512)
                    srck = "zT" if li == 0 else "h%d_%d" % (li - 1, ch)
                    mm(P, psF[b][:], wl[0:K, :], src[0:K, sl], True, True, [wk, srck], ["psF%d" % b])
                    ts(P, "vector", uu[b][:], psF[b][:], pv[:, 4:5], pv[:, 5 + li:6 + li], ALU.mult, ALU.add,
                       ["psF%d" % b, "pv"], ["uu%d" % b])
                    cp(P, "vector", ui[b][:], uu[b][:], ["uu%d" % b], ["ui%d" % b])
                    cp(P, "gpsimd", uf[b][:], ui[b][:], ["ui%d" % b], ["uf%d" % b])
                    tt(P, "gpsimd", uu[b][:], uu[b][:], uf[b][:], ALU.subtract, ["uu%d" % b, "uf%d" % b], ["uu%d" % b])
                    outap = h3b[:, sl] if li == 2 else dst[:, sl]
                    act(P, outap, uu[b][:], AF.Sin, ["uu%d" % b], ["h%d_%d" % (li, ch)], scale=2 * math.pi)
            P.emit()

        with contextlib.ExitStack() as st:
            S = lambda n, sh, dt: st.enter_context(nc.sbuf_tensor("hy_" + n, sh, dt))
            PS = lambda n, sh, dt: st.enter_context(nc.psum_tensor("hy_" + n, sh, dt))
            W = S("W", [128, 4, 128], BF16)
            dma(P, "sync", W[:].rearrange("p a b -> p (a b)"), A["c_W"], (), ["W"])
            Wre, Wim, nWim, nWre = W[:, 0, :], W[:, 1, :], W[:, 2, :], W[:, 3, :]
            tcol = S("tcol", [64, 128], F32)
            mf = S("mf", [64, 128], F32)
            mb = S("mb", [64, 128], F32)
            dma(P, "sync", tcol[:], A["c_tcol"], (), ["tcol"])
            dma(P, "sync", mf[:], A["c_mf"], (), ["mf"])
            dma(P, "sync", mb[:], A["c_mb"], (), ["mb"])
            nad = S("nad", [64, CH], F32)
            dma(P, "sync", nad[:], bass.AP(tensor=A["c_nad"].tensor, offset=0, ap=[[0, 64], [1, CH]]), (), ["nad"])
            hbias = S("hbias", [64, 2, CH], F32)
            dma(P, "sync", hbias[:], bass.AP(tensor=A["hy_bias"].tensor, offset=0, ap=[[0, 64], [CH, 2], [1, CH]]), (), ["hbias"])
            Eg = [S("Eg%d" % i, [64, NG, 128], BF16) for i in range(2)]
            Gg = [S("Gg%d" % i, [128, NG, 64], BF16) for i in range(2)]
            xin = [S("xin%d" % i, [64, NG, CB], BF16) for i in range(2)]
            x1s = [S("x1s%d" % i, [128, NG, CB], BF16) for i in range(2)]
            dec = [S("dec%d" % i, [64, CB], F32) for i in range(3)]
            hf = [S("hf%d" % i, [64, CB], BF16) for i in range(4)]
            xb1 = [S("xb1_%d" % i, [128, 2, KGRP, CB], BF16) for i in range(2)]
            xb2 = [S("xb2_%d" % i, [128, 2, KGRP, CB], BF16) for i in range(2)]
            khs = [S("khs%d" % i, [128, KGRP, 2, CB], BF16) for i in range(2)]
            yh = [S("yh%d" % i, [128, 2, CB], BF16) for i in range(3)]
            tmp = [S("tmp%d" % i, [128, CB], F32) for i in range(8)]
            vs = [S("vs%d" % i, [128, 2, KGRP, CB], BF16) for i in range(2)]
            vin = [S("vin%d" % i, [128, NG, CB], BF16) for i in range(2)]
            gv = [S("gv%d" % i, [64, NG, CB], BF16) for i in range(2)]
            gx = [S("gx%d" % i, [64, NG, CB], BF16) for i in range(2)]
            vbz = [S("vbz%d" % i, [64, CB], F32) for i in range(2)]
            zt = [S("zt%d" % i, [64, CB], F32) for i in range(2)]
            zo = [S("zo%d" % i, [64, NG, CB], BF16) for i in range(2)]
            pB = [PS("pB%d" % i, [128, CB], F32) for i in range(4)]
            pV = [PS("pV%d" % i, [128, CB], F32) for i in range(4)]
            cnt = {}

            def nxt(k, m):
                v = cnt.get(k, 0)
                cnt[k] = v + 1
                return v % m

            def ev_eng():
                return "scalar" if nxt("ev", 2) else "vector"

            Enat_v = A["c_Enat"].rearrange("p (a b) -> p a b", a=128, b=128)
            Edat_v = A["c_Edat"].rearrange("p (a b) -> p a b", a=128, b=128)
            G_v = A["c_G"].rearrange("p (a b) -> p a b", a=128, b=64)
            h3v = h3b[:].rearrange("p (a b) -> p b a", b=128)

            def tokview(ap2d):
                return ap2d.rearrange("(a b) c -> a b c", b=128)

            def stageA(slot, Ev, get_rhs):
                items = [(gi, j) for gi in range(128 // NG) for j in range(NG)]
                ebs, sbs = {}, {}
                queue = []
                for it in items + [None] * LOOK:
                    cur = None
                    if it is not None:
                        gi, j = it
                        if j == 0:
                            ebs[gi] = nxt("eg", 2)
                            dma(P, "sync", Eg[ebs[gi]][:], Ev[:, gi * NG:(gi + 1) * NG], (), ["Eg%d" % ebs[gi]])
                        rhs, rkeys = get_rhs(gi, j)
                        cur = (gi, j, rhs, rkeys)
                        queue.append(cur)
                    if len(queue) > LOOK or (it is None and queue):
                        gi, j, rhs, rkeys = queue.pop(0)
                        eb = ebs[gi]
                        if j == 0:
                            sbs[gi] = nxt("x1s", 2)
                        sb = sbs[gi]
                        p = nxt("pB", 4)
                        mm(P, pB[p][:], Eg[eb][:, j, :], rhs, True, True, ["Eg%d" % eb] + rkeys, ["pB%d" % p])
                        cp(P, "scalar" if nxt("evA", 2) else "vector", x1s[sb][:, j, :], pB[p][:], ["pB%d" % p], ["x1s%d_%d" % (sb, j)])
                        if j == NG - 1:
                            dst = S1[slot].rearrange("r k n c -> (r k) n c")[:, gi * NG:(gi + 1) * NG, :]
                            dma(P, "gpq", dst, x1s[sb][:], ["x1s%d_%d" % (sb, jj) for jj in range(NG)], ["S1_%d_%d" % (slot, gi % 2)])

            def data_rhs(src2d, skey):
                tv = tokview(src2d)
                state = {}

                def get(gi, j):
                    if j == 0:
                        b = nxt("xin", 2)
                        state["b"] = b
                        dma(P, "sync", xin[b][:], tv[:, gi * NG:(gi + 1) * NG, :], [skey], ["xin%d" % b])
                    b = state["b"]
                    return xin[b][:, j, :], ["xin%d" % b]
                return get

            def filt_rhs(od, c0):
                o, dr = od // 2, od % 2
                wcol = o * 2 * CH + dr * CH + c0
                msk, mk = (mf, "mf") if dr == 0 else (mb, "mb")

                def get(gi, j):
                    n2 = gi * NG + j
                    d = nxt("dec", 3)
                    act(P, dec[d][:], nad[:, c0:c0 + CB], AF.Exp, ["nad", "tcol"], ["dec%d" % d], scale=tcol[:, n2:n2 + 1])
                    p = nxt("pV", 4)
                    mm(P, pV[p][0:64, :], h3v[:, n2, :], w4b[:, wcol:wcol + CB], True, True, ["h3b", "w4b"], ["pV%d" % p])
                    hb_ = nxt("hf", 4)
                    stt(P, "vector", hf[hb_][:], pV[p][0:64, :], msk[:, n2:n2 + 1], dec[d][:], ALU.mult, ALU.mult,
                        ["pV%d" % p, mk, "dec%d" % d], ["hf%d" % hb_])
                    return hf[hb_][:], ["hf%d" % hb_]
                return get

            def load_xb(buf, bk, slot, k1a, k1b):
                for ri in range(2):
                    src = S1[slot, ri, k1a:k1b, :, :].rearrange("k n c -> n k c")
                    dma(P, "sync", buf[:, ri, 0:k1b - k1a, :], src, ["S1_%d_0" % slot, "S1_%d_1" % slot], [bk])

            def stageB_filter(o, cbi):
                for k1a in range(0, 64, KGRP):
                    k1b = min(64, k1a + KGRP)
                    b = nxt("xb", 2)
                    load_xb(xb1[b], "xb1_%d" % b, 2 * o, k1a, k1b)
                    load_xb(xb2[b], "xb2_%d" % b, 2 * o + 1, k1a, k1b)
                    kb = nxt("khs", 2)
                    rk = ["xb1_%d" % b, "xb2_%d" % b, "W"]
                    for kk in range(k1b - k1a):
                        fr, fi = xb1[b][:, 0, kk, :], xb1[b][:, 1, kk, :]
                        br, bi = xb2[b][:, 0, kk, :], xb2[b][:, 1, kk, :]
                        pr = nxt("pB", 4)
                        for n_, (w_, x_) in enumerate(((Wre, fr), (nWim, fi), (Wre, br), (nWim, bi))):
                            mm(P, pB[pr][:], w_, x_, n_ == 0, n_ == 3, rk, ["pB%d" % pr])
                        cp(P, ev_eng(), khs[kb][:, kk, 0, :], pB[pr][:], ["pB%d" % pr], ["khs%d" % kb])
                        pi = nxt("pB", 4)
                        for n_, (w_, x_) in enumerate(((Wim, fr), (Wre, fi), (nWim, br), (nWre, bi))):
                            mm(P, pB[pi][:], w_, x_, n_ == 0, n_ == 3, rk, ["pB%d" % pi])
                        cp(P, ev_eng(), khs[kb][:, kk, 1, :], pB[pi][:], ["pB%d" % pi], ["khs%d" % kb])
                    dma(P, "gpq", KH[o, cbi, :, k1a:k1b, :, :], khs[kb][:, 0:k1b - k1a, :, :], ["khs%d" % kb], ["KH%d" % o])

            def stageB_conv(slot, o, cbi):
                items = [(k1a, kk) for k1a in range(0, 64, KGRP) for kk in range(min(64, k1a + KGRP) - k1a)]
                grp = {}
                queue = []
                for it in items + [None] * LOOK:
                    cur = None
                    if it is not None:
                        k1a, kk = it
                        k1b = min(64, k1a + KGRP)
                        if kk == 0:
                            b = nxt("xb", 2)
                            load_xb(xb1[b], "xb1_%d" % b, slot, k1a, k1b)
                            kb = nxt("khs", 2)
                            dma(P, "sync", khs[kb][:, 0:k1b - k1a, :, :], KH[o, cbi, :, k1a:k1b, :, :], ["KH%d" % o], ["khs%d" % kb])
                            grp[k1a] = [b, kb, None]
                        b, kb, _ = grp[k1a]
                        rk = ["xb1_%d" % b, "W"]
                        xr, xi = xb1[b][:, 0, kk, :], xb1[b][:, 1, kk, :]
                        kr, ki = khs[kb][:, kk, 0, :], khs[kb][:, kk, 1, :]
                        pr = nxt("pB", 4)
                        mm(P, pB[pr][:], Wre, xr, True, False, rk, ["pB%d" % pr])
                        mm(P, pB[pr][:], nWim, xi, False, True, rk, ["pB%d" % pr])
                        pi = nxt("pB", 4)
                        mm(P, pB[pi][:], Wim, xr, True, False, rk, ["pB%d" % pi])
                        mm(P, pB[pi][:], Wre, xi, False, True, rk, ["pB%d" % pi])
                        t = [nxt("tmp", 8) for _ in range(4)]
                        kk_ = ["khs%d" % kb]
                        tt(P, "vector", tmp[t[0]][:], pB[pr][:], kr, ALU.mult, ["pB%d" % pr] + kk_, ["tmp%d" % t[0]])
                        tt(P, "vector", tmp[t[1]][:], pB[pi][:], ki, ALU.mult, ["pB%d" % pi] + kk_, ["tmp%d" % t[1]])
                        tt(P, "vector", tmp[t[2]][:], pB[pr][:], ki, ALU.mult, ["pB%d" % pr] + kk_, ["tmp%d" % t[2]])
                        tt(P, "vector", tmp[t[3]][:], pB[pi][:], kr, ALU.mult, ["pB%d" % pi] + kk_, ["tmp%d" % t[3]])
                        yb = nxt("yh", 3)
                        tt(P, "gpsimd", yh[yb][:, 0, :], tmp[t[0]][:], tmp[t[1]][:], ALU.subtract, ["tmp%d" % t[0], "tmp%d" % t[1]], ["yh%d" % yb])
                        tt(P, "gpsimd", yh[yb][:, 1, :], tmp[t[2]][:], tmp[t[3]][:], ALU.add, ["tmp%d" % t[2], "tmp%d" % t[3]], ["yh%d" % yb])
                        cur = (k1a, kk, yb)
                        queue.append(cur)
                    if len(queue) > LOOK or (it is None and queue):
                        k1a, kk, yb = queue.pop(0)
                        k1b = min(64, k1a + KGRP)
                        if kk == 0:
                            grp[k1a][2] = nxt("vs", 2)
                        vb = grp[k1a][2]
                        yr, yi = yh[yb][:, 0, :], yh[yb][:, 1, :]
                        yk = ["yh%d" % yb, "W"]
                        vr = nxt("pV", 4)
                        mm(P, pV[vr][:], Wre, yr, True, False, yk, ["pV%d" % vr])
                        mm(P, pV[vr][:], Wim, yi, False, True, yk, ["pV%d" % vr])
                        cp(P, "scalar", vs[vb][:, 0, kk, :], pV[vr][:], ["pV%d" % vr], ["vs%d" % vb])
                        vi = nxt("pV", 4)
                        mm(P, pV[vi][:], nWim, yr, True, False, yk, ["pV%d" % vi])
                        mm(P, pV[vi][:], Wre, yi, False, True, yk, ["pV%d" % vi])
                        cp(P, "scalar", vs[vb][:, 1, kk, :], pV[vi][:], ["pV%d" % vi], ["vs%d" % vb])
                        if kk == k1b - k1a - 1:
                            for ri in range(2):
                                dma(P, "gpq", S2[ri, :, k1a:k1b, :], vs[vb][:, ri, 0:k1b - k1a, :], ["vs%d" % vb], ["S2_%d" % ri])

            def stageAp(o, cbi):
                c0 = cbi * CB
                gate2d = u_hy[1:T + 1, (1 + o) * CH + c0:(1 + o) * CH + c0 + CB]
                zsrc2d, zkey = (u_hy[1:T + 1, c0:c0 + CB], "u_hy") if o == 0 else (z1d[:, c0:c0 + CB], "z1d")
                dst2d, dkey = (z1d[:, c0:c0 + CB], "z1d") if o == 0 else (y_hy[:, c0:c0 + CB], "y_hy")
                gview, zview, dview = tokview(gate2d), tokview(zsrc2d), tokview(dst2d)
                def ap_loads(gi):
                    gb = nxt("gg", 2)
                    dma(P, "sync", Gg[gb][:], G_v[:, gi * NG:(gi + 1) * NG], (), ["Gg%d" % gb])
                    vb = nxt("vin", 2)
                    for ri in range(2):
                        src = S2[ri, gi * NG:(gi + 1) * NG, :, :].rearrange("n k c -> k n c")
                        dma(P, "sync", vin[vb][ri * 64:(ri + 1) * 64, :, :], src, ["S2_%d" % ri], ["vin%d" % vb])
                    xb_ = nxt("gx", 2)
                    dma(P, "sync", gx[xb_][:], gview[:, gi * NG:(gi + 1) * NG, :], ["u_hy"], ["gx%d" % xb_])
                    dma(P, "sync", gv[xb_][:], zview[:, gi * NG:(gi + 1) * NG, :], [zkey], ["gv%d" % xb_])
                    return gb, vb, xb_

                nxt_ld = ap_loads(0)
                for gi in range(128 // NG):
                    gb, vb, xb_ = nxt_ld
                    if gi + 1 < 128 // NG:
                        nxt_ld = ap_loads(gi + 1)
                    ob_ = nxt("zo", 2)
                    for j in range(NG):
                        p = nxt("pV", 4)
                        mm(P, pV[p][0:64, :], Gg[gb][:, j, :], vin[vb][:, j, :], True, True, ["Gg%d" % gb, "vin%d" % vb], ["pV%d" % p])
                        q = nxt("vbz", 2)
                        tt(P, "gpsimd", vbz[q][:], gv[xb_][:, j, :], hbias[:, o, c0:c0 + CB], ALU.mult, ["gv%d" % xb_, "hbias"], ["vbz%d" % q])
                        tt(P, "vector", zt[q][:], pV[p][0:64, :], vbz[q][:], ALU.add, ["pV%d" % p, "vbz%d" % q], ["zt%d" % q])
                        tt(P, "vector", zo[ob_][:, j, :], zt[q][:], gx[xb_][:, j, :], ALU.mult, ["zt%d" % q, "gx%d" % xb_], ["zo%d" % ob_])
                    dma(P, "gpq", dview[:, gi * NG:(gi + 1) * NG, :], zo[ob_][:], ["zo%d" % ob_], [dkey])

            for cbi in range(NCB):
                c0 = cbi * CB
                for od in range(4):
                    stageA(od, Enat_v, filt_rhs(od, c0))
                for o in range(2):
                    stageB_filter(o, cbi)
                stageA(0, Edat_v, data_rhs(u_hy[1:T + 1, c0:c0 + CB], "u_hy"))
                stageB_conv(0, 0, cbi)
                stageAp(0, cbi)
                stageA(1, Edat_v, data_rhs(z1d[:, c0:c0 + CB], "z1d"))
                stageB_conv(1, 1, cbi)
                stageAp(1, cbi)
            P.emit()


def _phase_nat(nc, P, c, A):
    NH, CN, T = c.NH, c.CN, c.T
    NHP = NH // 2
    qT, kT, vn1, y_nat = A["qT"], A["kT"], A["vn1"], A["y_nat"]
    NTY = len(NAT_TYPES)
    with contextlib.ExitStack() as st:
        S = lambda n, sh, dt: st.enter_context(nc.sbuf_tensor("nat_" + n, sh, dt))
        PS = lambda n, sh, dt: st.enter_context(nc.psum_tensor("nat_" + n, sh, dt))
        ident = S("ident", [128, 128], BF16)
        dma(P, "sync", ident[:], A["c_ident"], (), ["ident"])
        msk = S("msk", [128, NTY, 896], BF16)
        dma(P, "sync", msk[:], A["c_mask"].rearrange("t p c -> p t c"), (), ["msk"])
        TT = S("TT", [128, NH, 960], BF16)
        TMint = S("TMint", [128, NH, 576], BF16)
        tst = [S("tst%d" % i, [128, 960], F32) for i in range(2)]
        for h in range(NH):
            b = h % 2
            dma(P, "sync", tst[b][:], A["nat_tt"][h], (), ["tst%d" % b])
            cp(P, "vector", TT[:, h, :], tst[b][:], ["tst%d" % b], ["TT%d" % h])
            tt(P, "gpsimd", TMint[:, h, :], TT[:, h, 192:768], msk[:, 0, 0:576], ALU.add, ["TT%d" % h, "msk"], ["TMint%d" % h])
        TMT = S("TMT", [128, NH, 640], BF16)
        ssb = [S("ssb%d" % i, [128, 640], F32) for i in range(3)]
        qt = [S("qt%d" % i, [128, NHP, 512], BF16) for i in range(2)]
        kt = [S("kt%d" % i, [128, NHP, 896], BF16) for i in range(2)]
        v1 = [S("v1_%d" % i, [128, 7, NH, 65], BF16) for i in range(2)]
        tmsp = [S("tmsp%d" % i, [128, 896], BF16) for i in range(2)]
        pt = [S("pt%d" % i, [128, 1024], BF16) for i in range(3)]
        ys = [S("ys%d" % i, [128, CN], BF16) for i in range(2)]
        rec = [S("rec%d" % i, [128, 4], F32) for i in range(2)]
        stS = contextlib.ExitStack()
        pSb = [stS.enter_context(nc.psum_tensor("nat_pSb0", [128, 1024], BF16))] * 2
        cnt = {}

        def nxt(k, m):
            v = cnt.get(k, 0)
            cnt[k] = v + 1
            return v % m

        qTv = qT.rearrange("(c p) t -> p c t", p=128)
        kTv = kT.rearrange("(c p) t -> p c t", p=128)
        mset(P, "vector", TMT[:].rearrange("p a b -> p (a b)"), 0.0, ["TMTall"])
        for h in range(NH):
            for kb in range(5):
                kp = 128 if kb < 4 else 64
                sb = 0
                tr(P, pSb[sb][0:kp, kb * 128:(kb + 1) * 128], TMint[:, h, kb * 128:kb * 128 + kp], ident[:], ["TMint%d" % h, "ident"], ["pSb%d" % sb])
            cp(P, "vector" if h % 2 else "scalar", TMT[:, h, 0:512], pSb[0][:, 0:512], ["pSb0"], ["TMT%d" % h, "TMTall"])
            cp(P, "vector" if h % 2 else "scalar", TMT[0:64, h, 512:640], pSb[0][0:64, 512:640], ["pSb0"], ["TMT%d" % h, "TMTall"])
        P.emit()
        stS.close()
        pS = [PS("pS%d" % i, [128, 1024], F32) for i in range(3)]
        pO = [PS("pO%d" % i, [128, 4, 65], F32) for i in range(2)]
        for m in range(T // 128):
            tname, ks, nr = nat_block(m)
            ti = NAT_TYPES.index(tname)
            Bs = ks - 2 * m + 7
            nfull = nr // 2
            nkb = (nr + 1) // 2
            if m % 4 == 0:
                qb = nxt("qt", 2)
                dma(P, "sync", qt[qb][:], qTv[:, :, m * 128:m * 128 + 512], (), ["qt%d" % qb])
            qcol = (m % 4) * 128
            kb_ = nxt("kt", 2)
            dma(P, "sync", kt[kb_][:, :, 0:nr * 64], kTv[:, :, ks * 64:(ks + nr) * 64], (), ["kt%d" % kb_])
            vb = nxt("v1", 2)
            t0 = ks * 64
            dma(P, "sync", v1[vb][:, 0:nfull, :, :], vn1[t0:t0 + nfull * 128].rearrange("(b p) h e -> p b h e", p=128), (), ["v1_%d" % vb])
            if nr % 2:
                dma(P, "sync", v1[vb][0:64, nfull, :, :], vn1[t0 + nfull * 128:t0 + nfull * 128 + 64], (), ["v1_%d" % vb])
            yb = nxt("ys", 2)
            pend = []
            obs = {}
            for h in range(NH):
                hp, hc = h % 2, h // 2
                if tname == "int":
                    TMh, tmk = TMint[:, h, :], "TMint%d" % h
                else:
                    tb = nxt("tmsp", 2)
                    tt(P, "vector", tmsp[tb][:, 0:nr * 64], TT[:, h, Bs * 64:(Bs + nr) * 64], msk[:, ti, 0:nr * 64], ALU.add,
                       ["TT%d" % h, "msk"], ["tmsp%d" % tb])
                    TMh, tmk = tmsp[tb], "tmsp%d" % tb
                sb = nxt("pS", 3)
                if tname == "int":
                    for kb in range(nkb):
                        kp = 128 if kb < nfull else 64
                        out = pS[sb][0:kp, kb * 128:(kb + 1) * 128]
                        mm(P, out, kt[kb_][hp * 64:(hp + 1) * 64, hc, kb * 128:kb * 128 + kp], qt[qb][hp * 64:(hp + 1) * 64, hc, qcol:qcol + 128],
                           True, True, ["kt%d" % kb_, "qt%d" % qb], ["pS%d" % sb])
                    tt(P, "vector", ssb[sb][:, 0:512], pS[sb][:, 0:512], TMT[:, h, 0:512], ALU.add, ["pS%d" % sb, "TMT%d" % h], ["ssb%d" % sb])
                    tt(P, "vector", ssb[sb][0:64, 512:640], pS[sb][0:64, 512:640], TMT[0:64, h, 512:640], ALU.add, ["pS%d" % sb, "TMT%d" % h], ["ssb%d" % sb])
                    act(P, pt[sb][:, 0:512], ssb[sb][:, 0:512], AF.Exp, ["ssb%d" % sb], ["pt%d" % sb])
                    act(P, pt[sb][0:64, 512:640], ssb[sb][0:64, 512:640], AF.Exp, ["ssb%d" % sb], ["pt%d" % sb])
                else:
                    for kb in range(nkb):
                        kp = 128 if kb < nfull else 64
                        out = pS[sb][0:kp, kb * 128:(kb + 1) * 128]
                        mm(P, out, kt[kb_][hp * 64:(hp + 1) * 64, hc, kb * 128:kb * 128 + kp], qt[qb][hp * 64:(hp + 1) * 64, hc, qcol:qcol + 128],
                           True, False, ["kt%d" % kb_, "qt%d" % qb], ["pS%d" % sb])
                        mm(P, out, TMh[:, kb * 128:kb * 128 + kp], ident[:], False, True, [tmk, "ident"], ["pS%d" % sb])
                    act(P, pt[sb][:, 0:nfull * 128], pS[sb][:, 0:nfull * 128], AF.Exp, ["pS%d" % sb], ["pt%d" % sb])
                    if nr % 2:
                        act(P, pt[sb][0:64, nfull * 128:nkb * 128], pS[sb][0:64, nfull * 128:nkb * 128], AF.Exp, ["pS%d" % sb], ["pt%d" % sb])
                while len(pend) > 1:
                    pend.pop(0)()
                if h % 4 == 0:
                    obs[h // 4] = nxt("pO", 2)

                def pv(h=h, sb=sb):
                    ob = obs[h // 4]
                    for kb in range(nkb):
                        kp = 128 if kb < nfull else 64
                        mm(P, pO[ob][:, h % 4, :], pt[sb][0:kp, kb * 128:(kb + 1) * 128], v1[vb][0:kp, kb, h, :], kb == 0, kb == nkb - 1,
                           ["pt%d" % sb, "v1_%d" % vb], ["pO%d" % ob])
                    if h % 4 == 3:
                        rb = nxt("rec", 2)
                        P.op("vector", (lambda o_, i_: (lambda e: e.reciprocal(out=o_, in_=i_)))(rec[rb][:], pO[ob][:, :, 64]),
                             ["pO%d" % ob], ["rec%d" % rb])
                        for hh in range(4):
                            hd = h - 3 + hh
                            if hh % 2:
                                act(P, ys[yb][:, hd * 64:(hd + 1) * 64], pO[ob][:, hh, 0:64], AF.Copy, ["pO%d" % ob, "rec%d" % rb], ["ys%d" % yb],
                                    scale=rec[rb][:, hh:hh + 1])
                            else:
                                ts(P, "vector", ys[yb][:, hd * 64:(hd + 1) * 64], pO[ob][:, hh, 0:64], rec[rb][:, hh:hh + 1], None, ALU.mult, None,
                                   ["pO%d" % ob, "rec%d" % rb], ["ys%d" % yb])
                pend.append(pv)
            while pend:
                pend.pop(0)()
            dma(P, "gpq", y_nat[m * 128:(m + 1) * 128, :], ys[yb][:], ["ys%d" % yb], ["y_nat%d" % yb])
        P.emit()


def _phase_p3(nc, P, c, A):
    D, CH, CN, DFF, T, KD, KM, KF, KG, DC = c.D, c.CH, c.CN, c.DFF, c.T, c.KD, c.KM, c.KF, c.KG, c.DC
    MIXC = c.MIXC
    TT3 = 512
    NB = TT3 // 128
    WK = max(KM, KD, KG)
    x, y, y_hy, y_nat = A["x"], A["y"], A["y_hy"], A["y_nat"]
    Wb_out, Wb_up, Wb_down = A["Wb_out"], A["Wb_up"], A["Wb_down"]
    NSL = KF // KG
    FFS = KG * 128
    with contextlib.ExitStack() as st:
        S = lambda n, sh, dt: st.enter_context(nc.sbuf_tensor(n, sh, dt))
        PS = lambda n, sh, dt: st.enter_context(nc.psum_tensor(n, sh, dt))
        ident = S("ident", [128, 128], BF16)
        dma(P, "sync", ident[:], A["c_ident"], (), ["ident"])
        gfin = S("gfin", [128, D], F32)
        dma(P, "sync", gfin[:], bass.AP(tensor=A["norm_f_g"].tensor, offset=0, ap=[[0, 128], [1, D]]), (), ["gfin"])
        yin = [S("yin%d" % i, [128, MIXC], BF16) for i in range(2)]
        mixb = [S("mixb%d" % i, [128, MIXC], BF16) for i in range(2)]
        actT = S("actT", [128, KD, TT3], BF16)
        mixT = S("mixT", [128, KM, TT3], BF16)
        xres = [S("xres%d" % i, [128, D], F32) for i in range(NB)]
        mbb = [S("mbb%d" % i, [128, D], BF16) for i in range(2)]
        uT = [S("uT%d" % i, [128, KG, TT3], BF16) for i in range(2)]
        rl = [S("rl%d" % i, [128, TT3], F32) for i in range(2)]
        outb = [S("outb%d" % i, [128, D], F32) for i in range(2)]
        wbuf = [S("wbuf%d" % i, [128, WK, 512], BF16) for i in range(3)]
        stt_ = S("stats", [128, 24], F32)
        psA = [PS("psA%d" % i, [128, 4, 128], BF16) for i in range(2)]
        psM = [PS("psM%d" % i, [128, 512], F32) for i in range(6)]
        cnt = {}

        def nxt(k, m):
            v = cnt.get(k, 0)
            cnt[k] = v + 1
            return v % m

        def ev_eng():
            return "scalar" if nxt("ev", 2) else "vector"

        def rstd_of(src_ap, n, skeys, col, junk_ap, junk_key):
            k0, k1 = "st%d" % col, "st%d" % (col + 1)
            act(P, junk_ap, src_ap, AF.Square, skeys, [junk_key, k0], accum_out=stt_[:, col:col + 1])
            act(P, stt_[:, col + 1:col + 2], stt_[:, col:col + 1], AF.Sqrt, [k0], [k1], bias=EPS, scale=1.0 / n)
            P.op("vector", (lambda o_, i_: (lambda e: e.reciprocal(out=o_, in_=i_)))(stt_[:, col + 1:col + 2], stt_[:, col + 1:col + 2]),
                 [k1], [k1])
            return stt_[:, col + 1:col + 2], k1

        def transposes(src, skey, nk, b, dstT=None, dkey="actT"):
            if dstT is None:
                dstT = actT
            for k0 in range(0, nk, 4):
                pa = nxt("psA", 2)
                n_ = min(4, nk - k0)
                for kk in range(n_):
                    tr(P, psA[pa][:, kk, :], src[:, (k0 + kk) * 128:(k0 + kk + 1) * 128], ident[:], [skey, "ident"], ["psA%d" % pa])
                cp(P, ev_eng(), dstT[:, k0:k0 + n_, b * 128:(b + 1) * 128], psA[pa][:, 0:n_, :], ["psA%d" % pa], [dkey])

        def load_w(src3d, nk, ncol, skey):
            wb = nxt("wbuf", 3)
            dma(P, "sync", wbuf[wb][:, 0:nk, 0:ncol], src3d, [skey], ["wbuf%d" % wb])
            return wb

        def stepA1(ti, b):
            r0 = ti * TT3 + b * 128
            yb = b % 2
            dma(P, "sync", yin[yb][:, 0:CH], y_hy[r0:r0 + 128, :], (), ["yin%d" % yb])
            dma(P, "sync", yin[yb][:, CH:MIXC], y_nat[r0:r0 + 128, :], (), ["yin%d" % yb])
            r_h, kh_ = rstd_of(yin[yb][:, 0:CH], CH, ["yin%d" % yb], 12 + 4 * yb, mixb[yb][:, 0:CH], "mixb%d" % yb)
            r_n, kn_ = rstd_of(yin[yb][:, CH:MIXC], CN, ["yin%d" % yb], 14 + 4 * yb, mixb[yb][:, CH:MIXC], "mixb%d" % yb)
            ts(P, "vector", mixb[yb][:, 0:CH], yin[yb][:, 0:CH], r_h, None, ALU.mult, None, ["yin%d" % yb, kh_], ["mixb%d" % yb])
            act(P, mixb[yb][:, CH:MIXC], yin[yb][:, CH:MIXC], AF.Copy, ["yin%d" % yb, kn_], ["mixb%d" % yb], scale=r_n)

        def stepA2(ti, b):
            yb = b % 2
            transposes(mixb[yb], "mixb%d" % yb, KM, b, mixT, "mixT")

        for b in range(NB):
            stepA1(0, b)
            stepA2(0, b)
        NTL3 = T // TT3
        pre_wb = [None]
        for ti in range(NTL3):
            t0 = ti * TT3
            for b in range(NB):
                dma(P, "sync", xres[b][:], x[t0 + b * 128:t0 + (b + 1) * 128, :], (), ["xres%d" % b])
            for cc in range(D // DC):
                if cc == 0 and pre_wb[0] is not None:
                    wb = pre_wb[0]
                    pre_wb[0] = None
                else:
                    wb = load_w(Wb_out[:, cc * DC:(cc + 1) * DC].rearrange("(k p) c -> p k c", p=128), KM, DC, "Wb_out")
                for b in range(NB):
                    pm = nxt("psM", 6)
                    for k in range(KM):
                        mm(P, psM[pm][:, 0:DC], mixT[:, k, b * 128:(b + 1) * 128], wbuf[wb][:, k, 0:DC], k == 0, k == KM - 1,
                           ["mixT", "wbuf%d" % wb], ["psM%d" % pm])
                    sl = slice(cc * DC, (cc + 1) * DC)
                    tt(P, "vector", xres[b][:, sl], psM[pm][:, 0:DC], xres[b][:, sl], ALU.add, ["psM%d" % pm, "xres%d" % b], ["xres%d" % b])
            for b in range(NB):
                mbi = nxt("mbb", 2)
                r_m, km_ = rstd_of(xres[b][:], D, ["xres%d" % b], 4, mbb[mbi][:], "mbb%d" % mbi)
                if b % 2:
                    act(P, mbb[mbi][:], xres[b][:], AF.Copy, ["xres%d" % b, km_], ["mbb%d" % mbi], scale=r_m)
                else:
                    ts(P, "vector", mbb[mbi][:], xres[b][:], r_m, None, ALU.mult, None, ["xres%d" % b, km_], ["mbb%d" % mbi])
                transposes(mbb[mbi], "mbb%d" % mbi, KD, b)
            for s_ in range(NSL):
                ub = nxt("uT", 2)
                for sub in range(FFS // 512):
                    f0 = s_ * FFS + sub * 512
                    wb = load_w(Wb_up[:, f0:f0 + 512].rearrange("(k p) c -> p k c", p=128), KD, 512, "Wb_up")
                    for j in range(4):
                        pm = nxt("psM", 6)
                        for k in range(KD):
                            mm(P, psM[pm][:], wbuf[wb][:, k, j * 128:(j + 1) * 128], actT[:, k, :], k == 0, k == KD - 1,
                               ["actT", "wbuf%d" % wb], ["psM%d" % pm])
                        rb = nxt("rl", 2)
                        act(P, rl[rb][:], psM[pm][:], AF.Relu, ["psM%d" % pm], ["rl%d" % rb])
                        tt(P, "gpsimd", uT[ub][:, sub * 4 + j, :], rl[rb][:], rl[rb][:], ALU.mult, ["rl%d" % rb], ["uT%d" % ub])
                for cc in range(D // DC):
                    src = Wb_down[s_ * FFS:(s_ + 1) * FFS, cc * DC:(cc + 1) * DC].rearrange("(k p) c -> p k c", p=128)
                    wb = load_w(src, KG, DC, "Wb_down")
                    for b in range(NB):
                        pm = nxt("psM", 6)
                        for k in range(KG):
                            mm(P, psM[pm][:, 0:DC], uT[ub][:, k, b * 128:(b + 1) * 128], wbuf[wb][:, k, 0:DC], k == 0, k == KG - 1,
                               ["uT%d" % ub, "wbuf%d" % wb], ["psM%d" % pm])
                        sl = slice(cc * DC, (cc + 1) * DC)
                        tt(P, "vector", xres[b][:, sl], psM[pm][:, 0:DC], xres[b][:, sl], ALU.add, ["psM%d" % pm, "xres%d" % b], ["xres%d" % b])
                if ti + 1 < NTL3:
                    stp = [st_ for st_ in range(NB + 1) if (st_ * NSL) // (NB + 1) == s_] if NSL >= 2 else (list(range(NB + 1)) if s_ == 0 else [])
                    for st_ in stp:
                        if st_ >= 1:
                            stepA2(ti + 1, st_ - 1)
                        if st_ < NB:
                            stepA1(ti + 1, st_)
            if ti + 1 < NTL3:
                pre_wb[0] = load_w(Wb_out[:, 0:DC].rearrange("(k p) c -> p k c", p=128), KM, DC, "Wb_out")
            for b in range(NB):
                mbi = nxt("mbb", 2)
                r_f, kf_ = rstd_of(xres[b][:], D, ["xres%d" % b], 6 + 2 * (b % 2), mbb[mbi][:], "mbb%d" % mbi)
                ob_ = nxt("outb", 2)
                stt(P, "vector", outb[ob_][:], xres[b][:], r_f, gfin[:], ALU.mult, ALU.mult, ["xres%d" % b, kf_, "gfin"], ["outb%d" % ob_])
                dma(P, "gpq", y[t0 + b * 128:t0 + (b + 1) * 128, :], outb[ob_][:], ["outb%d" % ob_], ["y%d" % ob_])
        P.emit()


_CACHE = {}


def _consts(cfg, ctype):
    f32 = np.float32
    C = {}
    C["c_ident"] = np.eye(128, dtype=f32).astype(NPBF)
    ft = fft_tables(ctype)
    C["c_Enat"], C["c_Edat"], C["c_G"], C["c_W"] = ft["Enat"], ft["Edat"], ft["G"], ft["W"]
    fc = filter_consts(ctype)
    C["c_zT"], C["c_tcol"], C["c_mf"], C["c_mb"] = fc["zT"], fc["tcol"], fc["mf"], fc["mb"]
    maxd = math.log(1e-2) / 0.3
    mind = math.log(1e-2) / 1.5
    C["c_nad"] = (-np.abs(np.linspace(mind, maxd, cfg.CH, dtype=f32))).astype(f32)
    C["c_mask"] = nat_masks(ctype)
    C["c_flag"] = np.array([float(ctype)], f32)
    return C


def kernel(x_prompt, x_sample, norm_mix_g, w_in, hy_conv_w, hy_conv_b, hy_pe_w1, hy_pe_b1, hy_pe_w2, hy_pe_b2,
           hy_pe_w3, hy_pe_b3, hy_pe_freq, hy_pe_w4, hy_bias, nat_rpb, gnorm_hy, gnorm_nat, w_out, norm_mlp_g,
           w_up, w_down, norm_f_g):
    cfg = FULL
    f32 = np.float32
    A = lambda v: np.ascontiguousarray(np.asarray(v), dtype=f32)
    x_prompt, x_sample = A(x_prompt), A(x_sample)
    if "nc" not in _CACHE:
        _CACHE["nc"] = build_program(cfg)
        _CACHE["c"] = [_consts(cfg, 0), _consts(cfg, 1)]
    nc = _CACHE["nc"]
    idx, ok = nat_tt_index()
    rpb = A(nat_rpb)[0].reshape(cfg.NH, -1)
    ntt = np.where(ok[None], rpb[:, idx], f32(0.0)).astype(f32)
    shared = {
        "norm_mix_g": A(norm_mix_g)[0], "w_in": A(w_in)[0], "hy_conv_w": A(hy_conv_w)[0], "hy_conv_b": A(hy_conv_b)[0],
        "pe_w1": A(hy_pe_w1)[0], "pe_b1": A(hy_pe_b1)[0], "pe_w2": A(hy_pe_w2)[0], "pe_b2": A(hy_pe_b2)[0],
        "pe_w3": A(hy_pe_w3)[0], "pe_b3": A(hy_pe_b3)[0], "pe_freq": A(hy_pe_freq)[0], "pe_w4": A(hy_pe_w4)[0],
        "hy_bias": A(hy_bias)[0], "nat_tt": ntt,
        "gnorm": np.concatenate([A(gnorm_hy)[0], A(gnorm_nat)[0]]), "w_out": A(w_out)[0],
        "norm_mlp_g": A(norm_mlp_g)[0], "w_up": A(w_up)[0], "w_down": A(w_down)[0], "norm_f_g": A(norm_f_g),
    }
    in_maps = []
    for core in range(8):
        d = dict(shared)
        if core < 4:
            d["x"] = x_sample[core]
            d.update(_CACHE["c"][0])
        else:
            j = core - 4
            d["x"] = x_prompt[2 * j:2 * j + 2].reshape(cfg.T, cfg.D)
            d.update(_CACHE["c"][1])
        in_maps.append(d)
    res = run_bass_kernel_spmd(nc, in_maps, core_ids=list(range(8)))
    y_sample = np.stack([res.results[cidx]["y"] for cidx in range(4)], 0).astype(f32)
    y_prompt = np.concatenate([res.results[4 + j]["y"].reshape(2, 4096, cfg.D) for j in range(4)], 0).astype(f32)
    return (y_prompt, y_sample)
```

```python
import contextlib
import math
import numpy as np
import ml_dtypes
import concourse.bass as bass
import concourse.mybir as mybir
from concourse.bass_utils import run_bass_kernel_spmd

F32 = mybir.dt.float32
BF16 = mybir.dt.bfloat16
I32 = mybir.dt.int32
ALU = mybir.AluOpType
AF = mybir.ActivationFunctionType
NPBF = ml_dtypes.bfloat16

COMPUTE = ("tensor", "vector", "scalar", "gpsimd")
DMAQ = ("sync", "gpq")
NDMASEM = 8
NEG = -30000.0
EPS = 1e-5


class _Op:
    __slots__ = ("eng", "fn", "deps", "idx", "waited", "semslot", "semval")

    def __init__(self, eng, fn):
        self.eng = eng
        self.fn = fn
        self.deps = set()
        self.waited = False


class Prog:
    def __init__(self, nc, stack):
        self.nc = nc
        self.sems = {}
        for e in COMPUTE:
            self.sems[e] = stack.enter_context(nc.semaphore("s_" + e))
        for q in DMAQ:
            self.sems[q] = [stack.enter_context(nc.semaphore("s_%s%d" % (q, i))) for i in range(NDMASEM)]
        self.count = {e: 0 for e in COMPUTE}
        self.dcount = {q: [0] * NDMASEM for q in DMAQ}
        self.dnext = {q: 0 for q in DMAQ}
        self.reset_phase()

    def reset_phase(self):
        self.ops = []
        self.lastw = {}
        self.rd_c = {}
        self.rd_d = {}

    def op(self, eng, fn, reads=(), writes=()):
        o = _Op(eng, fn)
        o.idx = len(self.ops)
        deps = o.deps
        for k in reads:
            w = self.lastw.get(k)
            if w is not None:
                deps.add(w)
        for k in writes:
            w = self.lastw.get(k)
            if w is not None:
                deps.add(w)
            rc = self.rd_c.get(k)
            if rc:
                deps.update(rc.values())
            rd = self.rd_d.get(k)
            if rd:
                deps.update(rd)
        if eng in COMPUTE:
            for k in reads:
                self.rd_c.setdefault(k, {})[eng] = o.idx
        else:
            for k in reads:
                self.rd_d.setdefault(k, []).append(o.idx)
        for k in writes:
            self.lastw[k] = o.idx
            self.rd_c[k] = {}
            self.rd_d[k] = []
        deps.discard(o.idx)
        self.ops.append(o)
        return o

    def emit(self):
        nc = self.nc
        ops = self.ops
        phys = {"tensor": "tensor", "vector": "vector", "scalar": "scalar", "gpsimd": "gpsimd",
                "sync": "sync", "gpq": "gpsimd"}
        for o in ops:
            best = {}
            dl = []
            for d in o.deps:
                od = ops[d]
                if od.eng in COMPUTE:
                    if od.eng == "tensor" and o.eng == "tensor":
                        continue
                    if od.eng not in best or best[od.eng] < d:
                        best[od.eng] = d
                else:
                    dl.append(d)
            o.deps = set(best.values()) | set(dl)
            for d in o.deps:
                ops[d].waited = True
        lastc = {}
        for o in ops:
            if o.eng in COMPUTE:
                lastc[o.eng] = o
        for o in lastc.values():
            o.waited = True
        for o in ops:
            if o.eng in COMPUTE:
                if o.waited:
                    self.count[o.eng] += 1
                    o.semval = self.count[o.eng]
            else:
                slot = self.dnext[o.eng] % NDMASEM
                self.dnext[o.eng] += 1
                self.dcount[o.eng][slot] += 16
                o.semslot = slot
                o.semval = self.dcount[o.eng][slot]
        streams = {"tensor": [], "vector": [], "scalar": [], "gpsimd": [], "sync": []}
        for o in ops:
            streams[phys[o.eng]].append(o)
        sems = self.sems

        def run(e, lst):
            seen = {}
            for o in lst:
                waits = []
                if o.eng in DMAQ and o.semval > 16:
                    waits.append((sems[o.eng][o.semslot], o.semval - 16))
                for d in sorted(o.deps):
                    od = ops[d]
                    if od.eng in COMPUTE:
                        waits.append((sems[od.eng], od.semval))
                    else:
                        waits.append((sems[od.eng][od.semslot], od.semval))
                for (s, v) in waits:
                    key = id(s)
                    if seen.get(key, 0) >= v:
                        continue
                    seen[key] = v
                    e.wait_ge(s, v)
                ins = o.fn(e)
                if o.eng in COMPUTE:
                    if o.waited:
                        ins.then_inc(sems[o.eng], 1)
                else:
                    ins.then_inc(sems[o.eng][o.semslot], 16)
            for c in COMPUTE:
                if self.count[c] > seen.get(id(sems[c]), 0):
                    e.wait_ge(sems[c], self.count[c])
            for q in DMAQ:
                for i in range(NDMASEM):
                    if self.dcount[q][i] > seen.get(id(sems[q][i]), 0):
                        e.wait_ge(sems[q][i], self.dcount[q][i])

        with nc.Block() as block:
            @block.tensor
            def _(e):
                run(e, streams["tensor"])

            @block.vector
            def _(e):
                run(e, streams["vector"])

            @block.scalar
            def _(e):
                run(e, streams["scalar"])

            @block.gpsimd
            def _(e):
                run(e, streams["gpsimd"])

            @block.sync
            def _(e):
                run(e, streams["sync"])
        self.reset_phase()


def dma(P, q, out, in_, reads, writes, slow=False):
    if slow:
        return P.op(q, lambda e: e.dma_start(out=out, in_=in_, allow_slow_non_contiguous=True), reads, writes)
    return P.op(q, lambda e: e.dma_start(out=out, in_=in_), reads, writes)


def mm(P, out, lhsT, rhs, start, stop, reads, writes):
    return P.op("tensor", lambda e: e.matmul(out, lhsT=lhsT, rhs=rhs, start=start, stop=stop), reads, writes)


def tr(P, out, in_, ident, reads, writes):
    return P.op("tensor", lambda e: e.transpose(out, in_, ident), reads, writes)


def act(P, out, in_, func, reads, writes, bias=None, scale=None, accum_out=None):
    kw = {}
    if bias is not None:
        kw["bias"] = bias
    if scale is not None:
        kw["scale"] = scale
    if accum_out is not None:
        kw["accum_out"] = accum_out
    return P.op("scalar", lambda e: e.activation(out=out, in_=in_, func=func, **kw), reads, writes)


def ts(P, eng, out, in0, s1, s2, op0, op1, reads, writes):
    if s2 is None:
        return P.op(eng, lambda e: e.tensor_scalar(out=out, in0=in0, scalar1=s1, scalar2=None, op0=op0), reads, writes)
    return P.op(eng, lambda e: e.tensor_scalar(out=out, in0=in0, scalar1=s1, scalar2=s2, op0=op0, op1=op1), reads, writes)


def tt(P, eng, out, in0, in1, op, reads, writes):
    return P.op(eng, lambda e: e.tensor_tensor(out=out, in0=in0, in1=in1, op=op), reads, writes)


def stt(P, eng, out, in0, scalar, in1, op0, op1, reads, writes):
    eng = "vector"
    return P.op(eng, lambda e: e.scalar_tensor_tensor(out=out, in0=in0, scalar=scalar, in1=in1, op0=op0, op1=op1),
                reads, writes)


def cp(P, eng, out, in_, reads, writes):
    if eng == "scalar":
        return P.op(eng, lambda e: e.activation(out=out, in_=in_, func=AF.Copy), reads, writes)
    return P.op(eng, lambda e: e.tensor_copy(out=out, in_=in_), reads, writes)


def mset(P, eng, ap, val, writes):
    return P.op(eng, lambda e: e.memset(ap, val), (), writes)


class Cfg:
    def __init__(self, D=2048, CH=1024, NH=16, DFF=8192):
        self.D = D
        self.CH = CH
        self.NH = NH
        self.CN = NH * 64
        self.DFF = DFF
        self.T = 8192
        self.KD = D // 128
        self.NCOL = 3 * CH + 3 * self.CN
        self.CB = min(512, CH)
        self.NCB = CH // self.CB
        self.DC = min(512, D)
        self.KF = DFF // 128
        self.KG = min(16, self.KF)
        self.MIXC = CH + self.CN
        self.KM = self.MIXC // 128


FULL = Cfg()

NAT_TYPES = ["int", "m0", "m1", "m30", "m31", "m32", "m33", "m62", "m63"]


def nat_block(m):
    if m == 0:
        return "m0", 0, 8
    if m == 1:
        return "m1", 0, 8
    if m == 62:
        return "m62", 120, 8
    if m == 63:
        return "m63", 120, 8
    if m in (31, 32, 33):
        return "m%d" % m, 2 * m - 6, 14
    if m == 30:
        return "m30", 56, 9
    return "int", 2 * m - 4, 9


def _n1p(ctype):
    n1 = np.arange(64)
    if ctype == 0:
        return n1
    return np.where(n1 < 32, n1, n1 + 32)


def fft_tables(ctype):
    N = 16384
    n2 = np.arange(128)[:, None, None]
    k1 = (np.arange(64) + 0.5)[None, None, :]
    out = {}
    for name, pl in (("nat", _n1p(0)), ("dat", _n1p(ctype))):
        n1p = pl[None, :, None]
        th = 2.0 * np.pi * np.mod((128 * n1p + n2) * k1, N) / N
        e = np.stack([np.cos(th), -np.sin(th)], axis=1)
        out["E" + name] = np.ascontiguousarray(e.transpose(2, 0, 1, 3)).reshape(64, 128 * 128)
        if name == "dat":
            g = np.stack([np.cos(th) * 2.0 / N, -np.sin(th) * 2.0 / N], axis=1)
            out["G"] = np.ascontiguousarray(g.transpose(1, 3, 0, 2)).reshape(128, 128 * 64)
    a = np.arange(128)
    ph = 2.0 * np.pi * (np.outer(a, a) % 128) / 128.0
    wre, wim = np.cos(ph), -np.sin(ph)
    out["W"] = np.concatenate([wre, wim, -wim, -wre], axis=1)
    return {k: v.astype(NPBF) for k, v in out.items()}


def filter_consts(ctype):
    T = 8192
    L = T if ctype == 0 else 4096
    f32 = np.float32
    j = np.arange(T)
    valid = j < L
    jj = np.where(valid, j, 0)
    t = (jj.astype(np.float64) / (L - 1)).astype(f32)
    bands = 16
    w_ang = (f32(2.0 * math.pi / L) * jj.astype(f32)).astype(f32)
    f = np.linspace(1e-4, bands - 1, bands, dtype=f32)
    ang = (w_ang[:, None] * f[None, :]).astype(f32)
    z = np.concatenate([t[:, None], np.cos(ang), -np.sin(ang)], axis=-1).astype(f32)
    zT = np.ascontiguousarray(z.T)
    tcol = np.ascontiguousarray(t.reshape(64, 128))
    mf = valid.astype(f32).reshape(64, 128)
    mb = mf.copy()
    mb[0, 0] = 0.0
    return {"zT": zT, "tcol": tcol, "mf": np.ascontiguousarray(mf), "mb": np.ascontiguousarray(mb)}


def nat_masks(ctype):
    def window(i):
        if ctype == 0:
            return int(np.clip(i - 4, 0, 120))
        base = 0 if i < 64 else 64
        return base + int(np.clip(i - base - 4, 0, 56))
    cols = np.arange(64)
    cs = np.clip(cols - 8, 0, 48)
    colok = (cols[None, :] >= cs[:, None]) & (cols[None, :] < cs[:, None] + 16)
    reps = {"int": 10, "m0": 0, "m1": 1, "m30": 30, "m31": 31, "m32": 32, "m33": 33, "m62": 62, "m63": 63}
    out = np.full((len(NAT_TYPES), 128, 896), NEG, np.float32)
    for ti, tn in enumerate(NAT_TYPES):
        m = reps[tn]
        _, ks, nr = nat_block(m)
        for ri in range(2):
            i = 2 * m + ri
            rs = window(i)
            for a in range(nr):
                r = ks + a
                if rs <= r < rs + 8:
                    blk = np.where(colok, 0.0, NEG)
                    out[ti, ri * 64:(ri + 1) * 64, a * 64:(a + 1) * 64] = blk
    return out.astype(NPBF)


def nat_tt_index():
    p = np.arange(128)
    ri, j = p // 64, p % 64
    B = np.arange(15)
    kc = np.arange(64)
    ro = B[None, :, None] - ri[:, None, None]
    co = 15 + kc[None, None, :] - j[:, None, None]
    ok = (ro >= 0) & (ro <= 14) & (co >= 0) & (co <= 30)
    idx = np.clip(ro, 0, 14) * 31 + np.clip(co, 0, 30)
    return idx.reshape(128, 960), ok.reshape(128, 960)


def build_program(cfg, phases=("w", "p1", "hy", "nat", "p3"), debug=False):
    c = cfg
    D, CH, NH, CN, DFF, T, KD, NCOL, CB, NCB = c.D, c.CH, c.NH, c.CN, c.DFF, c.T, c.KD, c.NCOL, c.CB, c.NCB
    nc = bass.Bass("TRN2", target_bir_lowering=False)

    def din(name, shape, dt=F32):
        return nc.dram_tensor(name, list(shape), dt, kind="ExternalInput").ap()

    okind = "ExternalOutput" if debug else "Internal"

    def dscr(name, shape, dt=BF16):
        if debug and name in debug:
            return nc.dram_tensor(name, list(shape), dt, kind="ExternalOutput").ap()
        return nc.dram_tensor(name, list(shape), dt).ap()

    x = din("x", [T, D])
    norm_mix_g = din("norm_mix_g", [D])
    w_in = din("w_in", [D, NCOL])
    hy_conv_w = din("hy_conv_w", [3, 3 * CH])
    hy_conv_b = din("hy_conv_b", [3 * CH])
    pe_w1 = din("pe_w1", [33, 64])
    pe_b1 = din("pe_b1", [64])
    pe_w2 = din("pe_w2", [64, 64])
    pe_b2 = din("pe_b2", [64])
    pe_w3 = din("pe_w3", [64, 64])
    pe_b3 = din("pe_b3", [64])
    pe_freq = din("pe_freq", [64])
    pe_w4 = din("pe_w4", [64, 4 * CH])
    hy_bias = din("hy_bias", [2, CH])
    nat_tt = din("nat_tt", [NH, 128, 960])
    gnorm = din("gnorm", [c.MIXC])
    w_out = din("w_out", [c.MIXC, D])
    norm_mlp_g = din("norm_mlp_g", [D])
    w_up = din("w_up", [D, DFF])
    w_down = din("w_down", [DFF, D])
    norm_f_g = din("norm_f_g", [D])
    c_ident = din("c_ident", [128, 128], BF16)
    c_Enat = din("c_Enat", [64, 128 * 128], BF16)
    c_Edat = din("c_Edat", [64, 128 * 128], BF16)
    c_G = din("c_G", [128, 128 * 64], BF16)
    c_W = din("c_W", [128, 512], BF16)
    c_zT = din("c_zT", [33, T])
    c_tcol = din("c_tcol", [64, 128])
    c_mf = din("c_mf", [64, 128])
    c_mb = din("c_mb", [64, 128])
    c_nad = din("c_nad", [CH])
    c_mask = din("c_mask", [len(NAT_TYPES), 128, 896], BF16)
    c_flag = din("c_flag", [1])
    y = nc.dram_tensor("y", [T, D], F32, kind="ExternalOutput").ap()
    Wb_in = dscr("Wb_in", [D, NCOL])
    Wb_out = dscr("Wb_out", [c.MIXC, D])
    Wb_up = dscr("Wb_up", [D, DFF])
    Wb_down = dscr("Wb_down", [DFF, D])
    u_hy = dscr("u_hy", [T + 2, 3 * CH])
    qT = dscr("qT", [CN, T])
    kT = dscr("kT", [CN, T])
    vn1 = dscr("vn1", [T, NH, 65])
    S1 = dscr("S1", [4, 2, 64, 128, CB])
    S2 = dscr("S2", [2, 128, 64, CB])
    KH = dscr("KH", [2, NCB, 128, 64, 2, CB])
    z1d = dscr("z1d", [T, CH])
    y_hy = dscr("y_hy", [T, CH])
    y_nat = dscr("y_nat", [T, CN])

    with contextlib.ExitStack() as top:
        P = Prog(nc, top)
        if "w" in phases:
            _phase_w(nc, P, c, locals())
        if "p1" in phases:
            _phase_p1(nc, P, c, locals())
        if "hy" in phases:
            _phase_hy(nc, P, c, locals())
        if "nat" in phases:
            _phase_nat(nc, P, c, locals())
        if "p3" in phases:
            _phase_p3(nc, P, c, locals())
    return nc


def _colvec_load(P, q, tile, key, src, n):
    v = src.rearrange("(k p) -> p k", p=128)
    dma(P, q, tile[:, 0:n], v, (), [key], slow=True)


def _phase_w(nc, P, c, A):
    D, CH, CN, DFF, NCOL = c.D, c.CH, c.CN, c.DFF, c.NCOL
    with contextlib.ExitStack() as st:
        S = lambda n, sh, dt: st.enter_context(nc.sbuf_tensor("w_" + n, sh, dt))
        gm = S("gm", [128, c.KD], F32)
        gq = S("gq", [128, c.KD], F32)
        gl = S("gl", [128, c.KD], F32)
        gh = S("gh", [128, c.KM], F32)
        one = S("one", [128, 1], F32)
        _colvec_load(P, "sync", gm, "gm", A["norm_mix_g"], c.KD)
        _colvec_load(P, "sync", gl, "gl", A["norm_mlp_g"], c.KD)
        _colvec_load(P, "sync", gh, "gh", A["gnorm"], c.KM)
        mset(P, "vector", one[:], 1.0, ["one"])
        ts(P, "vector", gq[:], gm[:], 0.125, None, ALU.mult, None, ["gm"], ["gq"])
        CW = 2048
        NWB = 6
        stg = [S("wst%d" % i, [128, CW], F32) for i in range(NWB)]
        stb = [S("wsb%d" % i, [128, CW], BF16) for i in range(NWB)]
        jobs = []
        q0, q1 = 3 * CH, 3 * CH + CN
        for r in range(D // 128):
            for (c0, c1, sc, sk) in ((0, q0, gm, "gm"), (q0, q1, gq, "gq"), (q1, NCOL, gm, "gm")):
                for cc in range(c0, c1, CW):
                    jobs.append((A["w_in"], A["Wb_in"], r, cc, min(CW, c1 - cc), sc, sk, r))
        engs = ["scalar", "vector"]
        for i, (src, dst, r, cc, w, sc, sk, si) in enumerate(jobs):
            b = i % NWB
            dma(P, "sync", stg[b][:, 0:w], src[r * 128:(r + 1) * 128, cc:cc + w], (), ["wst%d" % b])
            eng = engs[i % 2]
            if eng == "scalar":
                act(P, stb[b][:, 0:w], stg[b][:, 0:w], AF.Copy, ["wst%d" % b, sk], ["wsb%d" % b], scale=sc[:, si:si + 1])
            else:
                ts(P, eng, stb[b][:, 0:w], stg[b][:, 0:w], sc[:, si:si + 1], None, ALU.mult, None,
                   ["wst%d" % b, sk], ["wsb%d" % b])
            dma(P, "gpq", dst[r * 128:(r + 1) * 128, cc:cc + w], stb[b][:, 0:w], ["wsb%d" % b], [dst.tensor.name])
        P.emit()


def _phase_p1(nc, P, c, A):
    D, CH, CN, NH, T, KD, NCOL = c.D, c.CH, c.CN, c.NH, c.T, c.KD, c.NCOL
    TT1 = 1024
    NB = TT1 // 128
    NCC = 3 * CH // 128
    WG = 512
    x, Wb_in, u_hy, qT, kT, vn1 = A["x"], A["Wb_in"], A["u_hy"], A["qT"], A["kT"], A["vn1"]
    with contextlib.ExitStack() as st:
        S = lambda n, sh, dt: st.enter_context(nc.sbuf_tensor("p1_" + n, sh, dt))
        PS = lambda n, sh, dt: st.enter_context(nc.psum_tensor("p1_" + n, sh, dt))
        ident = S("ident", [128, 128], BF16)
        dma(P, "sync", ident[:], A["c_ident"], (), ["ident"])
        cw = S("cw", [128, NCC, 3], F32)
        cb = S("cb", [128, NCC], F32)
        for k in range(3):
            dma(P, "sync", cw[:, :, k], A["hy_conv_w"][k].rearrange("(k p) -> p k", p=128), (), ["cw"], slow=True)
        _colvec_load(P, "sync", cb, "cb", A["hy_conv_b"], NCC)
        flag = S("flag", [128, 1], F32)
        dma(P, "sync", flag[:], bass.AP(tensor=A["c_flag"].tensor, offset=0, ap=[[0, 128], [1, 1]]), (), ["flag"])
        nfw = S("nfw", [128, NCC, 3], F32)
        ts(P, "vector", nfw[:].rearrange("p a b -> p (a b)"), cw[:].rearrange("p a b -> p (a b)"), flag[:, 0:1], -1.0,
           ALU.mult, ALU.mult, ["cw", "flag"], ["nfw"])
        carry = S("carry", [128, NCC, 2], F32)
        mset(P, "vector", carry[:].rearrange("p a b -> p (a b)"), 0.0, ["carry"])
        xs = [S("xs%d" % i, [128, D], F32) for i in range(2)]
        ab = [S("ab%d" % i, [128, D], BF16) for i in range(2)]
        junk = S("junk", [128, D], F32)
        ss = S("ss", [128, 2 * NB], F32)
        aT = [S("aT%d" % i, [128, KD, TT1], BF16) for i in range(2)]
        wbuf = [S("wbuf%d" % i, [128, KD, WG], BF16) for i in range(2)]
        R = [S("R%d" % i, [128, 516], F32) for i in range(2)]
        o1 = [S("o1_%d" % i, [128, 512], F32) for i in range(2)]
        ob = [S("ob%d" % i, [128, 512], BF16) for i in range(2)]
        hst = [S("hst%d" % i, [128, 4, 512], BF16) for i in range(2)]
        qst = [S("qst%d" % i, [128, TT1], BF16) for i in range(2)]
        vst = [S("vst%d" % i, [128, 8, 65], BF16) for i in range(2)]
        for i in range(2):
            mset(P, "vector", vst[i][:].rearrange("p a b -> p (a b)"), 1.0, ["vst%d" % i])
        zrow = S("zrow", [1, 3 * CH], BF16)
        mset(P, "vector", zrow[:], 0.0, ["zrow"])
        dma(P, "gpq", u_hy[0:1, :], zrow[:], ["zrow"], ["u_hy_pad"])
        dma(P, "gpq", u_hy[T + 1:T + 2, :], zrow[:], ["zrow"], ["u_hy_pad"])
        psA = [PS("psA%d" % i, [128, 4, 128], BF16) for i in range(2)]
        psM = [PS("psM%d" % i, [128, 512], F32) for i in range(4)]
        psH = [PS("psH%d" % i, [128, 8, 128], BF16) for i in range(2)]
        nwg = NCOL // WG
        wcount = [0]

        def load_w(g):
            b = wcount[0] % 2
            wcount[0] += 1
            src = Wb_in[:, g * WG:(g + 1) * WG].rearrange("(k p) c -> p k c", p=128)
            dma(P, "sync", wbuf[b][:], src, ["Wb_in"], ["wbuf%d" % b])
            return b

        evi = [0]

        def ev_eng():
            evi[0] += 1
            return "scalar" if evi[0] % 2 else "vector"

        mcount = [0]
        hcount = [0]
        NTL = T // TT1
        DFF = c.DFF
        bgl = S("bgl", [128, KD], F32)
        bgh = S("bgh", [128, c.KM], F32)
        bone = S("bone", [128, 1], F32)
        _colvec_load(P, "sync", bgl, "bgl", A["norm_mlp_g"], KD)
        _colvec_load(P, "sync", bgh, "bgh", A["gnorm"], c.KM)
        mset(P, "vector", bone[:], 1.0, ["bone"])
        BCW = 1024
        NBG = 4
        bst = [S("bst%d" % i, [128, BCW], F32) for i in range(NBG)]
        bsb = [S("bsb%d" % i, [128, BCW], BF16) for i in range(NBG)]
        bjobs = []
        for r in range(c.MIXC // 128):
            for cc in range(0, D, BCW):
                bjobs.append((A["w_out"], A["Wb_out"], r, cc, min(BCW, D - cc), bgh, "bgh", r))
        for r in range(D // 128):
            for cc in range(0, DFF, BCW):
                bjobs.append((A["w_up"], A["Wb_up"], r, cc, min(BCW, DFF - cc), bgl, "bgl", r))
        for r in range(DFF // 128):
            for cc in range(0, D, BCW):
                bjobs.append((A["w_down"], A["Wb_down"], r, cc, min(BCW, D - cc), bone, "bone", 0))
        bgi = [0]

        def bg(n):
            for _ in range(n):
                i = bgi[0]
                if i >= len(bjobs):
                    return
                bgi[0] += 1
                src, dst, r, cc, w, sc, sk, si = bjobs[i]
                b = i % NBG
                dma(P, "sync", bst[b][:, 0:w], src[r * 128:(r + 1) * 128, cc:cc + w], (), ["bst%d" % b])
                ts(P, "vector", bsb[b][:, 0:w], bst[b][:, 0:w], sc[:, si:si + 1], None, ALU.mult, None, ["bst%d" % b, sk], ["bsb%d" % b])
                dma(P, "gpq", dst[r * 128:(r + 1) * 128, cc:cc + w], bsb[b][:, 0:w], ["bsb%d" % b], ["bg_" + dst.tensor.name])
        bg_per = -(-len(bjobs) // (NTL * (NCOL // WG)))

        def prepA(ti, b):
            t0 = ti * TT1
            xb = b % 2
            dma(P, "sync", xs[xb][:], x[t0 + b * 128:t0 + (b + 1) * 128, :], (), ["xs%d" % xb])
            act(P, junk[:], xs[xb][:], AF.Square, ["xs%d" % xb], ["junk", "ss%d" % b], accum_out=ss[:, b:b + 1])
            act(P, ss[:, NB + b:NB + b + 1], ss[:, b:b + 1], AF.Sqrt, ["ss%d" % b], ["sr%d" % b], bias=EPS, scale=1.0 / D)
            P.op("vector", (lambda o, i: (lambda e: e.reciprocal(out=o, in_=i)))(ss[:, NB + b:NB + b + 1], ss[:, NB + b:NB + b + 1]),
                 ["sr%d" % b], ["sr%d" % b])
            if b % 2:
                ts(P, "vector", ab[xb][:], xs[xb][:], ss[:, NB + b:NB + b + 1], None, ALU.mult, None,
                   ["xs%d" % xb, "sr%d" % b], ["ab%d" % xb])
            else:
                act(P, ab[xb][:], xs[xb][:], AF.Copy, ["xs%d" % xb, "sr%d" % b], ["ab%d" % xb], scale=ss[:, NB + b:NB + b + 1])

        def prepB(ti, b):
            xb = b % 2
            at = aT[ti % 2]
            for k0 in range(0, KD, 4):
                pa = (k0 // 4) % 2
                nk = min(4, KD - k0)
                for kk in range(nk):
                    tr(P, psA[pa][:, kk, :], ab[xb][:, (k0 + kk) * 128:(k0 + kk + 1) * 128], ident[:],
                       ["ab%d" % xb, "ident"], ["psA%d" % pa])
                cp(P, ev_eng(), at[:, k0:k0 + nk, b * 128:(b + 1) * 128], psA[pa][:, 0:nk, :], ["psA%d" % pa], ["aT%d" % (ti % 2)])

        for b in range(NB):
            prepA(0, b)
            prepB(0, b)
        pending = []

        def flush_pending():
            while pending:
                pending.pop(0)()

        for ti in range(NTL):
            t0 = ti * TT1
            at = aT[ti % 2]
            atk = "aT%d" % (ti % 2)
            sched = {}
            if ti + 1 < NTL:
                for st_ in range(NB + 1):
                    sched.setdefault((st_ * nwg) // (NB + 1), []).append(st_)
            for g in range(nwg):
                wb = load_w(g)
                wk = "wbuf%d" % wb
                col0 = g * WG
                bg(bg_per)
                for st_ in sched.get(g, ()):
                    if st_ >= 1:
                        prepB(ti + 1, st_ - 1)
                    if st_ < NB:
                        prepA(ti + 1, st_)
                if col0 < 3 * CH:
                    for half in range(TT1 // 512):
                        s = ti * (TT1 // 512) + half
                        hb = hcount[0] % 2
                        hcount[0] += 1
                        for j in range(WG // 128):
                            cc = col0 // 128 + j
                            pm = mcount[0] % 4
                            mcount[0] += 1
                            for k in range(KD):
                                mm(P, psM[pm][:], wbuf[wb][:, k, j * 128:(j + 1) * 128], at[:, k, half * 512:(half + 1) * 512],
                                   k == 0, k == KD - 1, [wk, atk], ["psM%d" % pm])
                            rb = cc % 2
                            rk = "R%d" % rb
                            cp(P, "scalar", R[rb][:, 2:514], psM[pm][:], ["psM%d" % pm], [rk])
                            cp(P, "gpsimd", R[rb][:, 0:2], carry[:, cc, :], ["carry%d" % cc], [rk])
                            act(P, o1[rb][:], R[rb][:, 0:512], AF.Identity, [rk, "cw", "cb"], ["o1_%d" % rb],
                                bias=cb[:, cc:cc + 1], scale=cw[:, cc, 0:1])
                            stt(P, "vector", o1[rb][:], R[rb][:, 1:513], cw[:, cc, 1:2], o1[rb][:], ALU.mult, ALU.add,
                                [rk, "cw", "o1_%d" % rb], ["o1_%d" % rb])
                            stt(P, "vector", ob[rb][:], R[rb][:, 2:514], cw[:, cc, 2:3], o1[rb][:], ALU.mult, ALU.add,
                                [rk, "cw", "o1_%d" % rb], ["ob%d" % rb])
                            cp(P, "gpsimd", carry[:, cc, :], R[rb][:, 512:514], [rk], ["carry%d" % cc])
                            if s == 8:
                                stt(P, "vector", ob[rb][:, 0:1], R[rb][:, 2:3], nfw[:, cc, 2:3], ob[rb][:, 0:1], ALU.mult, ALU.add,
                                    [rk, "nfw", "ob%d" % rb], ["ob%d" % rb])
                                stt(P, "vector", ob[rb][:, 1:2], R[rb][:, 1:2], nfw[:, cc, 0:1], ob[rb][:, 1:2], ALU.mult, ALU.add,
                                    [rk, "nfw", "ob%d" % rb], ["ob%d" % rb])
                            flush_pending()

                            def post(j=j, rb=rb, hb=hb, s=s, col0=col0):
                                for q in range(4):
                                    tr(P, psH[q // 2][:, (q % 2) * 4 + j, :], ob[rb][:, q * 128:(q + 1) * 128], ident[:], ["ob%d" % rb, "ident"],
                                       ["psH%d" % (q // 2)])
                                if j == WG // 128 - 1:
                                    for q in range(4):
                                        cp(P, ev_eng(), hst[hb][:, q, :], psH[q // 2][:, (q % 2) * 4:(q % 2) * 4 + 4, :].rearrange("p a b -> p (a b)"),
                                           ["psH%d" % (q // 2)], ["hst%d" % hb])
                                    r0 = 512 * s
                                    dst = u_hy[r0:r0 + 512, col0:col0 + WG].rearrange("(q p) c -> p q c", p=128)
                                    dma(P, "gpq", dst, hst[hb][:], ["hst%d" % hb], ["u_hy%d" % hb])
                            pending.append(post)
                elif col0 < 3 * CH + 2 * CN:
                    dstT = qT if col0 < 3 * CH + CN else kT
                    cbase = col0 - (3 * CH if col0 < 3 * CH + CN else 3 * CH + CN)
                    for j in range(WG // 128):
                        qb = j % 2
                        for half in range(TT1 // 512):
                            pm = mcount[0] % 4
                            mcount[0] += 1
                            for k in range(KD):
                                mm(P, psM[pm][:], wbuf[wb][:, k, j * 128:(j + 1) * 128], at[:, k, half * 512:(half + 1) * 512],
                                   k == 0, k == KD - 1, [wk, atk], ["psM%d" % pm])
                            flush_pending()
                            cp(P, ev_eng(), qst[qb][:, half * 512:(half + 1) * 512], psM[pm][:], ["psM%d" % pm], ["qst%d" % qb])
                        dma(P, "gpq", dstT[cbase + j * 128:cbase + (j + 1) * 128, t0:t0 + TT1], qst[qb][:], ["qst%d" % qb], ["qkT%d" % qb])
                else:
                    h0 = (col0 - 3 * CH - 2 * CN) // 64
                    for b in range(NB):
                        pm = mcount[0] % 4
                        mcount[0] += 1
                        vb = b % 2
                        for k in range(KD):
                            mm(P, psM[pm][:], at[:, k, b * 128:(b + 1) * 128], wbuf[wb][:, k, :], k == 0, k == KD - 1,
                               [wk, atk], ["psM%d" % pm])
                        cp(P, ev_eng(), vst[vb][:, :, 0:64], psM[pm][:].rearrange("p (h d) -> p h d", d=64), ["psM%d" % pm], ["vst%d" % vb])
                        dma(P, "gpq", vn1[t0 + b * 128:t0 + (b + 1) * 128, h0:h0 + 8, :], vst[vb][:], ["vst%d" % vb], ["vn1_%d" % vb])
        flush_pending()
        bg(len(bjobs))
        fl = S("fl", [128, NCC, 2], F32)
        flb = S("flb", [128, NCC], BF16)
        frow = S("frow", [1, NCC * 128], BF16)
        keys = ["carry%d" % i for i in range(NCC)]
        tt(P, "vector", fl[:], carry[:], cw[:, :, 0:2], ALU.mult, keys + ["cw", "fl"], ["fl"])
        tt(P, "vector", fl[:, :, 0], fl[:, :, 0], fl[:, :, 1], ALU.add, ["fl"], ["fl"])
        tt(P, "vector", flb[:], fl[:, :, 0], cb[:], ALU.add, ["fl", "cb"], ["flb"])
        for c0 in range(0, NCC, 4):
            pa = (c0 // 4) % 2
            for kk in range(4):
                tr(P, psA[pa][0:1, kk, :], flb[:, c0 + kk:c0 + kk + 1], ident[:], ["flb", "ident"], ["psA%d" % pa])
            cp(P, "vector", frow[:, c0 * 128:(c0 + 4) * 128], psA[pa][0:1, :, :].rearrange("p a b -> p (a b)"), ["psA%d" % pa], ["frow"])
        dma(P, "gpq", u_hy[T:T + 1, :], frow[:], ["frow"], ["u_hy"])
        P.emit()


def _phase_hy(nc, P, c, A):
    CH, T, CB, NCB = c.CH, c.T, c.CB, c.NCB
    u_hy, S1, S2, KH, y_hy, z1d = A["u_hy"], A["S1"], A["S2"], A["KH"], A["y_hy"], A["z1d"]
    NG = 4
    KGRP = 2
    LOOK = 2
    with contextlib.ExitStack() as st0:
        S0 = lambda n, sh, dt: st0.enter_context(nc.sbuf_tensor("hy_" + n, sh, dt))
        h3b = S0("h3b", [64, T], BF16)
        w4b = S0("w4b", [64, 4 * CH], BF16)
        with contextlib.ExitStack() as st:
            S = lambda n, sh, dt: st.enter_context(nc.sbuf_tensor("hy_" + n, sh, dt))
            PS = lambda n, sh, dt: st.enter_context(nc.psum_tensor("hy_" + n, sh, dt))
            zT = S("zT", [33, T], F32)
            dma(P, "sync", zT[:], A["c_zT"], (), ["zT"])
            w1 = S("w1", [33, 64], F32)
            w2 = S("w2", [64, 64], F32)
            w3 = S("w3", [64, 64], F32)
            dma(P, "sync", w1[:], A["pe_w1"], (), ["w1"])
            dma(P, "sync", w2[:], A["pe_w2"], (), ["w2"])
            dma(P, "sync", w3[:], A["pe_w3"], (), ["w3"])
            pv = S("pv", [64, 8], F32)
            for i, nm in enumerate(["pe_freq", "pe_b1", "pe_b2", "pe_b3"]):
                dma(P, "sync", pv[:, i:i + 1], A[nm].rearrange("(p o) -> p o", o=1), (), ["pv"], slow=True)
            ts(P, "vector", pv[:, 4:5], pv[:, 0:1], 1.0 / (2 * math.pi), None, ALU.mult, None, ["pv"], ["pv"])
            for i in range(3):
                tt(P, "vector", pv[:, 5 + i:6 + i], pv[:, 4:5], pv[:, 1 + i:2 + i], ALU.mult, ["pv"], ["pv"])
            hA = S("hA", [64, T], F32)
            hB = S("hB", [64, T], F32)
            w4f = S("w4f", [64, 4 * CH], F32)
            dma(P, "sync", w4f[:], A["pe_w4"], (), ["w4f"])
            cp(P, "vector", w4b[:], w4f[:], ["w4f"], ["w4b"])
            uu = [S("uu%d" % i, [64, 512], F32) for i in range(2)]
            ui = [S("ui%d" % i, [64, 512], I32) for i in range(2)]
            uf = [S("uf%d" % i, [64, 512], F32) for i in range(2)]
            psF = [PS("psF%d" % i, [64, 512], F32) for i in range(2)]
            layers = [(w1, "w1", zT, hA, 33), (w2, "w2", hA, hB, 64), (w3, "w3", hB, None, 64)]
            n = 0
            for li, (wl, wk, src, dst, K) in enumerate(layers):
                for ch in range(T // 512):
                    b = n % 2
                    n += 1
                    sl = slice(ch * 512, (ch + 1) * 512)
                    srck = "zT" if li == 0 else "h%d_%d" % (li - 1, ch)
                    mm(P, psF[b][:], wl[0:K, :], src[0:K, sl], True, True, [wk, srck], ["psF%d" % b])
                    ts(P, "vector", uu[b][:], psF[b][:], pv[:, 4:5], pv[:, 5 + li:6 + li], ALU.mult, ALU.add,
                       ["psF%d" % b, "pv"], ["uu%d" % b])
                    cp(P, "vector", ui[b][:], uu[b][:], ["uu%d" % b], ["ui%d" % b])
                    cp(P, "gpsimd", uf[b][:], ui[b][:], ["ui%d" % b], ["uf%d" % b])
                    tt(P, "gpsimd", uu[b][:], uu[b][:], uf[b][:], ALU.subtract, ["uu%d" % b, "uf%d" % b], ["uu%d" % b])
                    outap = h3b[:, sl] if li == 2 else dst[:, sl]
                    act(P, outap, uu[b][:], AF.Sin, ["uu%d" % b], ["h%d_%d" % (li, ch)], scale=2 * math.pi)
            P.emit()

        with contextlib.ExitStack() as st:
            S = lambda n, sh, dt: st.enter_context(nc.sbuf_tensor("hy_" + n, sh, dt))
            PS = lambda n, sh, dt: st.enter_context(nc.psum_tensor("hy_" + n, sh, dt))
            W = S("W", [128, 4, 128], BF16)
            dma(P, "sync", W[:].rearrange("p a b -> p (a b)"), A["c_W"], (), ["W"])
            Wre, Wim, nWim, nWre = W[:, 0, :], W[:, 1, :], W[:, 2, :], W[:, 3, :]
            tcol = S("tcol", [64, 128], F32)
            mf = S("mf", [64, 128], F32)
            mb = S("mb", [64, 128], F32)
            dma(P, "sync", tcol[:], A["c_tcol"], (), ["tcol"])
            dma(P, "sync", mf[:], A["c_mf"], (), ["mf"])
            dma(P, "sync", mb[:], A["c_mb"], (), ["mb"])
            nad = S("nad", [64, CH], F32)
            dma(P, "sync", nad[:], bass.AP(tensor=A["c_nad"].tensor, offset=0, ap=[[0, 64], [1, CH]]), (), ["nad"])
            hbias = S("hbias", [64, 2, CH], F32)
            dma(P, "sync", hbias[:], bass.AP(tensor=A["hy_bias"].tensor, offset=0, ap=[[0, 64], [CH, 2], [1, CH]]), (), ["hbias"])
            Eg = [S("Eg%d" % i, [64, NG, 128], BF16) for i in range(2)]
            Gg = [S("Gg%d" % i, [128, NG, 64], BF16) for i in range(2)]
            xin = [S("xin%d" % i, [64, NG, CB], BF16) for i in range(2)]
            x1s = [S("x1s%d" % i, [128, NG, CB], BF16) for i in range(2)]
            dec = [S("dec%d" % i, [64, CB], F32) for i in range(3)]
            hf = [S("hf%d" % i, [64, CB], BF16) for i in range(4)]
            xb1 = [S("xb1_%d" % i, [128, 2, KGRP, CB], BF16) for i in range(2)]
            xb2 = [S("xb2_%d" % i, [128, 2, KGRP, CB], BF16) for i in range(2)]
            khs = [S("khs%d" % i, [128, KGRP, 2, CB], BF16) for i in range(2)]
            yh = [S("yh%d" % i, [128, 2, CB], BF16) for i in range(3)]
            tmp = [S("tmp%d" % i, [128, CB], F32) for i in range(8)]
            vs = [S("vs%d" % i, [128, 2, KGRP, CB], BF16) for i in range(2)]
            vin = [S("vin%d" % i, [128, NG, CB], BF16) for i in range(2)]
            gv = [S("gv%d" % i, [64, NG, CB], BF16) for i in range(2)]
            gx = [S("gx%d" % i, [64, NG, CB], BF16) for i in range(2)]
            vbz = [S("vbz%d" % i, [64, CB], F32) for i in range(2)]
            zt = [S("zt%d" % i, [64, CB], F32) for i in range(2)]
            zo = [S("zo%d" % i, [64, NG, CB], BF16) for i in range(2)]
            pB = [PS("pB%d" % i, [128, CB], F32) for i in range(4)]
            pV = [PS("pV%d" % i, [128, CB], F32) for i in range(4)]
            cnt = {}

            def nxt(k, m):
                v = cnt.get(k, 0)
                cnt[k] = v + 1
                return v % m

            def ev_eng():
                return "scalar" if nxt("ev", 2) else "vector"

            Enat_v = A["c_Enat"].rearrange("p (a b) -> p a b", a=128, b=128)
            Edat_v = A["c_Edat"].rearrange("p (a b) -> p a b", a=128, b=128)
            G_v = A["c_G"].rearrange("p (a b) -> p a b", a=128, b=64)
            h3v = h3b[:].rearrange("p (a b) -> p b a", b=128)

            def tokview(ap2d):
                return ap2d.rearrange("(a b) c -> a b c", b=128)

            def stageA(slot, Ev, get_rhs):
                items = [(gi, j) for gi in range(128 // NG) for j in range(NG)]
                ebs, sbs = {}, {}
                queue = []
                for it in items + [None] * LOOK:
                    cur = None
                    if it is not None:
                        gi, j = it
                        if j == 0:
                            ebs[gi] = nxt("eg", 2)
                            dma(P, "sync", Eg[ebs[gi]][:], Ev[:, gi * NG:(gi + 1) * NG], (), ["Eg%d" % ebs[gi]])
                        rhs, rkeys = get_rhs(gi, j)
                        cur = (gi, j, rhs, rkeys)
                        queue.append(cur)
                    if len(queue) > LOOK or (it is None and queue):
                        gi, j, rhs, rkeys = queue.pop(0)
                        eb = ebs[gi]
                        if j == 0:
                            sbs[gi] = nxt("x1s", 2)
                        sb = sbs[gi]
                        p = nxt("pB", 4)
                        mm(P, pB[p][:], Eg[eb][:, j, :], rhs, True, True, ["Eg%d" % eb] + rkeys, ["pB%d" % p])
                        cp(P, "scalar" if nxt("evA", 2) else "vector", x1s[sb][:, j, :], pB[p][:], ["pB%d" % p], ["x1s%d_%d" % (sb, j)])
                        if j == NG - 1:
                            dst = S1[slot].rearrange("r k n c -> (r k) n c")[:, gi * NG:(gi + 1) * NG, :]
                            dma(P, "gpq", dst, x1s[sb][:], ["x1s%d_%d" % (sb, jj) for jj in range(NG)], ["S1_%d_%d" % (slot, gi % 2)])

            def data_rhs(src2d, skey):
                tv = tokview(src2d)
                state = {}

                def get(gi, j):
                    if j == 0:
                        b = nxt("xin", 2)
                        state["b"] = b
                        dma(P, "sync", xin[b][:], tv[:, gi * NG:(gi + 1) * NG, :], [skey], ["xin%d" % b])
                    b = state["b"]
                    return xin[b][:, j, :], ["xin%d" % b]
                return get

            def filt_rhs(od, c0):
                o, dr = od // 2, od % 2
                wcol = o * 2 * CH + dr * CH + c0
                msk, mk = (mf, "mf") if dr == 0 else (mb, "mb")

                def get(gi, j):
                    n2 = gi * NG + j
                    d = nxt("dec", 3)
                    act(P, dec[d][:], nad[:, c0:c0 + CB], AF.Exp, ["nad", "tcol"], ["dec%d" % d], scale=tcol[:, n2:n2 + 1])
                    p = nxt("pV", 4)
                    mm(P, pV[p][0:64, :], h3v[:, n2, :], w4b[:, wcol:wcol + CB], True, True, ["h3b", "w4b"], ["pV%d" % p])
                    hb_ = nxt("hf", 4)
                    stt(P, "vector", hf[hb_][:], pV[p][0:64, :], msk[:, n2:n2 + 1], dec[d][:], ALU.mult, ALU.mult,
                        ["pV%d" % p, mk, "dec%d" % d], ["hf%d" % hb_])
                    return hf[hb_][:], ["hf%d" % hb_]
                return get

            def load_xb(buf, bk, slot, k1a, k1b):
                for ri in range(2):
                    src = S1[slot, ri, k1a:k1b, :, :].rearrange("k n c -> n k c")
                    dma(P, "sync", buf[:, ri, 0:k1b - k1a, :], src, ["S1_%d_0" % slot, "S1_%d_1" % slot], [bk])

            def stageB_filter(o, cbi):
                for k1a in range(0, 64, KGRP):
                    k1b = min(64, k1a + KGRP)
                    b = nxt("xb", 2)
                    load_xb(xb1[b], "xb1_%d" % b, 2 * o, k1a, k1b)
                    load_xb(xb2[b], "xb2_%d" % b, 2 * o + 1, k1a, k1b)
                    kb = nxt("khs", 2)
                    rk = ["xb1_%d" % b, "xb2_%d" % b, "W"]
                    for kk in range(k1b - k1a):
                        fr, fi = xb1[b][:, 0, kk, :], xb1[b][:, 1, kk, :]
                        br, bi = xb2[b][:, 0, kk, :], xb2[b][:, 1, kk, :]
                        pr = nxt("pB", 4)
                        for n_, (w_, x_) in enumerate(((Wre, fr), (nWim, fi), (Wre, br), (nWim, bi))):
                            mm(P, pB[pr][:], w_, x_, n_ == 0, n_ == 3, rk, ["pB%d" % pr])
                        cp(P, ev_eng(), khs[kb][:, kk, 0, :], pB[pr][:], ["pB%d" % pr], ["khs%d" % kb])
                        pi = nxt("pB", 4)
                        for n_, (w_, x_) in enumerate(((Wim, fr), (Wre, fi), (nWim, br), (nWre, bi))):
                            mm(P, pB[pi][:], w_, x_, n_ == 0, n_ == 3, rk, ["pB%d" % pi])
                        cp(P, ev_eng(), khs[kb][:, kk, 1, :], pB[pi][:], ["pB%d" % pi], ["khs%d" % kb])
                    dma(P, "gpq", KH[o, cbi, :, k1a:k1b, :, :], khs[kb][:, 0:k1b - k1a, :, :], ["khs%d" % kb], ["KH%d" % o])

            def stageB_conv(slot, o, cbi):
                items = [(k1a, kk) for k1a in range(0, 64, KGRP) for kk in range(min(64, k1a + KGRP) - k1a)]
                grp = {}
                queue = []
                for it in items + [None] * LOOK:
                    cur = None
                    if it is not None:
                        k1a, kk = it
                        k1b = min(64, k1a + KGRP)
                        if kk == 0:
                            b = nxt("xb", 2)
                            load_xb(xb1[b], "xb1_%d" % b, slot, k1a, k1b)
                            kb = nxt("khs", 2)
                            dma(P, "sync", khs[kb][:, 0:k1b - k1a, :, :], KH[o, cbi, :, k1a:k1b, :, :], ["KH%d" % o], ["khs%d" % kb])
                            grp[k1a] = [b, kb, None]
                        b, kb, _ = grp[k1a]
                        rk = ["xb1_%d" % b, "W"]
                        xr, xi = xb1[b][:, 0, kk, :], xb1[b][:, 1, kk, :]
                        kr, ki = khs[kb][:, kk, 0, :], khs[kb][:, kk, 1, :]
                        pr = nxt("pB", 4)
                        mm(P, pB[pr][:], Wre, xr, True, False, rk, ["pB%d" % pr])
                        mm(P, pB[pr][:], nWim, xi, False, True, rk, ["pB%d" % pr])
                        pi = nxt("pB", 4)
                        mm(P, pB[pi][:], Wim, xr, True, False, rk, ["pB%d" % pi])
                        mm(P, pB[pi][:], Wre, xi, False, True, rk, ["pB%d" % pi])
                        t = [nxt("tmp", 8) for _ in range(4)]
                        kk_ = ["khs%d" % kb]
                        tt(P, "vector", tmp[t[0]][:], pB[pr][:], kr, ALU.mult, ["pB%d" % pr] + kk_, ["tmp%d" % t[0]])
                        tt(P, "vector", tmp[t[1]][:], pB[pi][:], ki, ALU.mult, ["pB%d" % pi] + kk_, ["tmp%d" % t[1]])
                        tt(P, "vector", tmp[t[2]][:], pB[pr][:], ki, ALU.mult, ["pB%d" % pr] + kk_, ["tmp%d" % t[2]])
                        tt(P, "vector", tmp[t[3]][:], pB[pi][:], kr, ALU.mult, ["pB%d" % pi] + kk_, ["tmp%d" % t[3]])
                        yb = nxt("yh", 3)
                        tt(P, "gpsimd", yh[yb][:, 0, :], tmp[t[0]][:], tmp[t[1]][:], ALU.subtract, ["tmp%d" % t[0], "tmp%d" % t[1]], ["yh%d" % yb])
                        tt(P, "gpsimd", yh[yb][:, 1, :], tmp[t[2]][:], tmp[t[3]][:], ALU.add, ["tmp%d" % t[2], "tmp%d" % t[3]], ["yh%d" % yb])
                        cur = (k1a, kk, yb)
                        queue.append(cur)
                    if len(queue) > LOOK or (it is None and queue):
                        k1a, kk, yb = queue.pop(0)
                        k1b = min(64, k1a + KGRP)
                        if kk == 0:
                            grp[k1a][2] = nxt("vs", 2)
                        vb = grp[k1a][2]
                        yr, yi = yh[yb][:, 0, :], yh[yb][:, 1, :]
                        yk = ["yh%d" % yb, "W"]
                        vr = nxt("pV", 4)
                        mm(P, pV[vr][:], Wre, yr, True, False, yk, ["pV%d" % vr])
                        mm(P, pV[vr][:], Wim, yi, False, True, yk, ["pV%d" % vr])
                        cp(P, "scalar", vs[vb][:, 0, kk, :], pV[vr][:], ["pV%d" % vr], ["vs%d" % vb])
                        vi = nxt("pV", 4)
                        mm(P, pV[vi][:], nWim, yr, True, False, yk, ["pV%d" % vi])
                        mm(P, pV[vi][:], Wre, yi, False, True, yk, ["pV%d" % vi])
                        cp(P, "scalar", vs[vb][:, 1, kk, :], pV[vi][:], ["pV%d" % vi], ["vs%d" % vb])
                        if kk == k1b - k1a - 1:
                            for ri in range(2):
                                dma(P, "gpq", S2[ri, :, k1a:k1b, :], vs[vb][:, ri, 0:k1b - k1a, :], ["vs%d" % vb], ["S2_%d" % ri])

            def stageAp(o, cbi):
                c0 = cbi * CB
                gate2d = u_hy[1:T + 1, (1 + o) * CH + c0:(1 + o) * CH + c0 + CB]
                zsrc2d, zkey = (u_hy[1:T + 1, c0:c0 + CB], "u_hy") if o == 0 else (z1d[:, c0:c0 + CB], "z1d")
                dst2d, dkey = (z1d[:, c0:c0 + CB], "z1d") if o == 0 else (y_hy[:, c0:c0 + CB], "y_hy")
                gview, zview, dview = tokview(gate2d), tokview(zsrc2d), tokview(dst2d)
                def ap_loads(gi):
                    gb = nxt("gg", 2)
                    dma(P, "sync", Gg[gb][:], G_v[:, gi * NG:(gi + 1) * NG], (), ["Gg%d" % gb])
                    vb = nxt("vin", 2)
                    for ri in range(2):
                        src = S2[ri, gi * NG:(gi + 1) * NG, :, :].rearrange("n k c -> k n c")
                        dma(P, "sync", vin[vb][ri * 64:(ri + 1) * 64, :, :], src, ["S2_%d" % ri], ["vin%d" % vb])
                    xb_ = nxt("gx", 2)
                    dma(P, "sync", gx[xb_][:], gview[:, gi * NG:(gi + 1) * NG, :], ["u_hy"], ["gx%d" % xb_])
                    dma(P, "sync", gv[xb_][:], zview[:, gi * NG:(gi + 1) * NG, :], [zkey], ["gv%d" % xb_])
                    return gb, vb, xb_

                nxt_ld = ap_loads(0)
                for gi in range(128 // NG):
                    gb, vb, xb_ = nxt_ld
                    if gi + 1 < 128 // NG:
                        nxt_ld = ap_loads(gi + 1)
                    ob_ = nxt("zo", 2)
                    for j in range(NG):
                        p = nxt("pV", 4)
                        mm(P, pV[p][0:64, :], Gg[gb][:, j, :], vin[vb][:, j, :], True, True, ["Gg%d" % gb, "vin%d" % vb], ["pV%d" % p])
                        q = nxt("vbz", 2)
                        tt(P, "gpsimd", vbz[q][:], gv[xb_][:, j, :], hbias[:, o, c0:c0 + CB], ALU.mult, ["gv%d" % xb_, "hbias"], ["vbz%d" % q])
                        tt(P, "vector", zt[q][:], pV[p][0:64, :], vbz[q][:], ALU.add, ["pV%d" % p, "vbz%d" % q], ["zt%d" % q])
                        tt(P, "vector", zo[ob_][:, j, :], zt[q][:], gx[xb_][:, j, :], ALU.mult, ["zt%d" % q, "gx%d" % xb_], ["zo%d" % ob_])
                    dma(P, "gpq", dview[:, gi * NG:(gi + 1) * NG, :], zo[ob_][:], ["zo%d" % ob_], [dkey])

            for cbi in range(NCB):
                c0 = cbi * CB
                for od in range(4):
                    stageA(od, Enat_v, filt_rhs(od, c0))
                for o in range(2):
                    stageB_filter(o, cbi)
                stageA(0, Edat_v, data_rhs(u_hy[1:T + 1, c0:c0 + CB], "u_hy"))
                stageB_conv(0, 0, cbi)
                stageAp(0, cbi)
                stageA(1, Edat_v, data_rhs(z1d[:, c0:c0 + CB], "z1d"))
                stageB_conv(1, 1, cbi)
                stageAp(1, cbi)
            P.emit()


def _phase_nat(nc, P, c, A):
    NH, CN, T = c.NH, c.CN, c.T
    NHP = NH // 2
    qT, kT, vn1, y_nat = A["qT"], A["kT"], A["vn1"], A["y_nat"]
    NTY = len(NAT_TYPES)
    with contextlib.ExitStack() as st:
        S = lambda n, sh, dt: st.enter_context(nc.sbuf_tensor("nat_" + n, sh, dt))
        PS = lambda n, sh, dt: st.enter_context(nc.psum_tensor("nat_" + n, sh, dt))
        ident = S("ident", [128, 128], BF16)
        dma(P, "sync", ident[:], A["c_ident"], (), ["ident"])
        msk = S("msk", [128, NTY, 896], BF16)
        dma(P, "sync", msk[:], A["c_mask"].rearrange("t p c -> p t c"), (), ["msk"])
        TT = S("TT", [128, NH, 960], BF16)
        TMint = S("TMint", [128, NH, 576], BF16)
        tst = [S("tst%d" % i, [128, 960], F32) for i in range(2)]
        for h in range(NH):
            b = h % 2
            dma(P, "sync", tst[b][:], A["nat_tt"][h], (), ["tst%d" % b])
            cp(P, "vector", TT[:, h, :], tst[b][:], ["tst%d" % b], ["TT%d" % h])
            tt(P, "gpsimd", TMint[:, h, :], TT[:, h, 192:768], msk[:, 0, 0:576], ALU.add, ["TT%d" % h, "msk"], ["TMint%d" % h])
        TMT = S("TMT", [128, NH, 640], BF16)
        ssb = [S("ssb%d" % i, [128, 640], F32) for i in range(3)]
        qt = [S("qt%d" % i, [128, NHP, 512], BF16) for i in range(2)]
        kt = [S("kt%d" % i, [128, NHP, 896], BF16) for i in range(2)]
        v1 = [S("v1_%d" % i, [128, 7, NH, 65], BF16) for i in range(2)]
        tmsp = [S("tmsp%d" % i, [128, 896], BF16) for i in range(2)]
        pt = [S("pt%d" % i, [128, 1024], BF16) for i in range(3)]
        ys = [S("ys%d" % i, [128, CN], BF16) for i in range(2)]
        rec = [S("rec%d" % i, [128, 4], F32) for i in range(2)]
        stS = contextlib.ExitStack()
        pSb = [stS.enter_context(nc.psum_tensor("nat_pSb0", [128, 1024], BF16))] * 2
        cnt = {}

        def nxt(k, m):
            v = cnt.get(k, 0)
            cnt[k] = v + 1
            return v % m

        qTv = qT.rearrange("(c p) t -> p c t", p=128)
        kTv = kT.rearrange("(c p) t -> p c t", p=128)
        mset(P, "vector", TMT[:].rearrange("p a b -> p (a b)"), 0.0, ["TMTall"])
        for h in range(NH):
            for kb in range(5):
                kp = 128 if kb < 4 else 64
                sb = 0
                tr(P, pSb[sb][0:kp, kb * 128:(kb + 1) * 128], TMint[:, h, kb * 128:kb * 128 + kp], ident[:], ["TMint%d" % h, "ident"], ["pSb%d" % sb])
            cp(P, "vector" if h % 2 else "scalar", TMT[:, h, 0:512], pSb[0][:, 0:512], ["pSb0"], ["TMT%d" % h, "TMTall"])
            cp(P, "vector" if h % 2 else "scalar", TMT[0:64, h, 512:640], pSb[0][0:64, 512:640], ["pSb0"], ["TMT%d" % h, "TMTall"])
        P.emit()
        stS.close()
        pS = [PS("pS%d" % i, [128, 1024], F32) for i in range(3)]
        pO = [PS("pO%d" % i, [128, 4, 65], F32) for i in range(2)]
        for m in range(T // 128):
            tname, ks, nr = nat_block(m)
            ti = NAT_TYPES.index(tname)
            Bs = ks - 2 * m + 7
            nfull = nr // 2
            nkb = (nr + 1) // 2
            if m % 4 == 0:
                qb = nxt("qt", 2)
                dma(P, "sync", qt[qb][:], qTv[:, :, m * 128:m * 128 + 512], (), ["qt%d" % qb])
            qcol = (m % 4) * 128
            kb_ = nxt("kt", 2)
            dma(P, "sync", kt[kb_][:, :, 0:nr * 64], kTv[:, :, ks * 64:(ks + nr) * 64], (), ["kt%d" % kb_])
            vb = nxt("v1", 2)
            t0 = ks * 64
            dma(P, "sync", v1[vb][:, 0:nfull, :, :], vn1[t0:t0 + nfull * 128].rearrange("(b p) h e -> p b h e", p=128), (), ["v1_%d" % vb])
            if nr % 2:
                dma(P, "sync", v1[vb][0:64, nfull, :, :], vn1[t0 + nfull * 128:t0 + nfull * 128 + 64], (), ["v1_%d" % vb])
            yb = nxt("ys", 2)
            pend = []
            obs = {}
            for h in range(NH):
                hp, hc = h % 2, h // 2
                if tname == "int":
                    TMh, tmk = TMint[:, h, :], "TMint%d" % h
                else:
                    tb = nxt("tmsp", 2)
                    tt(P, "vector", tmsp[tb][:, 0:nr * 64], TT[:, h, Bs * 64:(Bs + nr) * 64], msk[:, ti, 0:nr * 64], ALU.add,
                       ["TT%d" % h, "msk"], ["tmsp%d" % tb])
                    TMh, tmk = tmsp[tb], "tmsp%d" % tb
                sb = nxt("pS", 3)
                if tname == "int":
                    for kb in range(nkb):
                        kp = 128 if kb < nfull else 64
                        out = pS[sb][0:kp, kb * 128:(kb + 1) * 128]
                        mm(P, out, kt[kb_][hp * 64:(hp + 1) * 64, hc, kb * 128:kb * 128 + kp], qt[qb][hp * 64:(hp + 1) * 64, hc, qcol:qcol + 128],
                           True, True, ["kt%d" % kb_, "qt%d" % qb], ["pS%d" % sb])
                    tt(P, "vector", ssb[sb][:, 0:512], pS[sb][:, 0:512], TMT[:, h, 0:512], ALU.add, ["pS%d" % sb, "TMT%d" % h], ["ssb%d" % sb])
                    tt(P, "vector", ssb[sb][0:64, 512:640], pS[sb][0:64, 512:640], TMT[0:64, h, 512:640], ALU.add, ["pS%d" % sb, "TMT%d" % h], ["ssb%d" % sb])
                    act(P, pt[sb][:, 0:512], ssb[sb][:, 0:512], AF.Exp, ["ssb%d" % sb], ["pt%d" % sb])
                    act(P, pt[sb][0:64, 512:640], ssb[sb][0:64, 512:640], AF.Exp, ["ssb%d" % sb], ["pt%d" % sb])
                else:
                    for kb in range(nkb):
                        kp = 128 if kb < nfull else 64
                        out = pS[sb][0:kp, kb * 128:(kb + 1) * 128]
                        mm(P, out, kt[kb_][hp * 64:(hp + 1) * 64, hc, kb * 128:kb * 128 + kp], qt[qb][hp * 64:(hp + 1) * 64, hc, qcol:qcol + 128],
                           True, False, ["kt%d" % kb_, "qt%d" % qb], ["pS%d" % sb])
                        mm(P, out, TMh[:, kb * 128:kb * 128 + kp], ident[:], False, True, [tmk, "ident"], ["pS%d" % sb])
                    act(P, pt[sb][:, 0:nfull * 128], pS[sb][:, 0:nfull * 128], AF.Exp, ["pS%d" % sb], ["pt%d" % sb])
                    if nr % 2:
                        act(P, pt[sb][0:64, nfull * 128:nkb * 128], pS[sb][0:64, nfull * 128:nkb * 128], AF.Exp, ["pS%d" % sb], ["pt%d" % sb])
                while len(pend) > 1:
                    pend.pop(0)()
                if h % 4 == 0:
                    obs[h // 4] = nxt("pO", 2)

                def pv(h=h, sb=sb):
                    ob = obs[h // 4]
                    for kb in range(nkb):
                        kp = 128 if kb < nfull else 64
                        mm(P, pO[ob][:, h % 4, :], pt[sb][0:kp, kb * 128:(kb + 1) * 128], v1[vb][0:kp, kb, h, :], kb == 0, kb == nkb - 1,
                           ["pt%d" % sb, "v1_%d" % vb], ["pO%d" % ob])
                    if h % 4 == 3:
                        rb = nxt("rec", 2)
                        P.op("vector", (lambda o_, i_: (lambda e: e.reciprocal(out=o_, in_=i_)))(rec[rb][:], pO[ob][:, :, 64]),
                             ["pO%d" % ob], ["rec%d" % rb])
                        for hh in range(4):
                            hd = h - 3 + hh
                            if hh % 2:
                                act(P, ys[yb][:, hd * 64:(hd + 1) * 64], pO[ob][:, hh, 0:64], AF.Copy, ["pO%d" % ob, "rec%d" % rb], ["ys%d" % yb],
                                    scale=rec[rb][:, hh:hh + 1])
                            else:
                                ts(P, "vector", ys[yb][:, hd * 64:(hd + 1) * 64], pO[ob][:, hh, 0:64], rec[rb][:, hh:hh + 1], None, ALU.mult, None,
                                   ["pO%d" % ob, "rec%d" % rb], ["ys%d" % yb])
                pend.append(pv)
            while pend:
                pend.pop(0)()
            dma(P, "gpq", y_nat[m * 128:(m + 1) * 128, :], ys[yb][:], ["ys%d" % yb], ["y_nat%d" % yb])
        P.emit()


def _phase_p3(nc, P, c, A):
    D, CH, CN, DFF, T, KD, KM, KF, KG, DC = c.D, c.CH, c.CN, c.DFF, c.T, c.KD, c.KM, c.KF, c.KG, c.DC
    MIXC = c.MIXC
    TT3 = 512
    NB = TT3 // 128
    WK = max(KM, KD, KG)
    x, y, y_hy, y_nat = A["x"], A["y"], A["y_hy"], A["y_nat"]
    Wb_out, Wb_up, Wb_down = A["Wb_out"], A["Wb_up"], A["Wb_down"]
    NSL = KF // KG
    FFS = KG * 128
    with contextlib.ExitStack() as st:
        S = lambda n, sh, dt: st.enter_context(nc.sbuf_tensor(n, sh, dt))
        PS = lambda n, sh, dt: st.enter_context(nc.psum_tensor(n, sh, dt))
        ident = S("ident", [128, 128], BF16)
        dma(P, "sync", ident[:], A["c_ident"], (), ["ident"])
        gfin = S("gfin", [128, D], F32)
        dma(P, "sync", gfin[:], bass.AP(tensor=A["norm_f_g"].tensor, offset=0, ap=[[0, 128], [1, D]]), (), ["gfin"])
        yin = [S("yin%d" % i, [128, MIXC], BF16) for i in range(2)]
        mixb = [S("mixb%d" % i, [128, MIXC], BF16) for i in range(2)]
        actT = S("actT", [128, KD, TT3], BF16)
        mixT = S("mixT", [128, KM, TT3], BF16)
        xres = [S("xres%d" % i, [128, D], F32) for i in range(NB)]
        mbb = [S("mbb%d" % i, [128, D], BF16) for i in range(2)]
        uT = [S("uT%d" % i, [128, KG, TT3], BF16) for i in range(2)]
        rl = [S("rl%d" % i, [128, TT3], F32) for i in range(2)]
        outb = [S("outb%d" % i, [128, D], F32) for i in range(2)]
        wbuf = [S("wbuf%d" % i, [128, WK, 512], BF16) for i in range(3)]
        stt_ = S("stats", [128, 24], F32)
        psA = [PS("psA%d" % i, [128, 4, 128], BF16) for i in range(2)]
        psM = [PS("psM%d" % i, [128, 512], F32) for i in range(6)]
        cnt = {}

        def nxt(k, m):
            v = cnt.get(k, 0)
            cnt[k] = v + 1
            return v % m

        def ev_eng():
            return "scalar" if nxt("ev", 2) else "vector"

        def rstd_of(src_ap, n, skeys, col, junk_ap, junk_key):
            k0, k1 = "st%d" % col, "st%d" % (col + 1)
            act(P, junk_ap, src_ap, AF.Square, skeys, [junk_key, k0], accum_out=stt_[:, col:col + 1])
            act(P, stt_[:, col + 1:col + 2], stt_[:, col:col + 1], AF.Sqrt, [k0], [k1], bias=EPS, scale=1.0 / n)
            P.op("vector", (lambda o_, i_: (lambda e: e.reciprocal(out=o_, in_=i_)))(stt_[:, col + 1:col + 2], stt_[:, col + 1:col + 2]),
                 [k1], [k1])
            return stt_[:, col + 1:col + 2], k1

        def transposes(src, skey, nk, b, dstT=None, dkey="actT"):
            if dstT is None:
                dstT = actT
            for k0 in range(0, nk, 4):
                pa = nxt("psA", 2)
                n_ = min(4, nk - k0)
                for kk in range(n_):
                    tr(P, psA[pa][:, kk, :], src[:, (k0 + kk) * 128:(k0 + kk + 1) * 128], ident[:], [skey, "ident"], ["psA%d" % pa])
                cp(P, ev_eng(), dstT[:, k0:k0 + n_, b * 128:(b + 1) * 128], psA[pa][:, 0:n_, :], ["psA%d" % pa], [dkey])

        def load_w(src3d, nk, ncol, skey):
            wb = nxt("wbuf", 3)
            dma(P, "sync", wbuf[wb][:, 0:nk, 0:ncol], src3d, [skey], ["wbuf%d" % wb])
            return wb

        def stepA1(ti, b):
            r0 = ti * TT3 + b * 128
            yb = b % 2
            dma(P, "sync", yin[yb][:, 0:CH], y_hy[r0:r0 + 128, :], (), ["yin%d" % yb])
            dma(P, "sync", yin[yb][:, CH:MIXC], y_nat[r0:r0 + 128, :], (), ["yin%d" % yb])
            r_h, kh_ = rstd_of(yin[yb][:, 0:CH], CH, ["yin%d" % yb], 12 + 4 * yb, mixb[yb][:, 0:CH], "mixb%d" % yb)
            r_n, kn_ = rstd_of(yin[yb][:, CH:MIXC], CN, ["yin%d" % yb], 14 + 4 * yb, mixb[yb][:, CH:MIXC], "mixb%d" % yb)
            ts(P, "vector", mixb[yb][:, 0:CH], yin[yb][:, 0:CH], r_h, None, ALU.mult, None, ["yin%d" % yb, kh_], ["mixb%d" % yb])
            act(P, mixb[yb][:, CH:MIXC], yin[yb][:, CH:MIXC], AF.Copy, ["yin%d" % yb, kn_], ["mixb%d" % yb], scale=r_n)

        def stepA2(ti, b):
            yb = b % 2
            transposes(mixb[yb], "mixb%d" % yb, KM, b, mixT, "mixT")

        for b in range(NB):
            stepA1(0, b)
            stepA2(0, b)
        NTL3 = T // TT3
        pre_wb = [None]
        for ti in range(NTL3):
            t0 = ti * TT3
            for b in range(NB):
                dma(P, "sync", xres[b][:], x[t0 + b * 128:t0 + (b + 1) * 128, :], (), ["xres%d" % b])
            for cc in range(D // DC):
                if cc == 0 and pre_wb[0] is not None:
                    wb = pre_wb[0]
                    pre_wb[0] = None
                else:
                    wb = load_w(Wb_out[:, cc * DC:(cc + 1) * DC].rearrange("(k p) c -> p k c", p=128), KM, DC, "Wb_out")
                for b in range(NB):
                    pm = nxt("psM", 6)
                    for k in range(KM):
                        mm(P, psM[pm][:, 0:DC], mixT[:, k, b * 128:(b + 1) * 128], wbuf[wb][:, k, 0:DC], k == 0, k == KM - 1,
                           ["mixT", "wbuf%d" % wb], ["psM%d" % pm])
                    sl = slice(cc * DC, (cc + 1) * DC)
                    tt(P, "vector", xres[b][:, sl], psM[pm][:, 0:DC], xres[b][:, sl], ALU.add, ["psM%d" % pm, "xres%d" % b], ["xres%d" % b])
            for b in range(NB):
                mbi = nxt("mbb", 2)
                r_m, km_ = rstd_of(xres[b][:], D, ["xres%d" % b], 4, mbb[mbi][:], "mbb%d" % mbi)
                if b % 2:
                    act(P, mbb[mbi][:], xres[b][:], AF.Copy, ["xres%d" % b, km_], ["mbb%d" % mbi], scale=r_m)
                else:
                    ts(P, "vector", mbb[mbi][:], xres[b][:], r_m, None, ALU.mult, None, ["xres%d" % b, km_], ["mbb%d" % mbi])
                transposes(mbb[mbi], "mbb%d" % mbi, KD, b)
            for s_ in range(NSL):
                ub = nxt("uT", 2)
                for sub in range(FFS // 512):
                    f0 = s_ * FFS + sub * 512
                    wb = load_w(Wb_up[:, f0:f0 + 512].rearrange("(k p) c -> p k c", p=128), KD, 512, "Wb_up")
                    for j in range(4):
                        pm = nxt("psM", 6)
                        for k in range(KD):
                            mm(P, psM[pm][:], wbuf[wb][:, k, j * 128:(j + 1) * 128], actT[:, k, :], k == 0, k == KD - 1,
                               ["actT", "wbuf%d" % wb], ["psM%d" % pm])
                        rb = nxt("rl", 2)
                        act(P, rl[rb][:], psM[pm][:], AF.Relu, ["psM%d" % pm], ["rl%d" % rb])
                        tt(P, "gpsimd", uT[ub][:, sub * 4 + j, :], rl[rb][:], rl[rb][:], ALU.mult, ["rl%d" % rb], ["uT%d" % ub])
                for cc in range(D // DC):
                    src = Wb_down[s_ * FFS:(s_ + 1) * FFS, cc * DC:(cc + 1) * DC].rearrange("(k p) c -> p k c", p=128)
                    wb = load_w(src, KG, DC, "Wb_down")
                    for b in range(NB):
                        pm = nxt("psM", 6)
                        for k in range(KG):
                            mm(P, psM[pm][:, 0:DC], uT[ub][:, k, b * 128:(b + 1) * 128], wbuf[wb][:, k, 0:DC], k == 0, k == KG - 1,
                               ["uT%d" % ub, "wbuf%d" % wb], ["psM%d" % pm])
                        sl = slice(cc * DC, (cc + 1) * DC)
                        tt(P, "vector", xres[b][:, sl], psM[pm][:, 0:DC], xres[b][:, sl], ALU.add, ["psM%d" % pm, "xres%d" % b], ["xres%d" % b])
                if ti + 1 < NTL3:
                    stp = [st_ for st_ in range(NB + 1) if (st_ * NSL) // (NB + 1) == s_] if NSL >= 2 else (list(range(NB + 1)) if s_ == 0 else [])
                    for st_ in stp:
                        if st_ >= 1:
                            stepA2(ti + 1, st_ - 1)
                        if st_ < NB:
                            stepA1(ti + 1, st_)
            if ti + 1 < NTL3:
                pre_wb[0] = load_w(Wb_out[:, 0:DC].rearrange("(k p) c -> p k c", p=128), KM, DC, "Wb_out")
            for b in range(NB):
                mbi = nxt("mbb", 2)
                r_f, kf_ = rstd_of(xres[b][:], D, ["xres%d" % b], 6 + 2 * (b % 2), mbb[mbi][:], "mbb%d" % mbi)
                ob_ = nxt("outb", 2)
                stt(P, "vector", outb[ob_][:], xres[b][:], r_f, gfin[:], ALU.mult, ALU.mult, ["xres%d" % b, kf_, "gfin"], ["outb%d" % ob_])
                dma(P, "gpq", y[t0 + b * 128:t0 + (b + 1) * 128, :], outb[ob_][:], ["outb%d" % ob_], ["y%d" % ob_])
        P.emit()


_CACHE = {}


def _consts(cfg, ctype):
    f32 = np.float32
    C = {}
    C["c_ident"] = np.eye(128, dtype=f32).astype(NPBF)
    ft = fft_tables(ctype)
    C["c_Enat"], C["c_Edat"], C["c_G"], C["c_W"] = ft["Enat"], ft["Edat"], ft["G"], ft["W"]
    fc = filter_consts(ctype)
    C["c_zT"], C["c_tcol"], C["c_mf"], C["c_mb"] = fc["zT"], fc["tcol"], fc["mf"], fc["mb"]
    maxd = math.log(1e-2) / 0.3
    mind = math.log(1e-2) / 1.5
    C["c_nad"] = (-np.abs(np.linspace(mind, maxd, cfg.CH, dtype=f32))).astype(f32)
    C["c_mask"] = nat_masks(ctype)
    C["c_flag"] = np.array([float(ctype)], f32)
    return C


def kernel(x_prompt, x_sample, norm_mix_g, w_in, hy_conv_w, hy_conv_b, hy_pe_w1, hy_pe_b1, hy_pe_w2, hy_pe_b2,
           hy_pe_w3, hy_pe_b3, hy_pe_freq, hy_pe_w4, hy_bias, nat_rpb, gnorm_hy, gnorm_nat, w_out, norm_mlp_g,
           w_up, w_down, norm_f_g):
    cfg = FULL
    f32 = np.float32
    A = lambda v: np.ascontiguousarray(np.asarray(v), dtype=f32)
    x_prompt, x_sample = A(x_prompt), A(x_sample)
    if "nc" not in _CACHE:
        _CACHE["nc"] = build_program(cfg)
        _CACHE["c"] = [_consts(cfg, 0), _consts(cfg, 1)]
    nc = _CACHE["nc"]
    idx, ok = nat_tt_index()
    rpb = A(nat_rpb)[0].reshape(cfg.NH, -1)
    ntt = np.where(ok[None], rpb[:, idx], f32(0.0)).astype(f32)
    shared = {
        "norm_mix_g": A(norm_mix_g)[0], "w_in": A(w_in)[0], "hy_conv_w": A(hy_conv_w)[0], "hy_conv_b": A(hy_conv_b)[0],
        "pe_w1": A(hy_pe_w1)[0], "pe_b1": A(hy_pe_b1)[0], "pe_w2": A(hy_pe_w2)[0], "pe_b2": A(hy_pe_b2)[0],
        "pe_w3": A(hy_pe_w3)[0], "pe_b3": A(hy_pe_b3)[0], "pe_freq": A(hy_pe_freq)[0], "pe_w4": A(hy_pe_w4)[0],
        "hy_bias": A(hy_bias)[0], "nat_tt": ntt,
        "gnorm": np.concatenate([A(gnorm_hy)[0], A(gnorm_nat)[0]]), "w_out": A(w_out)[0],
        "norm_mlp_g": A(norm_mlp_g)[0], "w_up": A(w_up)[0], "w_down": A(w_down)[0], "norm_f_g": A(norm_f_g),
    }
    in_maps = []
    for core in range(8):
        d = dict(shared)
        if core < 4:
            d["x"] = x_sample[core]
            d.update(_CACHE["c"][0])
        else:
            j = core - 4
            d["x"] = x_prompt[2 * j:2 * j + 2].reshape(cfg.T, cfg.D)
            d.update(_CACHE["c"][1])
        in_maps.append(d)
    res = run_bass_kernel_spmd(nc, in_maps, core_ids=list(range(8)))
    y_sample = np.stack([res.results[cidx]["y"] for cidx in range(4)], 0).astype(f32)
    y_prompt = np.concatenate([res.results[4 + j]["y"].reshape(2, 4096, cfg.D) for j in range(4)], 0).astype(f32)
    return (y_prompt, y_sample)
```

```python
import contextlib
import math
import numpy as np
import ml_dtypes
import concourse.bass as bass
import concourse.mybir as mybir
from concourse.bass_utils import run_bass_kernel_spmd

F32 = mybir.dt.float32
BF16 = mybir.dt.bfloat16
I32 = mybir.dt.int32
ALU = mybir.AluOpType
AF = mybir.ActivationFunctionType
NPBF = ml_dtypes.bfloat16

COMPUTE = ("tensor", "vector", "scalar", "gpsimd")
DMAQ = ("sync", "gpq")
NDMASEM = 8
NEG = -30000.0
EPS = 1e-5


class _Op:
    __slots__ = ("eng", "fn", "deps", "idx", "waited", "semslot", "semval")

    def __init__(self, eng, fn):
        self.eng = eng
        self.fn = fn
        self.deps = set()
        self.waited = False


class Prog:
    def __init__(self, nc, stack):
        self.nc = nc
        self.sems = {}
        for e in COMPUTE:
            self.sems[e] = stack.enter_context(nc.semaphore("s_" + e))
        for q in DMAQ:
            self.sems[q] = [stack.enter_context(nc.semaphore("s_%s%d" % (q, i))) for i in range(NDMASEM)]
        self.count = {e: 0 for e in COMPUTE}
        self.dcount = {q: [0] * NDMASEM for q in DMAQ}
        self.dnext = {q: 0 for q in DMAQ}
        self.reset_phase()

    def reset_phase(self):
        self.ops = []
        self.lastw = {}
        self.rd_c = {}
        self.rd_d = {}

    def op(self, eng, fn, reads=(), writes=()):
        o = _Op(eng, fn)
        o.idx = len(self.ops)
        deps = o.deps
        for k in reads:
            w = self.lastw.get(k)
            if w is not None:
                deps.add(w)
        for k in writes:
            w = self.lastw.get(k)
            if w is not None:
                deps.add(w)
            rc = self.rd_c.get(k)
            if rc:
                deps.update(rc.values())
            rd = self.rd_d.get(k)
            if rd:
                deps.update(rd)
        if eng in COMPUTE:
            for k in reads:
                self.rd_c.setdefault(k, {})[eng] = o.idx
        else:
            for k in reads:
                self.rd_d.setdefault(k, []).append(o.idx)
        for k in writes:
            self.lastw[k] = o.idx
            self.rd_c[k] = {}
            self.rd_d[k] = []
        deps.discard(o.idx)
        self.ops.append(o)
        return o

    def emit(self):
        nc = self.nc
        ops = self.ops
        phys = {"tensor": "tensor", "vector": "vector", "scalar": "scalar", "gpsimd": "gpsimd",
                "sync": "sync", "gpq": "gpsimd"}
        for o in ops:
            best = {}
            dl = []
            for d in o.deps:
                od = ops[d]
                if od.eng in COMPUTE:
                    if od.eng == "tensor" and o.eng == "tensor":
                        continue
                    if od.eng not in best or best[od.eng] < d:
                        best[od.eng] = d
                else:
                    dl.append(d)
            o.deps = set(best.values()) | set(dl)
            for d in o.deps:
                ops[d].waited = True
        lastc = {}
        for o in ops:
            if o.eng in COMPUTE:
                lastc[o.eng] = o
        for o in lastc.values():
            o.waited = True
        for o in ops:
            if o.eng in COMPUTE:
                if o.waited:
                    self.count[o.eng] += 1
                    o.semval = self.count[o.eng]
            else:
                slot = self.dnext[o.eng] % NDMASEM
                self.dnext[o.eng] += 1
                self.dcount[o.eng][slot] += 16
                o.semslot = slot
                o.semval = self.dcount[o.eng][slot]
        streams = {"tensor": [], "vector": [], "scalar": [], "gpsimd": [], "sync": []}
        for o in ops:
            streams[phys[o.eng]].append(o)
        sems = self.sems

        def run(e, lst):
            seen = {}
            for o in lst:
                waits = []
                if o.eng in DMAQ and o.semval > 16:
                    waits.append((sems[o.eng][o.semslot], o.semval - 16))
                for d in sorted(o.deps):
                    od = ops[d]
                    if od.eng in COMPUTE:
                        waits.append((sems[od.eng], od.semval))
                    else:
                        waits.append((sems[od.eng][od.semslot], od.semval))
                for (s, v) in waits:
                    key = id(s)
                    if seen.get(key, 0) >= v:
                        continue
                    seen[key] = v
                    e.wait_ge(s, v)
                ins = o.fn(e)
                if o.eng in COMPUTE:
                    if o.waited:
                        ins.then_inc(sems[o.eng], 1)
                else:
                    ins.then_inc(sems[o.eng][o.semslot], 16)
            for c in COMPUTE:
                if self.count[c] > seen.get(id(sems[c]), 0):
                    e.wait_ge(sems[c], self.count[c])
            for q in DMAQ:
                for i in range(NDMASEM):
                    if self.dcount[q][i] > seen.get(id(sems[q][i]), 0):
                        e.wait_ge(sems[q][i], self.dcount[q][i])

        with nc.Block() as block:
            @block.tensor
            def _(e):
                run(e, streams["tensor"])

            @block.vector
            def _(e):
                run(e, streams["vector"])

            @block.scalar
            def _(e):
                run(e, streams["scalar"])

            @block.gpsimd
            def _(e):
                run(e, streams["gpsimd"])

            @block.sync
            def _(e):
                run(e, streams["sync"])
        self.reset_phase()


def dma(P, q, out, in_, reads, writes, slow=False):
    if slow:
        return P.op(q, lambda e: e.dma_start(out=out, in_=in_, allow_slow_non_contiguous=True), reads, writes)
    return P.op(q, lambda e: e.dma_start(out=out, in_=in_), reads, writes)


def mm(P, out, lhsT, rhs, start, stop, reads, writes):
    return P.op("tensor", lambda e: e.matmul(out, lhsT=lhsT, rhs=rhs, start=start, stop=stop), reads, writes)


def tr(P, out, in_, ident, reads, writes):
    return P.op("tensor", lambda e: e.transpose(out, in_, ident), reads, writes)


def act(P, out, in_, func, reads, writes, bias=None, scale=None, accum_out=None):
    kw = {}
    if bias is not None:
        kw["bias"] = bias
    if scale is not None:
        kw["scale"] = scale
    if accum_out is not None:
        kw["accum_out"] = accum_out
    return P.op("scalar", lambda e: e.activation(out=out, in_=in_, func=func, **kw), reads, writes)


def ts(P, eng, out, in0, s1, s2, op0, op1, reads, writes):
    if s2 is None:
        return P.op(eng, lambda e: e.tensor_scalar(out=out, in0=in0, scalar1=s1, scalar2=None, op0=op0), reads, writes)
    return P.op(eng, lambda e: e.tensor_scalar(out=out, in0=in0, scalar1=s1, scalar2=s2, op0=op0, op1=op1), reads, writes)


def tt(P, eng, out, in0, in1, op, reads, writes):
    return P.op(eng, lambda e: e.tensor_tensor(out=out, in0=in0, in1=in1, op=op), reads, writes)


def stt(P, eng, out, in0, scalar, in1, op0, op1, reads, writes):
    eng = "vector"
    return P.op(eng, lambda e: e.scalar_tensor_tensor(out=out, in0=in0, scalar=scalar, in1=in1, op0=op0, op1=op1),
                reads, writes)


def cp(P, eng, out, in_, reads, writes):
    if eng == "scalar":
        return P.op(eng, lambda e: e.activation(out=out, in_=in_, func=AF.Copy), reads, writes)
    return P.op(eng, lambda e: e.tensor_copy(out=out, in_=in_), reads, writes)


def mset(P, eng, ap, val, writes):
    return P.op(eng, lambda e: e.memset(ap, val), (), writes)


class Cfg:
    def __init__(self, D=2048, CH=1024, NH=16, DFF=8192):
        self.D = D
        self.CH = CH
        self.NH = NH
        self.CN = NH * 64
        self.DFF = DFF
        self.T = 8192
        self.KD = D // 128
        self.NCOL = 3 * CH + 3 * self.CN
        self.CB = min(512, CH)
        self.NCB = CH // self.CB
        self.DC = min(512, D)
        self.KF = DFF // 128
        self.KG = min(16, self.KF)
        self.MIXC = CH + self.CN
        self.KM = self.MIXC // 128


FULL = Cfg()

NAT_TYPES = ["int", "m0", "m1", "m30", "m31", "m32", "m33", "m62", "m63"]


def nat_block(m):
    if m == 0:
        return "m0", 0, 8
    if m == 1:
        return "m1", 0, 8
    if m == 62:
        return "m62", 120, 8
    if m == 63:
        return "m63", 120, 8
    if m in (31, 32, 33):
        return "m%d" % m, 2 * m - 6, 14
    if m == 30:
        return "m30", 56, 9
    return "int", 2 * m - 4, 9


def _n1p(ctype):
    n1 = np.arange(64)
    if ctype == 0:
        return n1
    return np.where(n1 < 32, n1, n1 + 32)


def fft_tables(ctype):
    N = 16384
    n2 = np.arange(128)[:, None, None]
    k1 = (np.arange(64) + 0.5)[None, None, :]
    out = {}
    for name, pl in (("nat", _n1p(0)), ("dat", _n1p(ctype))):
        n1p = pl[None, :, None]
        th = 2.0 * np.pi * np.mod((128 * n1p + n2) * k1, N) / N
        e = np.stack([np.cos(th), -np.sin(th)], axis=1)
        out["E" + name] = np.ascontiguousarray(e.transpose(2, 0, 1, 3)).reshape(64, 128 * 128)
        if name == "dat":
            g = np.stack([np.cos(th) * 2.0 / N, -np.sin(th) * 2.0 / N], axis=1)
            out["G"] = np.ascontiguousarray(g.transpose(1, 3, 0, 2)).reshape(128, 128 * 64)
    a = np.arange(128)
    ph = 2.0 * np.pi * (np.outer(a, a) % 128) / 128.0
    wre, wim = np.cos(ph), -np.sin(ph)
    out["W"] = np.concatenate([wre, wim, -wim, -wre], axis=1)
    return {k: v.astype(NPBF) for k, v in out.items()}


def filter_consts(ctype):
    T = 8192
    L = T if ctype == 0 else 4096
    f32 = np.float32
    j = np.arange(T)
    valid = j < L
    jj = np.where(valid, j, 0)
    t = (jj.astype(np.float64) / (L - 1)).astype(f32)
    bands = 16
    w_ang = (f32(2.0 * math.pi / L) * jj.astype(f32)).astype(f32)
    f = np.linspace(1e-4, bands - 1, bands, dtype=f32)
    ang = (w_ang[:, None] * f[None, :]).astype(f32)
    z = np.concatenate([t[:, None], np.cos(ang), -np.sin(ang)], axis=-1).astype(f32)
    zT = np.ascontiguousarray(z.T)
    tcol = np.ascontiguousarray(t.reshape(64, 128))
    mf = valid.astype(f32).reshape(64, 128)
    mb = mf.copy()
    mb[0, 0] = 0.0
    return {"zT": zT, "tcol": tcol, "mf": np.ascontiguousarray(mf), "mb": np.ascontiguousarray(mb)}


def nat_masks(ctype):
    def window(i):
        if ctype == 0:
            return int(np.clip(i - 4, 0, 120))
        base = 0 if i < 64 else 64
        return base + int(np.clip(i - base - 4, 0, 56))
    cols = np.arange(64)
    cs = np.clip(cols - 8, 0, 48)
    colok = (cols[None, :] >= cs[:, None]) & (cols[None, :] < cs[:, None] + 16)
    reps = {"int": 10, "m0": 0, "m1": 1, "m30": 30, "m31": 31, "m32": 32, "m33": 33, "m62": 62, "m63": 63}
    out = np.full((len(NAT_TYPES), 128, 896), NEG, np.float32)
    for ti, tn in enumerate(NAT_TYPES):
        m = reps[tn]
        _, ks, nr = nat_block(m)
        for ri in range(2):
            i = 2 * m + ri
            rs = window(i)
            for a in range(nr):
                r = ks + a
                if rs <= r < rs + 8:
                    blk = np.where(colok, 0.0, NEG)
                    out[ti, ri * 64:(ri + 1) * 64, a * 64:(a + 1) * 64] = blk
    return out.astype(NPBF)


def nat_tt_index():
    p = np.arange(128)
    ri, j = p // 64, p % 64
    B = np.arange(15)
    kc = np.arange(64)
    ro = B[None, :, None] - ri[:, None, None]
    co = 15 + kc[None, None, :] - j[:, None, None]
    ok = (ro >= 0) & (ro <= 14) & (co >= 0) & (co <= 30)
    idx = np.clip(ro, 0, 14) * 31 + np.clip(co, 0, 30)
    return idx.reshape(128, 960), ok.reshape(128, 960)


def build_program(cfg, phases=("w", "p1", "hy", "nat", "p3"), debug=False):
    c = cfg
    D, CH, NH, CN, DFF, T, KD, NCOL, CB, NCB = c.D, c.CH, c.NH, c.CN, c.DFF, c.T, c.KD, c.NCOL, c.CB, c.NCB
    nc = bass.Bass("TRN2", target_bir_lowering=False)

    def din(name, shape, dt=F32):
        return nc.dram_tensor(name, list(shape), dt, kind="ExternalInput").ap()

    okind = "ExternalOutput" if debug else "Internal"

    def dscr(name, shape, dt=BF16):
        if debug and name in debug:
            return nc.dram_tensor(name, list(shape), dt, kind="ExternalOutput").ap()
        return nc.dram_tensor(name, list(shape), dt).ap()

    x = din("x", [T, D])
    norm_mix_g = din("norm_mix_g", [D])
    w_in = din("w_in", [D, NCOL])
    hy_conv_w = din("hy_conv_w", [3, 3 * CH])
    hy_conv_b = din("hy_conv_b", [3 * CH])
    pe_w1 = din("pe_w1", [33, 64])
    pe_b1 = din("pe_b1", [64])
    pe_w2 = din("pe_w2", [64, 64])
    pe_b2 = din("pe_b2", [64])
    pe_w3 = din("pe_w3", [64, 64])
    pe_b3 = din("pe_b3", [64])
    pe_freq = din("pe_freq", [64])
    pe_w4 = din("pe_w4", [64, 4 * CH])
    hy_bias = din("hy_bias", [2, CH])
    nat_tt = din("nat_tt", [NH, 128, 960])
    gnorm = din("gnorm", [c.MIXC])
    w_out = din("w_out", [c.MIXC, D])
    norm_mlp_g = din("norm_mlp_g", [D])
    w_up = din("w_up", [D, DFF])
    w_down = din("w_down", [DFF, D])
    norm_f_g = din("norm_f_g", [D])
    c_ident = din("c_ident", [128, 128], BF16)
    c_Enat = din("c_Enat", [64, 128 * 128], BF16)
    c_Edat = din("c_Edat", [64, 128 * 128], BF16)
    c_G = din("c_G", [128, 128 * 64], BF16)
    c_W = din("c_W", [128, 512], BF16)
    c_zT = din("c_zT", [33, T])
    c_tcol = din("c_tcol", [64, 128])
    c_mf = din("c_mf", [64, 128])
    c_mb = din("c_mb", [64, 128])
    c_nad = din("c_nad", [CH])
    c_mask = din("c_mask", [len(NAT_TYPES), 128, 896], BF16)
    c_flag = din("c_flag", [1])
    y = nc.dram_tensor("y", [T, D], F32, kind="ExternalOutput").ap()
    Wb_in = dscr("Wb_in", [D, NCOL])
    Wb_out = dscr("Wb_out", [c.MIXC, D])
    Wb_up = dscr("Wb_up", [D, DFF])
    Wb_down = dscr("Wb_down", [DFF, D])
    u_hy = dscr("u_hy", [T + 2, 3 * CH])
    qT = dscr("qT", [CN, T])
    kT = dscr("kT", [CN, T])
    vn1 = dscr("vn1", [T, NH, 65])
    S1 = dscr("S1", [4, 2, 64, 128, CB])
    S2 = dscr("S2", [2, 128, 64, CB])
    KH = dscr("KH", [2, NCB, 128, 64, 2, CB])
    z1d = dscr("z1d", [T, CH])
    y_hy = dscr("y_hy", [T, CH])
    y_nat = dscr("y_nat", [T, CN])

    with contextlib.ExitStack() as top:
        P = Prog(nc, top)
        if "w" in phases:
            _phase_w(nc, P, c, locals())
        if "p1" in phases:
            _phase_p1(nc, P, c, locals())
        if "hy" in phases:
            _phase_hy(nc, P, c, locals())
        if "nat" in phases:
            _phase_nat(nc, P, c, locals())
        if "p3" in phases:
            _phase_p3(nc, P, c, locals())
    return nc


def _colvec_load(P, q, tile, key, src, n):
    v = src.rearrange("(k p) -> p k", p=128)
    dma(P, q, tile[:, 0:n], v, (), [key], slow=True)


def _phase_w(nc, P, c, A):
    D, CH, CN, DFF, NCOL = c.D, c.CH, c.CN, c.DFF, c.NCOL
    with contextlib.ExitStack() as st:
        S = lambda n, sh, dt: st.enter_context(nc.sbuf_tensor("w_" + n, sh, dt))
        gm = S("gm", [128, c.KD], F32)
        gq = S("gq", [128, c.KD], F32)
        gl = S("gl", [128, c.KD], F32)
        gh = S("gh", [128, c.KM], F32)
        one = S("one", [128, 1], F32)
        _colvec_load(P, "sync", gm, "gm", A["norm_mix_g"], c.KD)
        _colvec_load(P, "sync", gl, "gl", A["norm_mlp_g"], c.KD)
        _colvec_load(P, "sync", gh, "gh", A["gnorm"], c.KM)
        mset(P, "vector", one[:], 1.0, ["one"])
        ts(P, "vector", gq[:], gm[:], 0.125, None, ALU.mult, None, ["gm"], ["gq"])
        CW = 2048
        NWB = 6
        stg = [S("wst%d" % i, [128, CW], F32) for i in range(NWB)]
        stb = [S("wsb%d" % i, [128, CW], BF16) for i in range(NWB)]
        jobs = []
        q0, q1 = 3 * CH, 3 * CH + CN
        for r in range(D // 128):
            for (c0, c1, sc, sk) in ((0, q0, gm, "gm"), (q0, q1, gq, "gq"), (q1, NCOL, gm, "gm")):
                for cc in range(c0, c1, CW):
                    jobs.append((A["w_in"], A["Wb_in"], r, cc, min(CW, c1 - cc), sc, sk, r))
        engs = ["scalar", "vector"]
        for i, (src, dst, r, cc, w, sc, sk, si) in enumerate(jobs):
            b = i % NWB
            dma(P, "sync", stg[b][:, 0:w], src[r * 128:(r + 1) * 128, cc:cc + w], (), ["wst%d" % b])
            eng = engs[i % 2]
            if eng == "scalar":
                act(P, stb[b][:, 0:w], stg[b][:, 0:w], AF.Copy, ["wst%d" % b, sk], ["wsb%d" % b], scale=sc[:, si:si + 1])
            else:
                ts(P, eng, stb[b][:, 0:w], stg[b][:, 0:w], sc[:, si:si + 1], None, ALU.mult, None,
                   ["wst%d" % b, sk], ["wsb%d" % b])
            dma(P, "gpq", dst[r * 128:(r + 1) * 128, cc:cc + w], stb[b][:, 0:w], ["wsb%d" % b], [dst.tensor.name])
        P.emit()


def _phase_p1(nc, P, c, A):
    D, CH, CN, NH, T, KD, NCOL = c.D, c.CH, c.CN, c.NH, c.T, c.KD, c.NCOL
    TT1 = 1024
    NB = TT1 // 128
    NCC = 3 * CH // 128
    WG = 512
    x, Wb_in, u_hy, qT, kT, vn1 = A["x"], A["Wb_in"], A["u_hy"], A["qT"], A["kT"], A["vn1"]
    with contextlib.ExitStack() as st:
        S = lambda n, sh, dt: st.enter_context(nc.sbuf_tensor("p1_" + n, sh, dt))
        PS = lambda n, sh, dt: st.enter_context(nc.psum_tensor("p1_" + n, sh, dt))
        ident = S("ident", [128, 128], BF16)
        dma(P, "sync", ident[:], A["c_ident"], (), ["ident"])
        cw = S("cw", [128, NCC, 3], F32)
        cb = S("cb", [128, NCC], F32)
        for k in range(3):
            dma(P, "sync", cw[:, :, k], A["hy_conv_w"][k].rearrange("(k p) -> p k", p=128), (), ["cw"], slow=True)
        _colvec_load(P, "sync", cb, "cb", A["hy_conv_b"], NCC)
        flag = S("flag", [128, 1], F32)
        dma(P, "sync", flag[:], bass.AP(tensor=A["c_flag"].tensor, offset=0, ap=[[0, 128], [1, 1]]), (), ["flag"])
        nfw = S("nfw", [128, NCC, 3], F32)
        ts(P, "vector", nfw[:].rearrange("p a b -> p (a b)"), cw[:].rearrange("p a b -> p (a b)"), flag[:, 0:1], -1.0,
           ALU.mult, ALU.mult, ["cw", "flag"], ["nfw"])
        carry = S("carry", [128, NCC, 2], F32)
        mset(P, "vector", carry[:].rearrange("p a b -> p (a b)"), 0.0, ["carry"])
        xs = [S("xs%d" % i, [128, D], F32) for i in range(2)]
        ab = [S("ab%d" % i, [128, D], BF16) for i in range(2)]
        junk = S("junk", [128, D], F32)
        ss = S("ss", [128, 2 * NB], F32)
        aT = [S("aT%d" % i, [128, KD, TT1], BF16) for i in range(2)]
        wbuf = [S("wbuf%d" % i, [128, KD, WG], BF16) for i in range(2)]
        R = [S("R%d" % i, [128, 516], F32) for i in range(2)]
        o1 = [S("o1_%d" % i, [128, 512], F32) for i in range(2)]
        ob = [S("ob%d" % i, [128, 512], BF16) for i in range(2)]
        hst = [S("hst%d" % i, [128, 4, 512], BF16) for i in range(2)]
        qst = [S("qst%d" % i, [128, TT1], BF16) for i in range(2)]
        vst = [S("vst%d" % i, [128, 8, 65], BF16) for i in range(2)]
        for i in range(2):
            mset(P, "vector", vst[i][:].rearrange("p a b -> p (a b)"), 1.0, ["vst%d" % i])
        zrow = S("zrow", [1, 3 * CH], BF16)
        mset(P, "vector", zrow[:], 0.0, ["zrow"])
        dma(P, "gpq", u_hy[0:1, :], zrow[:], ["zrow"], ["u_hy_pad"])
        dma(P, "gpq", u_hy[T + 1:T + 2, :], zrow[:], ["zrow"], ["u_hy_pad"])
        psA = [PS("psA%d" % i, [128, 4, 128], BF16) for i in range(2)]
        psM = [PS("psM%d" % i, [128, 512], F32) for i in range(4)]
        psH = [PS("psH%d" % i, [128, 8, 128], BF16) for i in range(2)]
        nwg = NCOL // WG
        wcount = [0]

        def load_w(g):
            b = wcount[0] % 2
            wcount[0] += 1
            src = Wb_in[:, g * WG:(g + 1) * WG].rearrange("(k p) c -> p k c", p=128)
            dma(P, "sync", wbuf[b][:], src, ["Wb_in"], ["wbuf%d" % b])
            return b

        evi = [0]

        def ev_eng():
            evi[0] += 1
            return "scalar" if evi[0] % 2 else "vector"

        mcount = [0]
        hcount = [0]
        NTL = T // TT1
        DFF = c.DFF
        bgl = S("bgl", [128, KD], F32)
        bgh = S("bgh", [128, c.KM], F32)
        bone = S("bone", [128, 1], F32)
        _colvec_load(P, "sync", bgl, "bgl", A["norm_mlp_g"], KD)
        _colvec_load(P, "sync", bgh, "bgh", A["gnorm"], c.KM)
        mset(P, "vector", bone[:], 1.0, ["bone"])
        BCW = 1024
        NBG = 4
        bst = [S("bst%d" % i, [128, BCW], F32) for i in range(NBG)]
        bsb = [S("bsb%d" % i, [128, BCW], BF16) for i in range(NBG)]
        bjobs = []
        for r in range(c.MIXC // 128):
            for cc in range(0, D, BCW):
                bjobs.append((A["w_out"], A["Wb_out"], r, cc, min(BCW, D - cc), bgh, "bgh", r))
        for r in range(D // 128):
            for cc in range(0, DFF, BCW):
                bjobs.append((A["w_up"], A["Wb_up"], r, cc, min(BCW, DFF - cc), bgl, "bgl", r))
        for r in range(DFF // 128):
            for cc in range(0, D, BCW):
                bjobs.append((A["w_down"], A["Wb_down"], r, cc, min(BCW, D - cc), bone, "bone", 0))
        bgi = [0]

        def bg(n):
            for _ in range(n):
                i = bgi[0]
                if i >= len(bjobs):
                    return
                bgi[0] += 1
                src, dst, r, cc, w, sc, sk, si = bjobs[i]
                b = i % NBG
                dma(P, "sync", bst[b][:, 0:w], src[r * 128:(r + 1) * 128, cc:cc + w], (), ["bst%d" % b])
                if i % 2:
                    act(P, bsb[b][:, 0:w], bst[b][:, 0:w], AF.Copy, ["bst%d" % b, sk], ["bsb%d" % b], scale=sc[:, si:si + 1])
                else:
                    ts(P, "vector", bsb[b][:, 0:w], bst[b][:, 0:w], sc[:, si:si + 1], None, ALU.mult, None, ["bst%d" % b, sk], ["bsb%d" % b])
                dma(P, "gpq", dst[r * 128:(r + 1) * 128, cc:cc + w], bsb[b][:, 0:w], ["bsb%d" % b], ["bg_" + dst.tensor.name])
        bg_per = -(-len(bjobs) // (NTL * (NCOL // WG)))

        def prepA(ti, b):
            t0 = ti * TT1
            xb = b % 2
            dma(P, "sync", xs[xb][:], x[t0 + b * 128:t0 + (b + 1) * 128, :], (), ["xs%d" % xb])
            act(P, junk[:], xs[xb][:], AF.Square, ["xs%d" % xb], ["junk", "ss%d" % b], accum_out=ss[:, b:b + 1])
            act(P, ss[:, NB + b:NB + b + 1], ss[:, b:b + 1], AF.Sqrt, ["ss%d" % b], ["sr%d" % b], bias=EPS, scale=1.0 / D)
            P.op("vector", (lambda o, i: (lambda e: e.reciprocal(out=o, in_=i)))(ss[:, NB + b:NB + b + 1], ss[:, NB + b:NB + b + 1]),
                 ["sr%d" % b], ["sr%d" % b])
            if b % 2:
                ts(P, "vector", ab[xb][:], xs[xb][:], ss[:, NB + b:NB + b + 1], None, ALU.mult, None,
                   ["xs%d" % xb, "sr%d" % b], ["ab%d" % xb])
            else:
                act(P, ab[xb][:], xs[xb][:], AF.Copy, ["xs%d" % xb, "sr%d" % b], ["ab%d" % xb], scale=ss[:, NB + b:NB + b + 1])

        def prepB(ti, b):
            xb = b % 2
            at = aT[ti % 2]
            for k0 in range(0, KD, 4):
                pa = (k0 // 4) % 2
                nk = min(4, KD - k0)
                for kk in range(nk):
                    tr(P, psA[pa][:, kk, :], ab[xb][:, (k0 + kk) * 128:(k0 + kk + 1) * 128], ident[:],
                       ["ab%d" % xb, "ident"], ["psA%d" % pa])
                cp(P, ev_eng(), at[:, k0:k0 + nk, b * 128:(b + 1) * 128], psA[pa][:, 0:nk, :], ["psA%d" % pa], ["aT%d" % (ti % 2)])

        for b in range(NB):
            prepA(0, b)
            prepB(0, b)
        pending = []

        def flush_pending():
            while pending:
                pending.pop(0)()

        for ti in range(NTL):
            t0 = ti * TT1
            at = aT[ti % 2]
            atk = "aT%d" % (ti % 2)
            sched = {}
            if ti + 1 < NTL:
                for st_ in range(NB + 1):
                    sched.setdefault((st_ * nwg) // (NB + 1), []).append(st_)
            for g in range(nwg):
                wb = load_w(g)
                wk = "wbuf%d" % wb
                col0 = g * WG
                bg(bg_per)
                for st_ in sched.get(g, ()):
                    if st_ >= 1:
                        prepB(ti + 1, st_ - 1)
                    if st_ < NB:
                        prepA(ti + 1, st_)
                if col0 < 3 * CH:
                    for half in range(TT1 // 512):
                        s = ti * (TT1 // 512) + half
                        hb = hcount[0] % 2
                        hcount[0] += 1
                        for j in range(WG // 128):
                            cc = col0 // 128 + j
                            pm = mcount[0] % 4
                            mcount[0] += 1
                            for k in range(KD):
                                mm(P, psM[pm][:], wbuf[wb][:, k, j * 128:(j + 1) * 128], at[:, k, half * 512:(half + 1) * 512],
                                   k == 0, k == KD - 1, [wk, atk], ["psM%d" % pm])
                            rb = cc % 2
                            rk = "R%d" % rb
                            cp(P, "scalar", R[rb][:, 2:514], psM[pm][:], ["psM%d" % pm], [rk])
                            cp(P, "gpsimd", R[rb][:, 0:2], carry[:, cc, :], ["carry%d" % cc], [rk])
                            act(P, o1[rb][:], R[rb][:, 0:512], AF.Identity, [rk, "cw", "cb"], ["o1_%d" % rb],
                                bias=cb[:, cc:cc + 1], scale=cw[:, cc, 0:1])
                            stt(P, "vector", o1[rb][:], R[rb][:, 1:513], cw[:, cc, 1:2], o1[rb][:], ALU.mult, ALU.add,
                                [rk, "cw", "o1_%d" % rb], ["o1_%d" % rb])
                            stt(P, "vector", ob[rb][:], R[rb][:, 2:514], cw[:, cc, 2:3], o1[rb][:], ALU.mult, ALU.add,
                                [rk, "cw", "o1_%d" % rb], ["ob%d" % rb])
                            cp(P, "gpsimd", carry[:, cc, :], R[rb][:, 512:514], [rk], ["carry%d" % cc])
                            if s == 8:
                                stt(P, "vector", ob[rb][:, 0:1], R[rb][:, 2:3], nfw[:, cc, 2:3], ob[rb][:, 0:1], ALU.mult, ALU.add,
                                    [rk, "nfw", "ob%d" % rb], ["ob%d" % rb])
                                stt(P, "vector", ob[rb][:, 1:2], R[rb][:, 1:2], nfw[:, cc, 0:1], ob[rb][:, 1:2], ALU.mult, ALU.add,
                                    [rk, "nfw", "ob%d" % rb], ["ob%d" % rb])
                            flush_pending()

                            def post(j=j, rb=rb, hb=hb, s=s, col0=col0):
                                for q in range(4):
                                    tr(P, psH[q // 2][:, (q % 2) * 4 + j, :], ob[rb][:, q * 128:(q + 1) * 128], ident[:], ["ob%d" % rb, "ident"],
                                       ["psH%d" % (q // 2)])
                                if j == WG // 128 - 1:
                                    for q in range(4):
                                        cp(P, ev_eng(), hst[hb][:, q, :], psH[q // 2][:, (q % 2) * 4:(q % 2) * 4 + 4, :].rearrange("p a b -> p (a b)"),
                                           ["psH%d" % (q // 2)], ["hst%d" % hb])
                                    r0 = 512 * s
                                    dst = u_hy[r0:r0 + 512, col0:col0 + WG].rearrange("(q p) c -> p q c", p=128)
                                    dma(P, "gpq", dst, hst[hb][:], ["hst%d" % hb], ["u_hy%d" % hb])
                            pending.append(post)
                elif col0 < 3 * CH + 2 * CN:
                    dstT = qT if col0 < 3 * CH + CN else kT
                    cbase = col0 - (3 * CH if col0 < 3 * CH + CN else 3 * CH + CN)
                    for j in range(WG // 128):
                        qb = j % 2
                        for half in range(TT1 // 512):
                            pm = mcount[0] % 4
                            mcount[0] += 1
                            for k in range(KD):
                                mm(P, psM[pm][:], wbuf[wb][:, k, j * 128:(j + 1) * 128], at[:, k, half * 512:(half + 1) * 512],
                                   k == 0, k == KD - 1, [wk, atk], ["psM%d" % pm])
                            flush_pending()
                            cp(P, ev_eng(), qst[qb][:, half * 512:(half + 1) * 512], psM[pm][:], ["psM%d" % pm], ["qst%d" % qb])
                        dma(P, "gpq", dstT[cbase + j * 128:cbase + (j + 1) * 128, t0:t0 + TT1], qst[qb][:], ["qst%d" % qb], ["qkT%d" % qb])
                else:
                    h0 = (col0 - 3 * CH - 2 * CN) // 64
                    for b in range(NB):
                        pm = mcount[0] % 4
                        mcount[0] += 1
                        vb = b % 2
                        for k in range(KD):
                            mm(P, psM[pm][:], at[:, k, b * 128:(b + 1) * 128], wbuf[wb][:, k, :], k == 0, k == KD - 1,
                               [wk, atk], ["psM%d" % pm])
                        cp(P, ev_eng(), vst[vb][:, :, 0:64], psM[pm][:].rearrange("p (h d) -> p h d", d=64), ["psM%d" % pm], ["vst%d" % vb])
                        dma(P, "gpq", vn1[t0 + b * 128:t0 + (b + 1) * 128, h0:h0 + 8, :], vst[vb][:], ["vst%d" % vb], ["vn1_%d" % vb])
        flush_pending()
        bg(len(bjobs))
        fl = S("fl", [128, NCC, 2], F32)
        flb = S("flb", [128, NCC], BF16)
        frow = S("frow", [1, NCC * 128], BF16)
        keys = ["carry%d" % i for i in range(NCC)]
        tt(P, "vector", fl[:], carry[:], cw[:, :, 0:2], ALU.mult, keys + ["cw", "fl"], ["fl"])
        tt(P, "vector", fl[:, :, 0], fl[:, :, 0], fl[:, :, 1], ALU.add, ["fl"], ["fl"])
        tt(P, "vector", flb[:], fl[:, :, 0], cb[:], ALU.add, ["fl", "cb"], ["flb"])
        for c0 in range(0, NCC, 4):
            pa = (c0 // 4) % 2
            for kk in range(4):
                tr(P, psA[pa][0:1, kk, :], flb[:, c0 + kk:c0 + kk + 1], ident[:], ["flb", "ident"], ["psA%d" % pa])
            cp(P, "vector", frow[:, c0 * 128:(c0 + 4) * 128], psA[pa][0:1, :, :].rearrange("p a b -> p (a b)"), ["psA%d" % pa], ["frow"])
        dma(P, "gpq", u_hy[T:T + 1, :], frow[:], ["frow"], ["u_hy"])
        P.emit()


def _phase_hy(nc, P, c, A):
    CH, T, CB, NCB = c.CH, c.T, c.CB, c.NCB
    u_hy, S1, S2, KH, y_hy, z1d = A["u_hy"], A["S1"], A["S2"], A["KH"], A["y_hy"], A["z1d"]
    NG = 4
    KGRP = 2
    LOOK = 2
    with contextlib.ExitStack() as st0:
        S0 = lambda n, sh, dt: st0.enter_context(nc.sbuf_tensor("hy_" + n, sh, dt))
        h3b = S0("h3b", [64, T], BF16)
        w4b = S0("w4b", [64, 4 * CH], BF16)
        with contextlib.ExitStack() as st:
            S = lambda n, sh, dt: st.enter_context(nc.sbuf_tensor("hy_" + n, sh, dt))
            PS = lambda n, sh, dt: st.enter_context(nc.psum_tensor("hy_" + n, sh, dt))
            zT = S("zT", [33, T], F32)
            dma(P, "sync", zT[:], A["c_zT"], (), ["zT"])
            w1 = S("w1", [33, 64], F32)
            w2 = S("w2", [64, 64], F32)
            w3 = S("w3", [64, 64], F32)
            dma(P, "sync", w1[:], A["pe_w1"], (), ["w1"])
            dma(P, "sync", w2[:], A["pe_w2"], (), ["w2"])
            dma(P, "sync", w3[:], A["pe_w3"], (), ["w3"])
            pv = S("pv", [64, 8], F32)
            for i, nm in enumerate(["pe_freq", "pe_b1", "pe_b2", "pe_b3"]):
                dma(P, "sync", pv[:, i:i + 1], A[nm].rearrange("(p o) -> p o", o=1), (), ["pv"], slow=True)
            ts(P, "vector", pv[:, 4:5], pv[:, 0:1], 1.0 / (2 * math.pi), None, ALU.mult, None, ["pv"], ["pv"])
            for i in range(3):
                tt(P, "vector", pv[:, 5 + i:6 + i], pv[:, 4:5], pv[:, 1 + i:2 + i], ALU.mult, ["pv"], ["pv"])
            hA = S("hA", [64, T], F32)
            hB = S("hB", [64, T], F32)
            w4f = S("w4f", [64, 4 * CH], F32)
            dma(P, "sync", w4f[:], A["pe_w4"], (), ["w4f"])
            cp(P, "vector", w4b[:], w4f[:], ["w4f"], ["w4b"])
            uu = [S("uu%d" % i, [64, 512], F32) for i in range(2)]
            ui = [S("ui%d" % i, [64, 512], I32) for i in range(2)]
            uf = [S("uf%d" % i, [64, 512], F32) for i in range(2)]
            psF = [PS("psF%d" % i, [64, 512], F32) for i in range(2)]
            layers = [(w1, "w1", zT, hA, 33), (w2, "w2", hA, hB, 64), (w3, "w3", hB, None, 64)]
            n = 0
            for li, (wl, wk, src, dst, K) in enumerate(layers):
                for ch in range(T // 512):
                    b = n % 2
                    n += 1
                    sl = slice(ch * 512, (ch + 1) * 512)
                    srck = "zT" if li == 0 else "h%d_%d" % (li - 1, ch)
                    mm(P, psF[b][:], wl[0:K, :], src[0:K, sl], True, True, [wk, srck], ["psF%d" % b])
                    ts(P, "vector", uu[b][:], psF[b][:], pv[:, 4:5], pv[:, 5 + li:6 + li], ALU.mult, ALU.add,
                       ["psF%d" % b, "pv"], ["uu%d" % b])
                    cp(P, "vector", ui[b][:], uu[b][:], ["uu%d" % b], ["ui%d" % b])
                    cp(P, "gpsimd", uf[b][:], ui[b][:], ["ui%d" % b], ["uf%d" % b])
                    tt(P, "gpsimd", uu[b][:], uu[b][:], uf[b][:], ALU.subtract, ["uu%d" % b, "uf%d" % b], ["uu%d" % b])
                    outap = h3b[:, sl] if li == 2 else dst[:, sl]
                    act(P, outap, uu[b][:], AF.Sin, ["uu%d" % b], ["h%d_%d" % (li, ch)], scale=2 * math.pi)
            P.emit()

        with contextlib.ExitStack() as st:
            S = lambda n, sh, dt: st.enter_context(nc.sbuf_tensor("hy_" + n, sh, dt))
            PS = lambda n, sh, dt: st.enter_context(nc.psum_tensor("hy_" + n, sh, dt))
            W = S("W", [128, 4, 128], BF16)
            dma(P, "sync", W[:].rearrange("p a b -> p (a b)"), A["c_W"], (), ["W"])
            Wre, Wim, nWim, nWre = W[:, 0, :], W[:, 1, :], W[:, 2, :], W[:, 3, :]
            tcol = S("tcol", [64, 128], F32)
            mf = S("mf", [64, 128], F32)
            mb = S("mb", [64, 128], F32)
            dma(P, "sync", tcol[:], A["c_tcol"], (), ["tcol"])
            dma(P, "sync", mf[:], A["c_mf"], (), ["mf"])
            dma(P, "sync", mb[:], A["c_mb"], (), ["mb"])
            nad = S("nad", [64, CH], F32)
            dma(P, "sync", nad[:], bass.AP(tensor=A["c_nad"].tensor, offset=0, ap=[[0, 64], [1, CH]]), (), ["nad"])
            hbias = S("hbias", [64, 2, CH], F32)
            dma(P, "sync", hbias[:], bass.AP(tensor=A["hy_bias"].tensor, offset=0, ap=[[0, 64], [CH, 2], [1, CH]]), (), ["hbias"])
            Eg = [S("Eg%d" % i, [64, NG, 128], BF16) for i in range(2)]
            Gg = [S("Gg%d" % i, [128, NG, 64], BF16) for i in range(2)]
            xin = [S("xin%d" % i, [64, NG, CB], BF16) for i in range(2)]
            x1s = [S("x1s%d" % i, [128, NG, CB], BF16) for i in range(2)]
            dec = [S("dec%d" % i, [64, CB], F32) for i in range(3)]
            hf = [S("hf%d" % i, [64, CB], BF16) for i in range(4)]
            xb1 = [S("xb1_%d" % i, [128, 2, KGRP, CB], BF16) for i in range(2)]
            xb2 = [S("xb2_%d" % i, [128, 2, KGRP, CB], BF16) for i in range(2)]
            khs = [S("khs%d" % i, [128, KGRP, 2, CB], BF16) for i in range(2)]
            yh = [S("yh%d" % i, [128, 2, CB], BF16) for i in range(3)]
            tmp = [S("tmp%d" % i, [128, CB], F32) for i in range(8)]
            vs = [S("vs%d" % i, [128, 2, KGRP, CB], BF16) for i in range(2)]
            vin = [S("vin%d" % i, [128, NG, CB], BF16) for i in range(2)]
            gv = [S("gv%d" % i, [64, NG, CB], BF16) for i in range(2)]
            gx = [S("gx%d" % i, [64, NG, CB], BF16) for i in range(2)]
            vbz = [S("vbz%d" % i, [64, CB], F32) for i in range(2)]
            zt = [S("zt%d" % i, [64, CB], F32) for i in range(2)]
            zo = [S("zo%d" % i, [64, NG, CB], BF16) for i in range(2)]
            pB = [PS("pB%d" % i, [128, CB], F32) for i in range(4)]
            pV = [PS("pV%d" % i, [128, CB], F32) for i in range(4)]
            cnt = {}

            def nxt(k, m):
                v = cnt.get(k, 0)
                cnt[k] = v + 1
                return v % m

            def ev_eng():
                return "scalar" if nxt("ev", 2) else "vector"

            Enat_v = A["c_Enat"].rearrange("p (a b) -> p a b", a=128, b=128)
            Edat_v = A["c_Edat"].rearrange("p (a b) -> p a b", a=128, b=128)
            G_v = A["c_G"].rearrange("p (a b) -> p a b", a=128, b=64)
            h3v = h3b[:].rearrange("p (a b) -> p b a", b=128)

            def tokview(ap2d):
                return ap2d.rearrange("(a b) c -> a b c", b=128)

            def stageA(slot, Ev, get_rhs):
                items = [(gi, j) for gi in range(128 // NG) for j in range(NG)]
                ebs, sbs = {}, {}
                queue = []
                for it in items + [None] * LOOK:
                    cur = None
                    if it is not None:
                        gi, j = it
                        if j == 0:
                            ebs[gi] = nxt("eg", 2)
                            dma(P, "sync", Eg[ebs[gi]][:], Ev[:, gi * NG:(gi + 1) * NG], (), ["Eg%d" % ebs[gi]])
                        rhs, rkeys = get_rhs(gi, j)
                        cur = (gi, j, rhs, rkeys)
                        queue.append(cur)
                    if len(queue) > LOOK or (it is None and queue):
                        gi, j, rhs, rkeys = queue.pop(0)
                        eb = ebs[gi]
                        if j == 0:
                            sbs[gi] = nxt("x1s", 2)
                        sb = sbs[gi]
                        p = nxt("pB", 4)
                        mm(P, pB[p][:], Eg[eb][:, j, :], rhs, True, True, ["Eg%d" % eb] + rkeys, ["pB%d" % p])
                        cp(P, "scalar" if nxt("evA", 2) else "vector", x1s[sb][:, j, :], pB[p][:], ["pB%d" % p], ["x1s%d_%d" % (sb, j)])
                        if j == NG - 1:
                            dst = S1[slot].rearrange("r k n c -> (r k) n c")[:, gi * NG:(gi + 1) * NG, :]
                            dma(P, "gpq", dst, x1s[sb][:], ["x1s%d_%d" % (sb, jj) for jj in range(NG)], ["S1_%d_%d" % (slot, gi % 2)])

            def data_rhs(src2d, skey):
                tv = tokview(src2d)
                state = {}

                def get(gi, j):
                    if j == 0:
                        b = nxt("xin", 2)
                        state["b"] = b
                        dma(P, "sync", xin[b][:], tv[:, gi * NG:(gi + 1) * NG, :], [skey], ["xin%d" % b])
                    b = state["b"]
                    return xin[b][:, j, :], ["xin%d" % b]
                return get

            def filt_rhs(od, c0):
                o, dr = od // 2, od % 2
                wcol = o * 2 * CH + dr * CH + c0
                msk, mk = (mf, "mf") if dr == 0 else (mb, "mb")

                def get(gi, j):
                    n2 = gi * NG + j
                    d = nxt("dec", 3)
                    act(P, dec[d][:], nad[:, c0:c0 + CB], AF.Exp, ["nad", "tcol"], ["dec%d" % d], scale=tcol[:, n2:n2 + 1])
                    p = nxt("pV", 4)
                    mm(P, pV[p][0:64, :], h3v[:, n2, :], w4b[:, wcol:wcol + CB], True, True, ["h3b", "w4b"], ["pV%d" % p])
                    hb_ = nxt("hf", 4)
                    stt(P, "vector", hf[hb_][:], pV[p][0:64, :], msk[:, n2:n2 + 1], dec[d][:], ALU.mult, ALU.mult,
                        ["pV%d" % p, mk, "dec%d" % d], ["hf%d" % hb_])
                    return hf[hb_][:], ["hf%d" % hb_]
                return get

            def load_xb(buf, bk, slot, k1a, k1b):
                for ri in range(2):
                    src = S1[slot, ri, k1a:k1b, :, :].rearrange("k n c -> n k c")
                    dma(P, "sync", buf[:, ri, 0:k1b - k1a, :], src, ["S1_%d_0" % slot, "S1_%d_1" % slot], [bk])

            def stageB_filter(o, cbi):
                for k1a in range(0, 64, KGRP):
                    k1b = min(64, k1a + KGRP)
                    b = nxt("xb", 2)
                    load_xb(xb1[b], "xb1_%d" % b, 2 * o, k1a, k1b)
                    load_xb(xb2[b], "xb2_%d" % b, 2 * o + 1, k1a, k1b)
                    kb = nxt("khs", 2)
                    rk = ["xb1_%d" % b, "xb2_%d" % b, "W"]
                    for kk in range(k1b - k1a):
                        fr, fi = xb1[b][:, 0, kk, :], xb1[b][:, 1, kk, :]
                        br, bi = xb2[b][:, 0, kk, :], xb2[b][:, 1, kk, :]
                        pr = nxt("pB", 4)
                        for n_, (w_, x_) in enumerate(((Wre, fr), (nWim, fi), (Wre, br), (nWim, bi))):
                            mm(P, pB[pr][:], w_, x_, n_ == 0, n_ == 3, rk, ["pB%d" % pr])
                        cp(P, ev_eng(), khs[kb][:, kk, 0, :], pB[pr][:], ["pB%d" % pr], ["khs%d" % kb])
                        pi = nxt("pB", 4)
                        for n_, (w_, x_) in enumerate(((Wim, fr), (Wre, fi), (nWim, br), (nWre, bi))):
                            mm(P, pB[pi][:], w_, x_, n_ == 0, n_ == 3, rk, ["pB%d" % pi])
                        cp(P, ev_eng(), khs[kb][:, kk, 1, :], pB[pi][:], ["pB%d" % pi], ["khs%d" % kb])
                    dma(P, "gpq", KH[o, cbi, :, k1a:k1b, :, :], khs[kb][:, 0:k1b - k1a, :, :], ["khs%d" % kb], ["KH%d" % o])

            def stageB_conv(slot, o, cbi):
                items = [(k1a, kk) for k1a in range(0, 64, KGRP) for kk in range(min(64, k1a + KGRP) - k1a)]
                grp = {}
                queue = []
                for it in items + [None] * LOOK:
                    cur = None
                    if it is not None:
                        k1a, kk = it
                        k1b = min(64, k1a + KGRP)
                        if kk == 0:
                            b = nxt("xb", 2)
                            load_xb(xb1[b], "xb1_%d" % b, slot, k1a, k1b)
                            kb = nxt("khs", 2)
                            dma(P, "sync", khs[kb][:, 0:k1b - k1a, :, :], KH[o, cbi, :, k1a:k1b, :, :], ["KH%d" % o], ["khs%d" % kb])
                            grp[k1a] = [b, kb, None]
                        b, kb, _ = grp[k1a]
                        rk = ["xb1_%d" % b, "W"]
                        xr, xi = xb1[b][:, 0, kk, :], xb1[b][:, 1, kk, :]
                        kr, ki = khs[kb][:, kk, 0, :], khs[kb][:, kk, 1, :]
                        pr = nxt("pB", 4)
                        mm(P, pB[pr][:], Wre, xr, True, False, rk, ["pB%d" % pr])
                        mm(P, pB[pr][:], nWim, xi, False, True, rk, ["pB%d" % pr])
                        pi = nxt("pB", 4)
                        mm(P, pB[pi][:], Wim, xr, True, False, rk, ["pB%d" % pi])
                        mm(P, pB[pi][:], Wre, xi, False, True, rk, ["pB%d" % pi])
                        t = [nxt("tmp", 8) for _ in range(4)]
                        kk_ = ["khs%d" % kb]
                        tt(P, "vector", tmp[t[0]][:], pB[pr][:], kr, ALU.mult, ["pB%d" % pr] + kk_, ["tmp%d" % t[0]])
                        tt(P, "vector", tmp[t[1]][:], pB[pi][:], ki, ALU.mult, ["pB%d" % pi] + kk_, ["tmp%d" % t[1]])
                        tt(P, "vector", tmp[t[2]][:], pB[pr][:], ki, ALU.mult, ["pB%d" % pr] + kk_, ["tmp%d" % t[2]])
                        tt(P, "vector", tmp[t[3]][:], pB[pi][:], kr, ALU.mult, ["pB%d" % pi] + kk_, ["tmp%d" % t[3]])
                        yb = nxt("yh", 3)
                        tt(P, "gpsimd", yh[yb][:, 0, :], tmp[t[0]][:], tmp[t[1]][:], ALU.subtract, ["tmp%d" % t[0], "tmp%d" % t[1]], ["yh%d" % yb])
                        tt(P, "gpsimd", yh[yb][:, 1, :], tmp[t[2]][:], tmp[t[3]][:], ALU.add, ["tmp%d" % t[2], "tmp%d" % t[3]], ["yh%d" % yb])
                        cur = (k1a, kk, yb)
                        queue.append(cur)
                    if len(queue) > LOOK or (it is None and queue):
                        k1a, kk, yb = queue.pop(0)
                        k1b = min(64, k1a + KGRP)
                        if kk == 0:
                            grp[k1a][2] = nxt("vs", 2)
                        vb = grp[k1a][2]
                        yr, yi = yh[yb][:, 0, :], yh[yb][:, 1, :]
                        yk = ["yh%d" % yb, "W"]
                        vr = nxt("pV", 4)
                        mm(P, pV[vr][:], Wre, yr, True, False, yk, ["pV%d" % vr])
                        mm(P, pV[vr][:], Wim, yi, False, True, yk, ["pV%d" % vr])
                        cp(P, "scalar", vs[vb][:, 0, kk, :], pV[vr][:], ["pV%d" % vr], ["vs%d" % vb])
                        vi = nxt("pV", 4)
                        mm(P, pV[vi][:], nWim, yr, True, False, yk, ["pV%d" % vi])
                        mm(P, pV[vi][:], Wre, yi, False, True, yk, ["pV%d" % vi])
                        cp(P, "scalar", vs[vb][:, 1, kk, :], pV[vi][:], ["pV%d" % vi], ["vs%d" % vb])
                        if kk == k1b - k1a - 1:
                            for ri in range(2):
                                dma(P, "gpq", S2[ri, :, k1a:k1b, :], vs[vb][:, ri, 0:k1b - k1a, :], ["vs%d" % vb], ["S2_%d" % ri])

            def stageAp(o, cbi):
                c0 = cbi * CB
                gate2d = u_hy[1:T + 1, (1 + o) * CH + c0:(1 + o) * CH + c0 + CB]
                zsrc2d, zkey = (u_hy[1:T + 1, c0:c0 + CB], "u_hy") if o == 0 else (z1d[:, c0:c0 + CB], "z1d")
                dst2d, dkey = (z1d[:, c0:c0 + CB], "z1d") if o == 0 else (y_hy[:, c0:c0 + CB], "y_hy")
                gview, zview, dview = tokview(gate2d), tokview(zsrc2d), tokview(dst2d)
                def ap_loads(gi):
                    gb = nxt("gg", 2)
                    dma(P, "sync", Gg[gb][:], G_v[:, gi * NG:(gi + 1) * NG], (), ["Gg%d" % gb])
                    vb = nxt("vin", 2)
                    for ri in range(2):
                        src = S2[ri, gi * NG:(gi + 1) * NG, :, :].rearrange("n k c -> k n c")
                        dma(P, "sync", vin[vb][ri * 64:(ri + 1) * 64, :, :], src, ["S2_%d" % ri], ["vin%d" % vb])
                    xb_ = nxt("gx", 2)
                    dma(P, "sync", gx[xb_][:], gview[:, gi * NG:(gi + 1) * NG, :], ["u_hy"], ["gx%d" % xb_])
                    dma(P, "sync", gv[xb_][:], zview[:, gi * NG:(gi + 1) * NG, :], [zkey], ["gv%d" % xb_])
                    return gb, vb, xb_

                nxt_ld = ap_loads(0)
                for gi in range(128 // NG):
                    gb, vb, xb_ = nxt_ld
                    if gi + 1 < 128 // NG:
                        nxt_ld = ap_loads(gi + 1)
                    ob_ = nxt("zo", 2)
                    for j in range(NG):
                        p = nxt("pV", 4)
                        mm(P, pV[p][0:64, :], Gg[gb][:, j, :], vin[vb][:, j, :], True, True, ["Gg%d" % gb, "vin%d" % vb], ["pV%d" % p])
                        q = nxt("vbz", 2)
                        tt(P, "gpsimd", vbz[q][:], gv[xb_][:, j, :], hbias[:, o, c0:c0 + CB], ALU.mult, ["gv%d" % xb_, "hbias"], ["vbz%d" % q])
                        tt(P, "vector", zt[q][:], pV[p][0:64, :], vbz[q][:], ALU.add, ["pV%d" % p, "vbz%d" % q], ["zt%d" % q])
                        tt(P, "vector", zo[ob_][:, j, :], zt[q][:], gx[xb_][:, j, :], ALU.mult, ["zt%d" % q, "gx%d" % xb_], ["zo%d" % ob_])
                    dma(P, "gpq", dview[:, gi * NG:(gi + 1) * NG, :], zo[ob_][:], ["zo%d" % ob_], [dkey])

            for cbi in range(NCB):
                c0 = cbi * CB
                for od in range(4):
                    stageA(od, Enat_v, filt_rhs(od, c0))
                for o in range(2):
                    stageB_filter(o, cbi)
                stageA(0, Edat_v, data_rhs(u_hy[1:T + 1, c0:c0 + CB], "u_hy"))
                stageB_conv(0, 0, cbi)
                stageAp(0, cbi)
                stageA(1, Edat_v, data_rhs(z1d[:, c0:c0 + CB], "z1d"))
                stageB_conv(1, 1, cbi)
                stageAp(1, cbi)
            P.emit()


def _phase_nat(nc, P, c, A):
    NH, CN, T = c.NH, c.CN, c.T
    NHP = NH // 2
    qT, kT, vn1, y_nat = A["qT"], A["kT"], A["vn1"], A["y_nat"]
    NTY = len(NAT_TYPES)
    with contextlib.ExitStack() as st:
        S = lambda n, sh, dt: st.enter_context(nc.sbuf_tensor("nat_" + n, sh, dt))
        PS = lambda n, sh, dt: st.enter_context(nc.psum_tensor("nat_" + n, sh, dt))
        ident = S("ident", [128, 128], BF16)
        dma(P, "sync", ident[:], A["c_ident"], (), ["ident"])
        msk = S("msk", [128, NTY, 896], BF16)
        dma(P, "sync", msk[:], A["c_mask"].rearrange("t p c -> p t c"), (), ["msk"])
        TT = S("TT", [128, NH, 960], BF16)
        TMint = S("TMint", [128, NH, 576], BF16)
        tst = [S("tst%d" % i, [128, 960], F32) for i in range(2)]
        for h in range(NH):
            b = h % 2
            dma(P, "sync", tst[b][:], A["nat_tt"][h], (), ["tst%d" % b])
            cp(P, "vector", TT[:, h, :], tst[b][:], ["tst%d" % b], ["TT%d" % h])
            tt(P, "gpsimd", TMint[:, h, :], TT[:, h, 192:768], msk[:, 0, 0:576], ALU.add, ["TT%d" % h, "msk"], ["TMint%d" % h])
        TMT = S("TMT", [128, NH, 640], BF16)
        ssb = [S("ssb%d" % i, [128, 640], F32) for i in range(3)]
        qt = [S("qt%d" % i, [128, NHP, 512], BF16) for i in range(2)]
        kt = [S("kt%d" % i, [128, NHP, 896], BF16) for i in range(2)]
        v1 = [S("v1_%d" % i, [128, 7, NH, 65], BF16) for i in range(2)]
        tmsp = [S("tmsp%d" % i, [128, 896], BF16) for i in range(2)]
        pt = [S("pt%d" % i, [128, 1024], BF16) for i in range(3)]
        ys = [S("ys%d" % i, [128, CN], BF16) for i in range(2)]
        rec = [S("rec%d" % i, [128, 4], F32) for i in range(2)]
        stS = contextlib.ExitStack()
        pSb = [stS.enter_context(nc.psum_tensor("nat_pSb0", [128, 1024], BF16))] * 2
        cnt = {}

        def nxt(k, m):
            v = cnt.get(k, 0)
            cnt[k] = v + 1
            return v % m

        qTv = qT.rearrange("(c p) t -> p c t", p=128)
        kTv = kT.rearrange("(c p) t -> p c t", p=128)
        mset(P, "vector", TMT[:].rearrange("p a b -> p (a b)"), 0.0, ["TMTall"])
        for h in range(NH):
            for kb in range(5):
                kp = 128 if kb < 4 else 64
                sb = 0
                tr(P, pSb[sb][0:kp, kb * 128:(kb + 1) * 128], TMint[:, h, kb * 128:kb * 128 + kp], ident[:], ["TMint%d" % h, "ident"], ["pSb%d" % sb])
            cp(P, "vector" if h % 2 else "scalar", TMT[:, h, 0:512], pSb[0][:, 0:512], ["pSb0"], ["TMT%d" % h, "TMTall"])
            cp(P, "vector" if h % 2 else "scalar", TMT[0:64, h, 512:640], pSb[0][0:64, 512:640], ["pSb0"], ["TMT%d" % h, "TMTall"])
        P.emit()
        stS.close()
        pS = [PS("pS%d" % i, [128, 1024], F32) for i in range(3)]
        pO = [PS("pO%d" % i, [128, 4, 65], F32) for i in range(2)]
        for m in range(T // 128):
            tname, ks, nr = nat_block(m)
            ti = NAT_TYPES.index(tname)
            Bs = ks - 2 * m + 7
            nfull = nr // 2
            nkb = (nr + 1) // 2
            if m % 4 == 0:
                qb = nxt("qt", 2)
                dma(P, "sync", qt[qb][:], qTv[:, :, m * 128:m * 128 + 512], (), ["qt%d" % qb])
            qcol = (m % 4) * 128
            kb_ = nxt("kt", 2)
            dma(P, "sync", kt[kb_][:, :, 0:nr * 64], kTv[:, :, ks * 64:(ks + nr) * 64], (), ["kt%d" % kb_])
            vb = nxt("v1", 2)
            t0 = ks * 64
            dma(P, "sync", v1[vb][:, 0:nfull, :, :], vn1[t0:t0 + nfull * 128].rearrange("(b p) h e -> p b h e", p=128), (), ["v1_%d" % vb])
            if nr % 2:
                dma(P, "sync", v1[vb][0:64, nfull, :, :], vn1[t0 + nfull * 128:t0 + nfull * 128 + 64], (), ["v1_%d" % vb])
            yb = nxt("ys", 2)
            pend = []
            obs = {}
            for h in range(NH):
                hp, hc = h % 2, h // 2
                if tname == "int":
                    TMh, tmk = TMint[:, h, :], "TMint%d" % h
                else:
                    tb = nxt("tmsp", 2)
                    tt(P, "vector", tmsp[tb][:, 0:nr * 64], TT[:, h, Bs * 64:(Bs + nr) * 64], msk[:, ti, 0:nr * 64], ALU.add,
                       ["TT%d" % h, "msk"], ["tmsp%d" % tb])
                    TMh, tmk = tmsp[tb], "tmsp%d" % tb
                sb = nxt("pS", 3)
                if tname == "int":
                    for kb in range(nkb):
                        kp = 128 if kb < nfull else 64
                        out = pS[sb][0:kp, kb * 128:(kb + 1) * 128]
                        mm(P, out, kt[kb_][hp * 64:(hp + 1) * 64, hc, kb * 128:kb * 128 + kp], qt[qb][hp * 64:(hp + 1) * 64, hc, qcol:qcol + 128],
                           True, True, ["kt%d" % kb_, "qt%d" % qb], ["pS%d" % sb])
                    tt(P, "vector", ssb[sb][:, 0:512], pS[sb][:, 0:512], TMT[:, h, 0:512], ALU.add, ["pS%d" % sb, "TMT%d" % h], ["ssb%d" % sb])
                    tt(P, "vector", ssb[sb][0:64, 512:640], pS[sb][0:64, 512:640], TMT[0:64, h, 512:640], ALU.add, ["pS%d" % sb, "TMT%d" % h], ["ssb%d" % sb])
                    act(P, pt[sb][:, 0:512], ssb[sb][:, 0:512], AF.Exp, ["ssb%d" % sb], ["pt%d" % sb])
                    act(P, pt[sb][0:64, 512:640], ssb[sb][0:64, 512:640], AF.Exp, ["ssb%d" % sb], ["pt%d" % sb])
                else:
                    for kb in range(nkb):
                        kp = 128 if kb < nfull else 64
                        out = pS[sb][0:kp, kb * 128:(kb + 1) * 128]
                        mm(P, out, kt[kb_][hp * 64:(hp + 1) * 64, hc, kb * 128:kb * 128 + kp], qt[qb][hp * 64:(hp + 1) * 64, hc, qcol:qcol + 128],
                           True, False, ["kt%d" % kb_, "qt%d" % qb], ["pS%d" % sb])
                        mm(P, out, TMh[:, kb * 128:kb * 128 + kp], ident[:], False, True, [tmk, "ident"], ["pS%d" % sb])
                    act(P, pt[sb][:, 0:nfull * 128], pS[sb][:, 0:nfull * 128], AF.Exp, ["pS%d" % sb], ["pt%d" % sb])
                    if nr % 2:
                        act(P, pt[sb][0:64, nfull * 128:nkb * 128], pS[sb][0:64, nfull * 128:nkb * 128], AF.Exp, ["pS%d" % sb], ["pt%d" % sb])
                while len(pend) > 1:
                    pend.pop(0)()
                if h % 4 == 0:
                    obs[h // 4] = nxt("pO", 2)

                def pv(h=h, sb=sb):
                    ob = obs[h // 4]
                    for kb in range(nkb):
                        kp = 128 if kb < nfull else 64
                        mm(P, pO[ob][:, h % 4, :], pt[sb][0:kp, kb * 128:(kb + 1) * 128], v1[vb][0:kp, kb, h, :], kb == 0, kb == nkb - 1,
                           ["pt%d" % sb, "v1_%d" % vb], ["pO%d" % ob])
                    if h % 4 == 3:
                        rb = nxt("rec", 2)
                        P.op("vector", (lambda o_, i_: (lambda e: e.reciprocal(out=o_, in_=i_)))(rec[rb][:], pO[ob][:, :, 64]),
                             ["pO%d" % ob], ["rec%d" % rb])
                        for hh in range(4):
                            hd = h - 3 + hh
                            if hh % 2:
                                act(P, ys[yb][:, hd * 64:(hd + 1) * 64], pO[ob][:, hh, 0:64], AF.Copy, ["pO%d" % ob, "rec%d" % rb], ["ys%d" % yb],
                                    scale=rec[rb][:, hh:hh + 1])
                            else:
                                ts(P, "vector", ys[yb][:, hd * 64:(hd + 1) * 64], pO[ob][:, hh, 0:64], rec[rb][:, hh:hh + 1], None, ALU.mult, None,
                                   ["pO%d" % ob, "rec%d" % rb], ["ys%d" % yb])
                pend.append(pv)
            while pend:
                pend.pop(0)()
            dma(P, "gpq", y_nat[m * 128:(m + 1) * 128, :], ys[yb][:], ["ys%d" % yb], ["y_nat%d" % yb])
        P.emit()


def _phase_p3(nc, P, c, A):
    D, CH, CN, DFF, T, KD, KM, KF, KG, DC = c.D, c.CH, c.CN, c.DFF, c.T, c.KD, c.KM, c.KF, c.KG, c.DC
    MIXC = c.MIXC
    TT3 = 512
    NB = TT3 // 128
    WK = max(KM, KD, KG)
    x, y, y_hy, y_nat = A["x"], A["y"], A["y_hy"], A["y_nat"]
    Wb_out, Wb_up, Wb_down = A["Wb_out"], A["Wb_up"], A["Wb_down"]
    NSL = KF // KG
    FFS = KG * 128
    with contextlib.ExitStack() as st:
        S = lambda n, sh, dt: st.enter_context(nc.sbuf_tensor(n, sh, dt))
        PS = lambda n, sh, dt: st.enter_context(nc.psum_tensor(n, sh, dt))
        ident = S("ident", [128, 128], BF16)
        dma(P, "sync", ident[:], A["c_ident"], (), ["ident"])
        gfin = S("gfin", [128, D], F32)
        dma(P, "sync", gfin[:], bass.AP(tensor=A["norm_f_g"].tensor, offset=0, ap=[[0, 128], [1, D]]), (), ["gfin"])
        yin = [S("yin%d" % i, [128, MIXC], BF16) for i in range(2)]
        mixb = [S("mixb%d" % i, [128, MIXC], BF16) for i in range(2)]
        actT = S("actT", [128, KD, TT3], BF16)
        mixT = S("mixT", [128, KM, TT3], BF16)
        xres = [S("xres%d" % i, [128, D], F32) for i in range(NB)]
        mbb = [S("mbb%d" % i, [128, D], BF16) for i in range(2)]
        uT = [S("uT%d" % i, [128, KG, TT3], BF16) for i in range(2)]
        rl = [S("rl%d" % i, [128, TT3], F32) for i in range(2)]
        outb = [S("outb%d" % i, [128, D], F32) for i in range(2)]
        wbuf = [S("wbuf%d" % i, [128, WK, 512], BF16) for i in range(3)]
        stt_ = S("stats", [128, 24], F32)
        psA = [PS("psA%d" % i, [128, 4, 128], BF16) for i in range(2)]
        psM = [PS("psM%d" % i, [128, 512], F32) for i in range(6)]
        cnt = {}

        def nxt(k, m):
            v = cnt.get(k, 0)
            cnt[k] = v + 1
            return v % m

        def ev_eng():
            return "scalar" if nxt("ev", 2) else "vector"

        def rstd_of(src_ap, n, skeys, col, junk_ap, junk_key):
            k0, k1 = "st%d" % col, "st%d" % (col + 1)
            act(P, junk_ap, src_ap, AF.Square, skeys, [junk_key, k0], accum_out=stt_[:, col:col + 1])
            act(P, stt_[:, col + 1:col + 2], stt_[:, col:col + 1], AF.Sqrt, [k0], [k1], bias=EPS, scale=1.0 / n)
            P.op("vector", (lambda o_, i_: (lambda e: e.reciprocal(out=o_, in_=i_)))(stt_[:, col + 1:col + 2], stt_[:, col + 1:col + 2]),
                 [k1], [k1])
            return stt_[:, col + 1:col + 2], k1

        def transposes(src, skey, nk, b, dstT=None, dkey="actT"):
            if dstT is None:
                dstT = actT
            for k0 in range(0, nk, 4):
                pa = nxt("psA", 2)
                n_ = min(4, nk - k0)
                for kk in range(n_):
                    tr(P, psA[pa][:, kk, :], src[:, (k0 + kk) * 128:(k0 + kk + 1) * 128], ident[:], [skey, "ident"], ["psA%d" % pa])
                cp(P, ev_eng(), dstT[:, k0:k0 + n_, b * 128:(b + 1) * 128], psA[pa][:, 0:n_, :], ["psA%d" % pa], [dkey])

        def load_w(src3d, nk, ncol, skey):
            wb = nxt("wbuf", 3)
            dma(P, "sync", wbuf[wb][:, 0:nk, 0:ncol], src3d, [skey], ["wbuf%d" % wb])
            return wb

        def stepA1(ti, b):
            r0 = ti * TT3 + b * 128
            yb = b % 2
            dma(P, "sync", yin[yb][:, 0:CH], y_hy[r0:r0 + 128, :], (), ["yin%d" % yb])
            dma(P, "sync", yin[yb][:, CH:MIXC], y_nat[r0:r0 + 128, :], (), ["yin%d" % yb])
            r_h, kh_ = rstd_of(yin[yb][:, 0:CH], CH, ["yin%d" % yb], 12 + 4 * yb, mixb[yb][:, 0:CH], "mixb%d" % yb)
            r_n, kn_ = rstd_of(yin[yb][:, CH:MIXC], CN, ["yin%d" % yb], 14 + 4 * yb, mixb[yb][:, CH:MIXC], "mixb%d" % yb)
            ts(P, "vector", mixb[yb][:, 0:CH], yin[yb][:, 0:CH], r_h, None, ALU.mult, None, ["yin%d" % yb, kh_], ["mixb%d" % yb])
            act(P, mixb[yb][:, CH:MIXC], yin[yb][:, CH:MIXC], AF.Copy, ["yin%d" % yb, kn_], ["mixb%d" % yb], scale=r_n)

        def stepA2(ti, b):
            yb = b % 2
            transposes(mixb[yb], "mixb%d" % yb, KM, b, mixT, "mixT")

        for b in range(NB):
            stepA1(0, b)
            stepA2(0, b)
        NTL3 = T // TT3
        pre_wb = [None]
        for ti in range(NTL3):
            t0 = ti * TT3
            for b in range(NB):
                dma(P, "sync", xres[b][:], x[t0 + b * 128:t0 + (b + 1) * 128, :], (), ["xres%d" % b])
            for cc in range(D // DC):
                if cc == 0 and pre_wb[0] is not None:
                    wb = pre_wb[0]
                    pre_wb[0] = None
                else:
                    wb = load_w(Wb_out[:, cc * DC:(cc + 1) * DC].rearrange("(k p) c -> p k c", p=128), KM, DC, "Wb_out")
                for b in range(NB):
                    pm = nxt("psM", 6)
                    for k in range(KM):
                        mm(P, psM[pm][:, 0:DC], mixT[:, k, b * 128:(b + 1) * 128], wbuf[wb][:, k, 0:DC], k == 0, k == KM - 1,
                           ["mixT", "wbuf%d" % wb], ["psM%d" % pm])
                    sl = slice(cc * DC, (cc + 1) * DC)
                    tt(P, "vector", xres[b][:, sl], psM[pm][:, 0:DC], xres[b][:, sl], ALU.add, ["psM%d" % pm, "xres%d" % b], ["xres%d" % b])
            for b in range(NB):
                mbi = nxt("mbb", 2)
                r_m, km_ = rstd_of(xres[b][:], D, ["xres%d" % b], 4, mbb[mbi][:], "mbb%d" % mbi)
                if b % 2:
                    act(P, mbb[mbi][:], xres[b][:], AF.Copy, ["xres%d" % b, km_], ["mbb%d" % mbi], scale=r_m)
                else:
                    ts(P, "vector", mbb[mbi][:], xres[b][:], r_m, None, ALU.mult, None, ["xres%d" % b, km_], ["mbb%d" % mbi])
                transposes(mbb[mbi], "mbb%d" % mbi, KD, b)
            for s_ in range(NSL):
                ub = nxt("uT", 2)
                for sub in range(FFS // 512):
                    f0 = s_ * FFS + sub * 512
                    wb = load_w(Wb_up[:, f0:f0 + 512].rearrange("(k p) c -> p k c", p=128), KD, 512, "Wb_up")
                    for j in range(4):
                        pm = nxt("psM", 6)
                        for k in range(KD):
                            mm(P, psM[pm][:], wbuf[wb][:, k, j * 128:(j + 1) * 128], actT[:, k, :], k == 0, k == KD - 1,
                               ["actT", "wbuf%d" % wb], ["psM%d" % pm])
                        rb = nxt("rl", 2)
                        act(P, rl[rb][:], psM[pm][:], AF.Relu, ["psM%d" % pm], ["rl%d" % rb])
                        tt(P, "gpsimd", uT[ub][:, sub * 4 + j, :], rl[rb][:], rl[rb][:], ALU.mult, ["rl%d" % rb], ["uT%d_%d" % (ub, sub * 4 + j)])
                for cc in range(D // DC):
                    src = Wb_down[s_ * FFS:(s_ + 1) * FFS, cc * DC:(cc + 1) * DC].rearrange("(k p) c -> p k c", p=128)
                    wb = load_w(src, KG, DC, "Wb_down")
                    for b in range(NB):
                        pm = nxt("psM", 6)
                        for k in range(KG):
                            mm(P, psM[pm][:, 0:DC], uT[ub][:, k, b * 128:(b + 1) * 128], wbuf[wb][:, k, 0:DC], k == 0, k == KG - 1,
                               ["uT%d_%d" % (ub, k), "wbuf%d" % wb], ["psM%d" % pm])
                        sl = slice(cc * DC, (cc + 1) * DC)
                        tt(P, "vector", xres[b][:, sl], psM[pm][:, 0:DC], xres[b][:, sl], ALU.add, ["psM%d" % pm, "xres%d" % b], ["xres%d" % b])
                if ti + 1 < NTL3:
                    stp = [st_ for st_ in range(NB + 1) if (st_ * NSL) // (NB + 1) == s_] if NSL >= 2 else (list(range(NB + 1)) if s_ == 0 else [])
                    for st_ in stp:
                        if st_ >= 1:
                            stepA2(ti + 1, st_ - 1)
                        if st_ < NB:
                            stepA1(ti + 1, st_)
            if ti + 1 < NTL3:
                pre_wb[0] = load_w(Wb_out[:, 0:DC].rearrange("(k p) c -> p k c", p=128), KM, DC, "Wb_out")
            for b in range(NB):
                mbi = nxt("mbb", 2)
                r_f, kf_ = rstd_of(xres[b][:], D, ["xres%d" % b], 6 + 2 * (b % 2), mbb[mbi][:], "mbb%d" % mbi)
                ob_ = nxt("outb", 2)
                stt(P, "vector", outb[ob_][:], xres[b][:], r_f, gfin[:], ALU.mult, ALU.mult, ["xres%d" % b, kf_, "gfin"], ["outb%d" % ob_])
                dma(P, "gpq", y[t0 + b * 128:t0 + (b + 1) * 128, :], outb[ob_][:], ["outb%d" % ob_], ["y%d" % ob_])
        P.emit()


_CACHE = {}


def _consts(cfg, ctype):
    f32 = np.float32
    C = {}
    C["c_ident"] = np.eye(128, dtype=f32).astype(NPBF)
    ft = fft_tables(ctype)
    C["c_Enat"], C["c_Edat"], C["c_G"], C["c_W"] = ft["Enat"], ft["Edat"], ft["G"], ft["W"]
    fc = filter_consts(ctype)
    C["c_zT"], C["c_tcol"], C["c_mf"], C["c_mb"] = fc["zT"], fc["tcol"], fc["mf"], fc["mb"]
    maxd = math.log(1e-2) / 0.3
    mind = math.log(1e-2) / 1.5
    C["c_nad"] = (-np.abs(np.linspace(mind, maxd, cfg.CH, dtype=f32))).astype(f32)
    C["c_mask"] = nat_masks(ctype)
    C["c_flag"] = np.array([float(ctype)], f32)
    return C


def kernel(x_prompt, x_sample, norm_mix_g, w_in, hy_conv_w, hy_conv_b, hy_pe_w1, hy_pe_b1, hy_pe_w2, hy_pe_b2,
           hy_pe_w3, hy_pe_b3, hy_pe_freq, hy_pe_w4, hy_bias, nat_rpb, gnorm_hy, gnorm_nat, w_out, norm_mlp_g,
           w_up, w_down, norm_f_g):
    cfg = FULL
    f32 = np.float32
    A = lambda v: np.ascontiguousarray(np.asarray(v), dtype=f32)
    x_prompt, x_sample = A(x_prompt), A(x_sample)
    if "nc" not in _CACHE:
        _CACHE["nc"] = build_program(cfg)
        _CACHE["c"] = [_consts(cfg, 0), _consts(cfg, 1)]
    nc = _CACHE["nc"]
    idx, ok = nat_tt_index()
    rpb = A(nat_rpb)[0].reshape(cfg.NH, -1)
    ntt = np.where(ok[None], rpb[:, idx], f32(0.0)).astype(f32)
    shared = {
        "norm_mix_g": A(norm_mix_g)[0], "w_in": A(w_in)[0], "hy_conv_w": A(hy_conv_w)[0], "hy_conv_b": A(hy_conv_b)[0],
        "pe_w1": A(hy_pe_w1)[0], "pe_b1": A(hy_pe_b1)[0], "pe_w2": A(hy_pe_w2)[0], "pe_b2": A(hy_pe_b2)[0],
        "pe_w3": A(hy_pe_w3)[0], "pe_b3": A(hy_pe_b3)[0], "pe_freq": A(hy_pe_freq)[0], "pe_w4": A(hy_pe_w4)[0],
        "hy_bias": A(hy_bias)[0], "nat_tt": ntt,
        "gnorm": np.concatenate([A(gnorm_hy)[0], A(gnorm_nat)[0]]), "w_out": A(w_out)[0],
        "norm_mlp_g": A(norm_mlp_g)[0], "w_up": A(w_up)[0], "w_down": A(w_down)[0], "norm_f_g": A(norm_f_g),
    }
    in_maps = []
    for core in range(8):
        d = dict(shared)
        if core < 4:
            d["x"] = x_sample[core]
            d.update(_CACHE["c"][0])
        else:
            j = core - 4
            d["x"] = x_prompt[2 * j:2 * j + 2].reshape(cfg.T, cfg.D)
            d.update(_CACHE["c"][1])
        in_maps.append(d)
    res = run_bass_kernel_spmd(nc, in_maps, core_ids=list(range(8)))
    y_sample = np.stack([res.results[cidx]["y"] for cidx in range(4)], 0).astype(f32)
    y_prompt = np.concatenate([res.results[4 + j]["y"].reshape(2, 4096, cfg.D) for j in range(4)], 0).astype(f32)
    return (y_prompt, y_sample)
```

```python
import contextlib
import math
import numpy as np
import ml_dtypes
import concourse.bass as bass
import concourse.mybir as mybir
from concourse.bass_utils import run_bass_kernel_spmd

F32 = mybir.dt.float32
BF16 = mybir.dt.bfloat16
I32 = mybir.dt.int32
ALU = mybir.AluOpType
AF = mybir.ActivationFunctionType
NPBF = ml_dtypes.bfloat16

COMPUTE = ("tensor", "vector", "scalar", "gpsimd")
DMAQ = ("sync", "gpq")
NDMASEM = 8
NEG = -30000.0
EPS = 1e-5


class _Op:
    __slots__ = ("eng", "fn", "deps", "idx", "waited", "semslot", "semval")

    def __init__(self, eng, fn):
        self.eng = eng
        self.fn = fn
        self.deps = set()
        self.waited = False


class Prog:
    def __init__(self, nc, stack):
        self.nc = nc
        self.sems = {}
        for e in COMPUTE:
            self.sems[e] = stack.enter_context(nc.semaphore("s_" + e))
        for q in DMAQ:
            self.sems[q] = [stack.enter_context(nc.semaphore("s_%s%d" % (q, i))) for i in range(NDMASEM)]
        self.count = {e: 0 for e in COMPUTE}
        self.dcount = {q: [0] * NDMASEM for q in DMAQ}
        self.dnext = {q: 0 for q in DMAQ}
        self.reset_phase()

    def reset_phase(self):
        self.ops = []
        self.lastw = {}
        self.rd_c = {}
        self.rd_d = {}

    def op(self, eng, fn, reads=(), writes=()):
        o = _Op(eng, fn)
        o.idx = len(self.ops)
        deps = o.deps
        for k in reads:
            w = self.lastw.get(k)
            if w is not None:
                deps.add(w)
        for k in writes:
            w = self.lastw.get(k)
            if w is not None:
                deps.add(w)
            rc = self.rd_c.get(k)
            if rc:
                deps.update(rc.values())
            rd = self.rd_d.get(k)
            if rd:
                deps.update(rd)
        if eng in COMPUTE:
            for k in reads:
                self.rd_c.setdefault(k, {})[eng] = o.idx
        else:
            for k in reads:
                self.rd_d.setdefault(k, []).append(o.idx)
        for k in writes:
            self.lastw[k] = o.idx
            self.rd_c[k] = {}
            self.rd_d[k] = []
        deps.discard(o.idx)
        self.ops.append(o)
        return o

    def emit(self):
        nc = self.nc
        ops = self.ops
        phys = {"tensor": "tensor", "vector": "vector", "scalar": "scalar", "gpsimd": "gpsimd",
                "sync": "sync", "gpq": "gpsimd"}
        for o in ops:
            best = {}
            dl = []
            for d in o.deps:
                od = ops[d]
                if od.eng in COMPUTE:
                    if od.eng == "tensor" and o.eng == "tensor":
                        continue
                    if od.eng not in best or best[od.eng] < d:
                        best[od.eng] = d
                else:
                    dl.append(d)
            o.deps = set(best.values()) | set(dl)
            for d in o.deps:
                ops[d].waited = True
        lastc = {}
        for o in ops:
            if o.eng in COMPUTE:
                lastc[o.eng] = o
        for o in lastc.values():
            o.waited = True
        for o in ops:
            if o.eng in COMPUTE:
                if o.waited:
                    self.count[o.eng] += 1
                    o.semval = self.count[o.eng]
            else:
                slot = self.dnext[o.eng] % NDMASEM
                self.dnext[o.eng] += 1
                self.dcount[o.eng][slot] += 16
                o.semslot = slot
                o.semval = self.dcount[o.eng][slot]
        streams = {"tensor": [], "vector": [], "scalar": [], "gpsimd": [], "sync": []}
        for o in ops:
            streams[phys[o.eng]].append(o)
        sems = self.sems

        def run(e, lst):
            seen = {}
            for o in lst:
                waits = []
                if o.eng in DMAQ and o.semval > 16:
                    waits.append((sems[o.eng][o.semslot], o.semval - 16))
                for d in sorted(o.deps):
                    od = ops[d]
                    if od.eng in COMPUTE:
                        waits.append((sems[od.eng], od.semval))
                    else:
                        waits.append((sems[od.eng][od.semslot], od.semval))
                for (s, v) in waits:
                    key = id(s)
                    if seen.get(key, 0) >= v:
                        continue
                    seen[key] = v
                    e.wait_ge(s, v)
                ins = o.fn(e)
                if o.eng in COMPUTE:
                    if o.waited:
                        ins.then_inc(sems[o.eng], 1)
                else:
                    ins.then_inc(sems[o.eng][o.semslot], 16)
            for c in COMPUTE:
                if self.count[c] > seen.get(id(sems[c]), 0):
                    e.wait_ge(sems[c], self.count[c])
            for q in DMAQ:
                for i in range(NDMASEM):
                    if self.dcount[q][i] > seen.get(id(sems[q][i]), 0):
                        e.wait_ge(sems[q][i], self.dcount[q][i])

        with nc.Block() as block:
            @block.tensor
            def _(e):
                run(e, streams["tensor"])

            @block.vector
            def _(e):
                run(e, streams["vector"])

            @block.scalar
            def _(e):
                run(e, streams["scalar"])

            @block.gpsimd
            def _(e):
                run(e, streams["gpsimd"])

            @block.sync
            def _(e):
                run(e, streams["sync"])
        self.reset_phase()


def dma(P, q, out, in_, reads, writes, slow=False):
    if slow:
        return P.op(q, lambda e: e.dma_start(out=out, in_=in_, allow_slow_non_contiguous=True), reads, writes)
    return P.op(q, lambda e: e.dma_start(out=out, in_=in_), reads, writes)


def mm(P, out, lhsT, rhs, start, stop, reads, writes):
    return P.op("tensor", lambda e: e.matmul(out, lhsT=lhsT, rhs=rhs, start=start, stop=stop), reads, writes)


def tr(P, out, in_, ident, reads, writes):
    return P.op("tensor", lambda e: e.transpose(out, in_, ident), reads, writes)


def act(P, out, in_, func, reads, writes, bias=None, scale=None, accum_out=None):
    kw = {}
    if bias is not None:
        kw["bias"] = bias
    if scale is not None:
        kw["scale"] = scale
    if accum_out is not None:
        kw["accum_out"] = accum_out
    return P.op("scalar", lambda e: e.activation(out=out, in_=in_, func=func, **kw), reads, writes)


def ts(P, eng, out, in0, s1, s2, op0, op1, reads, writes):
    if s2 is None:
        return P.op(eng, lambda e: e.tensor_scalar(out=out, in0=in0, scalar1=s1, scalar2=None, op0=op0), reads, writes)
    return P.op(eng, lambda e: e.tensor_scalar(out=out, in0=in0, scalar1=s1, scalar2=s2, op0=op0, op1=op1), reads, writes)


def tt(P, eng, out, in0, in1, op, reads, writes):
    return P.op(eng, lambda e: e.tensor_tensor(out=out, in0=in0, in1=in1, op=op), reads, writes)


def stt(P, eng, out, in0, scalar, in1, op0, op1, reads, writes):
    eng = "vector"
    return P.op(eng, lambda e: e.scalar_tensor_tensor(out=out, in0=in0, scalar=scalar, in1=in1, op0=op0, op1=op1),
                reads, writes)


def cp(P, eng, out, in_, reads, writes):
    if eng == "scalar":
        return P.op(eng, lambda e: e.activation(out=out, in_=in_, func=AF.Copy), reads, writes)
    return P.op(eng, lambda e: e.tensor_copy(out=out, in_=in_), reads, writes)


def mset(P, eng, ap, val, writes):
    return P.op(eng, lambda e: e.memset(ap, val), (), writes)


class Cfg:
    def __init__(self, D=2048, CH=1024, NH=16, DFF=8192):
        self.D = D
        self.CH = CH
        self.NH = NH
        self.CN = NH * 64
        self.DFF = DFF
        self.T = 8192
        self.KD = D // 128
        self.NCOL = 3 * CH + 3 * self.CN
        self.CB = min(512, CH)
        self.NCB = CH // self.CB
        self.DC = min(512, D)
        self.KF = DFF // 128
        self.KG = min(16, self.KF)
        self.MIXC = CH + self.CN
        self.KM = self.MIXC // 128


FULL = Cfg()

NAT_TYPES = ["int", "m0", "m1", "m30", "m31", "m32", "m33", "m62", "m63"]


def nat_block(m):
    if m == 0:
        return "m0", 0, 8
    if m == 1:
        return "m1", 0, 8
    if m == 62:
        return "m62", 120, 8
    if m == 63:
        return "m63", 120, 8
    if m in (31, 32, 33):
        return "m%d" % m, 2 * m - 6, 14
    if m == 30:
        return "m30", 56, 9
    return "int", 2 * m - 4, 9


def _n1p(ctype):
    n1 = np.arange(64)
    if ctype == 0:
        return n1
    return np.where(n1 < 32, n1, n1 + 32)


def fft_tables(ctype):
    N = 16384
    n2 = np.arange(128)[:, None, None]
    k1 = (np.arange(64) + 0.5)[None, None, :]
    out = {}
    for name, pl in (("nat", _n1p(0)), ("dat", _n1p(ctype))):
        n1p = pl[None, :, None]
        th = 2.0 * np.pi * np.mod((128 * n1p + n2) * k1, N) / N
        e = np.stack([np.cos(th), -np.sin(th)], axis=1)
        out["E" + name] = np.ascontiguousarray(e.transpose(2, 0, 1, 3)).reshape(64, 128 * 128)
        if name == "dat":
            g = np.stack([np.cos(th) * 2.0 / N, -np.sin(th) * 2.0 / N], axis=1)
            out["G"] = np.ascontiguousarray(g.transpose(1, 3, 0, 2)).reshape(128, 128 * 64)
    a = np.arange(128)
    ph = 2.0 * np.pi * (np.outer(a, a) % 128) / 128.0
    wre, wim = np.cos(ph), -np.sin(ph)
    out["W"] = np.concatenate([wre, wim, -wim, -wre], axis=1)
    return {k: v.astype(NPBF) for k, v in out.items()}


def filter_consts(ctype):
    T = 8192
    L = T if ctype == 0 else 4096
    f32 = np.float32
    j = np.arange(T)
    valid = j < L
    jj = np.where(valid, j, 0)
    t = (jj.astype(np.float64) / (L - 1)).astype(f32)
    bands = 16
    w_ang = (f32(2.0 * math.pi / L) * jj.astype(f32)).astype(f32)
    f = np.linspace(1e-4, bands - 1, bands, dtype=f32)
    ang = (w_ang[:, None] * f[None, :]).astype(f32)
    z = np.concatenate([t[:, None], np.cos(ang), -np.sin(ang)], axis=-1).astype(f32)
    zT = np.ascontiguousarray(z.T)
    tcol = np.ascontiguousarray(t.reshape(64, 128))
    mf = valid.astype(f32).reshape(64, 128)
    mb = mf.copy()
    mb[0, 0] = 0.0
    return {"zT": zT, "tcol": tcol, "mf": np.ascontiguousarray(mf), "mb": np.ascontiguousarray(mb)}


def nat_masks(ctype):
    def window(i):
        if ctype == 0:
            return int(np.clip(i - 4, 0, 120))
        base = 0 if i < 64 else 64
        return base + int(np.clip(i - base - 4, 0, 56))
    cols = np.arange(64)
    cs = np.clip(cols - 8, 0, 48)
    colok = (cols[None, :] >= cs[:, None]) & (cols[None, :] < cs[:, None] + 16)
    reps = {"int": 10, "m0": 0, "m1": 1, "m30": 30, "m31": 31, "m32": 32, "m33": 33, "m62": 62, "m63": 63}
    out = np.full((len(NAT_TYPES), 128, 896), NEG, np.float32)
    for ti, tn in enumerate(NAT_TYPES):
        m = reps[tn]
        _, ks, nr = nat_block(m)
        for ri in range(2):
            i = 2 * m + ri
            rs = window(i)
            for a in range(nr):
                r = ks + a
                if rs <= r < rs + 8:
                    blk = np.where(colok, 0.0, NEG)
                    out[ti, ri * 64:(ri + 1) * 64, a * 64:(a + 1) * 64] = blk
    return out.astype(NPBF)


def nat_tt_index():
    p = np.arange(128)
    ri, j = p // 64, p % 64
    B = np.arange(15)
    kc = np.arange(64)
    ro = B[None, :, None] - ri[:, None, None]
    co = 15 + kc[None, None, :] - j[:, None, None]
    ok = (ro >= 0) & (ro <= 14) & (co >= 0) & (co <= 30)
    idx = np.clip(ro, 0, 14) * 31 + np.clip(co, 0, 30)
    return idx.reshape(128, 960), ok.reshape(128, 960)


def build_program(cfg, phases=("w", "p1", "hy", "nat", "p3"), debug=False):
    c = cfg
    D, CH, NH, CN, DFF, T, KD, NCOL, CB, NCB = c.D, c.CH, c.NH, c.CN, c.DFF, c.T, c.KD, c.NCOL, c.CB, c.NCB
    nc = bass.Bass("TRN2", target_bir_lowering=False)

    def din(name, shape, dt=F32):
        return nc.dram_tensor(name, list(shape), dt, kind="ExternalInput").ap()

    okind = "ExternalOutput" if debug else "Internal"

    def dscr(name, shape, dt=BF16):
        if debug and name in debug:
            return nc.dram_tensor(name, list(shape), dt, kind="ExternalOutput").ap()
        return nc.dram_tensor(name, list(shape), dt).ap()

    x = din("x", [T, D])
    norm_mix_g = din("norm_mix_g", [D])
    w_in = din("w_in", [D, NCOL])
    hy_conv_w = din("hy_conv_w", [3, 3 * CH])
    hy_conv_b = din("hy_conv_b", [3 * CH])
    pe_w1 = din("pe_w1", [33, 64])
    pe_b1 = din("pe_b1", [64])
    pe_w2 = din("pe_w2", [64, 64])
    pe_b2 = din("pe_b2", [64])
    pe_w3 = din("pe_w3", [64, 64])
    pe_b3 = din("pe_b3", [64])
    pe_freq = din("pe_freq", [64])
    pe_w4 = din("pe_w4", [64, 4 * CH])
    hy_bias = din("hy_bias", [2, CH])
    nat_tt = din("nat_tt", [NH, 128, 960])
    gnorm = din("gnorm", [c.MIXC])
    w_out = din("w_out", [c.MIXC, D])
    norm_mlp_g = din("norm_mlp_g", [D])
    w_up = din("w_up", [D, DFF])
    w_down = din("w_down", [DFF, D])
    norm_f_g = din("norm_f_g", [D])
    c_ident = din("c_ident", [128, 128], BF16)
    c_Enat = din("c_Enat", [64, 128 * 128], BF16)
    c_Edat = din("c_Edat", [64, 128 * 128], BF16)
    c_G = din("c_G", [128, 128 * 64], BF16)
    c_W = din("c_W", [128, 512], BF16)
    c_zT = din("c_zT", [33, T])
    c_tcol = din("c_tcol", [64, 128])
    c_mf = din("c_mf", [64, 128])
    c_mb = din("c_mb", [64, 128])
    c_nad = din("c_nad", [CH])
    c_mask = din("c_mask", [len(NAT_TYPES), 128, 896], BF16)
    c_flag = din("c_flag", [1])
    y = nc.dram_tensor("y", [T, D], F32, kind="ExternalOutput").ap()
    Wb_in = dscr("Wb_in", [D, NCOL])
    Wb_out = dscr("Wb_out", [c.MIXC, D])
    Wb_up = dscr("Wb_up", [D, DFF])
    Wb_down = dscr("Wb_down", [DFF, D])
    u_hy = dscr("u_hy", [T + 2, 3 * CH])
    qT = dscr("qT", [CN, T])
    kT = dscr("kT", [CN, T])
    vn1 = dscr("vn1", [T, NH, 65])
    S1 = dscr("S1", [4, 2, 64, 128, CB])
    S2 = dscr("S2", [2, 128, 64, CB])
    KH = dscr("KH", [2, NCB, 128, 64, 2, CB])
    z1d = dscr("z1d", [T, CH])
    y_hy = dscr("y_hy", [T, CH])
    y_nat = dscr("y_nat", [T, CN])

    with contextlib.ExitStack() as top:
        P = Prog(nc, top)
        if "w" in phases:
            _phase_w(nc, P, c, locals())
        if "p1" in phases:
            _phase_p1(nc, P, c, locals())
        if "hy" in phases:
            _phase_hy(nc, P, c, locals())
        if "nat" in phases:
            _phase_nat(nc, P, c, locals())
        if "p3" in phases:
            _phase_p3(nc, P, c, locals())
    return nc


def _colvec_load(P, q, tile, key, src, n):
    v = src.rearrange("(k p) -> p k", p=128)
    dma(P, q, tile[:, 0:n], v, (), [key], slow=True)


def _phase_w(nc, P, c, A):
    D, CH, CN, DFF, NCOL = c.D, c.CH, c.CN, c.DFF, c.NCOL
    with contextlib.ExitStack() as st:
        S = lambda n, sh, dt: st.enter_context(nc.sbuf_tensor("w_" + n, sh, dt))
        gm = S("gm", [128, c.KD], F32)
        gq = S("gq", [128, c.KD], F32)
        gl = S("gl", [128, c.KD], F32)
        gh = S("gh", [128, c.KM], F32)
        one = S("one", [128, 1], F32)
        _colvec_load(P, "sync", gm, "gm", A["norm_mix_g"], c.KD)
        _colvec_load(P, "sync", gl, "gl", A["norm_mlp_g"], c.KD)
        _colvec_load(P, "sync", gh, "gh", A["gnorm"], c.KM)
        mset(P, "vector", one[:], 1.0, ["one"])
        ts(P, "vector", gq[:], gm[:], 0.125, None, ALU.mult, None, ["gm"], ["gq"])
        CW = 2048
        NWB = 6
        stg = [S("wst%d" % i, [128, CW], F32) for i in range(NWB)]
        stb = [S("wsb%d" % i, [128, CW], BF16) for i in range(NWB)]
        jobs = []
        q0, q1 = 3 * CH, 3 * CH + CN
        for r in range(D // 128):
            for (c0, c1, sc, sk) in ((0, q0, gm, "gm"), (q0, q1, gq, "gq"), (q1, NCOL, gm, "gm")):
                for cc in range(c0, c1, CW):
                    jobs.append((A["w_in"], A["Wb_in"], r, cc, min(CW, c1 - cc), sc, sk, r))
        engs = ["scalar", "vector"]
        for i, (src, dst, r, cc, w, sc, sk, si) in enumerate(jobs):
            b = i % NWB
            dma(P, "sync", stg[b][:, 0:w], src[r * 128:(r + 1) * 128, cc:cc + w], (), ["wst%d" % b])
            eng = engs[i % 2]
            if eng == "scalar":
                act(P, stb[b][:, 0:w], stg[b][:, 0:w], AF.Copy, ["wst%d" % b, sk], ["wsb%d" % b], scale=sc[:, si:si + 1])
            else:
                ts(P, eng, stb[b][:, 0:w], stg[b][:, 0:w], sc[:, si:si + 1], None, ALU.mult, None,
                   ["wst%d" % b, sk], ["wsb%d" % b])
            dma(P, "gpq", dst[r * 128:(r + 1) * 128, cc:cc + w], stb[b][:, 0:w], ["wsb%d" % b], [dst.tensor.name])
        P.emit()


def _phase_p1(nc, P, c, A):
    D, CH, CN, NH, T, KD, NCOL = c.D, c.CH, c.CN, c.NH, c.T, c.KD, c.NCOL
    TT1 = 1024
    NB = TT1 // 128
    NCC = 3 * CH // 128
    WG = 512
    x, Wb_in, u_hy, qT, kT, vn1 = A["x"], A["Wb_in"], A["u_hy"], A["qT"], A["kT"], A["vn1"]
    with contextlib.ExitStack() as st:
        S = lambda n, sh, dt: st.enter_context(nc.sbuf_tensor("p1_" + n, sh, dt))
        PS = lambda n, sh, dt: st.enter_context(nc.psum_tensor("p1_" + n, sh, dt))
        ident = S("ident", [128, 128], BF16)
        dma(P, "sync", ident[:], A["c_ident"], (), ["ident"])
        cw = S("cw", [128, NCC, 3], F32)
        cb = S("cb", [128, NCC], F32)
        for k in range(3):
            dma(P, "sync", cw[:, :, k], A["hy_conv_w"][k].rearrange("(k p) -> p k", p=128), (), ["cw"], slow=True)
        _colvec_load(P, "sync", cb, "cb", A["hy_conv_b"], NCC)
        flag = S("flag", [128, 1], F32)
        dma(P, "sync", flag[:], bass.AP(tensor=A["c_flag"].tensor, offset=0, ap=[[0, 128], [1, 1]]), (), ["flag"])
        nfw = S("nfw", [128, NCC, 3], F32)
        ts(P, "vector", nfw[:].rearrange("p a b -> p (a b)"), cw[:].rearrange("p a b -> p (a b)"), flag[:, 0:1], -1.0,
           ALU.mult, ALU.mult, ["cw", "flag"], ["nfw"])
        carry = S("carry", [128, NCC, 2], F32)
        mset(P, "vector", carry[:].rearrange("p a b -> p (a b)"), 0.0, ["carry"])
        xs = [S("xs%d" % i, [128, D], F32) for i in range(2)]
        ab = [S("ab%d" % i, [128, D], BF16) for i in range(2)]
        junk = S("junk", [128, D], F32)
        ss = S("ss", [128, 2 * NB], F32)
        aT = [S("aT%d" % i, [128, KD, TT1], BF16) for i in range(2)]
        wbuf = [S("wbuf%d" % i, [128, KD, WG], BF16) for i in range(2)]
        R = [S("R%d" % i, [128, 516], F32) for i in range(2)]
        o1 = [S("o1_%d" % i, [128, 512], F32) for i in range(2)]
        ob = [S("ob%d" % i, [128, 512], BF16) for i in range(2)]
        hst = [S("hst%d" % i, [128, 4, 512], BF16) for i in range(2)]
        qst = [S("qst%d" % i, [128, TT1], BF16) for i in range(2)]
        vst = [S("vst%d" % i, [128, 8, 65], BF16) for i in range(2)]
        for i in range(2):
            mset(P, "vector", vst[i][:].rearrange("p a b -> p (a b)"), 1.0, ["vst%d" % i])
        zrow = S("zrow", [1, 3 * CH], BF16)
        mset(P, "vector", zrow[:], 0.0, ["zrow"])
        dma(P, "gpq", u_hy[0:1, :], zrow[:], ["zrow"], ["u_hy_pad"])
        dma(P, "gpq", u_hy[T + 1:T + 2, :], zrow[:], ["zrow"], ["u_hy_pad"])
        psA = [PS("psA%d" % i, [128, 4, 128], BF16) for i in range(2)]
        psM = [PS("psM%d" % i, [128, 512], F32) for i in range(4)]
        psH = [PS("psH%d" % i, [128, 8, 128], BF16) for i in range(2)]
        nwg = NCOL // WG
        wcount = [0]

        def load_w(g):
            b = wcount[0] % 2
            wcount[0] += 1
            src = Wb_in[:, g * WG:(g + 1) * WG].rearrange("(k p) c -> p k c", p=128)
            dma(P, "sync", wbuf[b][:], src, ["Wb_in"], ["wbuf%d" % b])
            return b

        evi = [0]

        def ev_eng():
            evi[0] += 1
            return "scalar" if evi[0] % 2 else "vector"

        mcount = [0]
        hcount = [0]
        NTL = T // TT1
        DFF = c.DFF
        bgl = S("bgl", [128, KD], F32)
        bgh = S("bgh", [128, c.KM], F32)
        bone = S("bone", [128, 1], F32)
        _colvec_load(P, "sync", bgl, "bgl", A["norm_mlp_g"], KD)
        _colvec_load(P, "sync", bgh, "bgh", A["gnorm"], c.KM)
        mset(P, "vector", bone[:], 1.0, ["bone"])
        BCW = 1024
        NBG = 4
        bst = [S("bst%d" % i, [128, BCW], F32) for i in range(NBG)]
        bsb = [S("bsb%d" % i, [128, BCW], BF16) for i in range(NBG)]
        bjobs = []
        for r in range(c.MIXC // 128):
            for cc in range(0, D, BCW):
                bjobs.append((A["w_out"], A["Wb_out"], r, cc, min(BCW, D - cc), bgh, "bgh", r))
        for r in range(D // 128):
            for cc in range(0, DFF, BCW):
                bjobs.append((A["w_up"], A["Wb_up"], r, cc, min(BCW, DFF - cc), bgl, "bgl", r))
        for r in range(DFF // 128):
            for cc in range(0, D, BCW):
                bjobs.append((A["w_down"], A["Wb_down"], r, cc, min(BCW, D - cc), bone, "bone", 0))
        bgi = [0]

        def bg(n):
            for _ in range(n):
                i = bgi[0]
                if i >= len(bjobs):
                    return
                bgi[0] += 1
                src, dst, r, cc, w, sc, sk, si = bjobs[i]
                b = i % NBG
                dma(P, "sync", bst[b][:, 0:w], src[r * 128:(r + 1) * 128, cc:cc + w], (), ["bst%d" % b])
                if i % 2:
                    act(P, bsb[b][:, 0:w], bst[b][:, 0:w], AF.Copy, ["bst%d" % b, sk], ["bsb%d" % b], scale=sc[:, si:si + 1])
                else:
                    ts(P, "vector", bsb[b][:, 0:w], bst[b][:, 0:w], sc[:, si:si + 1], None, ALU.mult, None, ["bst%d" % b, sk], ["bsb%d" % b])
                dma(P, "gpq", dst[r * 128:(r + 1) * 128, cc:cc + w], bsb[b][:, 0:w], ["bsb%d" % b], ["bg_" + dst.tensor.name])
        bg_per = -(-len(bjobs) // (NTL * (NCOL // WG)))

        def prepA(ti, b):
            t0 = ti * TT1
            xb = b % 2
            dma(P, "sync", xs[xb][:], x[t0 + b * 128:t0 + (b + 1) * 128, :], (), ["xs%d" % xb])
            act(P, junk[:], xs[xb][:], AF.Square, ["xs%d" % xb], ["junk", "ss%d" % b], accum_out=ss[:, b:b + 1])
            act(P, ss[:, NB + b:NB + b + 1], ss[:, b:b + 1], AF.Sqrt, ["ss%d" % b], ["sr%d" % b], bias=EPS, scale=1.0 / D)
            P.op("vector", (lambda o, i: (lambda e: e.reciprocal(out=o, in_=i)))(ss[:, NB + b:NB + b + 1], ss[:, NB + b:NB + b + 1]),
                 ["sr%d" % b], ["sr%d" % b])
            if b % 2:
                ts(P, "vector", ab[xb][:], xs[xb][:], ss[:, NB + b:NB + b + 1], None, ALU.mult, None,
                   ["xs%d" % xb, "sr%d" % b], ["ab%d" % xb])
            else:
                act(P, ab[xb][:], xs[xb][:], AF.Copy, ["xs%d" % xb, "sr%d" % b], ["ab%d" % xb], scale=ss[:, NB + b:NB + b + 1])

        def prepB(ti, b):
            xb = b % 2
            at = aT[ti % 2]
            for k0 in range(0, KD, 4):
                pa = (k0 // 4) % 2
                nk = min(4, KD - k0)
                for kk in range(nk):
                    tr(P, psA[pa][:, kk, :], ab[xb][:, (k0 + kk) * 128:(k0 + kk + 1) * 128], ident[:],
                       ["ab%d" % xb, "ident"], ["psA%d" % pa])
                cp(P, ev_eng(), at[:, k0:k0 + nk, b * 128:(b + 1) * 128], psA[pa][:, 0:nk, :], ["psA%d" % pa], ["aT%d" % (ti % 2)])

        for b in range(NB):
            prepA(0, b)
            prepB(0, b)
        pending = []

        def flush_pending():
            while pending:
                pending.pop(0)()

        for ti in range(NTL):
            t0 = ti * TT1
            at = aT[ti % 2]
            atk = "aT%d" % (ti % 2)
            sched = {}
            if ti + 1 < NTL:
                for st_ in range(NB + 1):
                    sched.setdefault((st_ * nwg) // (NB + 1), []).append(st_)
            for g in range(nwg):
                wb = load_w(g)
                wk = "wbuf%d" % wb
                col0 = g * WG
                bg(bg_per)
                for st_ in sched.get(g, ()):
                    if st_ >= 1:
                        prepB(ti + 1, st_ - 1)
                    if st_ < NB:
                        prepA(ti + 1, st_)
                if col0 < 3 * CH:
                    for half in range(TT1 // 512):
                        s = ti * (TT1 // 512) + half
                        hb = hcount[0] % 2
                        hcount[0] += 1
                        for j in range(WG // 128):
                            cc = col0 // 128 + j
                            pm = mcount[0] % 4
                            mcount[0] += 1
                            for k in range(KD):
                                mm(P, psM[pm][:], wbuf[wb][:, k, j * 128:(j + 1) * 128], at[:, k, half * 512:(half + 1) * 512],
                                   k == 0, k == KD - 1, [wk, atk], ["psM%d" % pm])
                            rb = cc % 2
                            rk = "R%d" % rb
                            cp(P, "scalar", R[rb][:, 2:514], psM[pm][:], ["psM%d" % pm], [rk])
                            cp(P, "gpsimd", R[rb][:, 0:2], carry[:, cc, :], ["carry%d" % cc], [rk])
                            act(P, o1[rb][:], R[rb][:, 0:512], AF.Identity, [rk, "cw", "cb"], ["o1_%d" % rb],
                                bias=cb[:, cc:cc + 1], scale=cw[:, cc, 0:1])
                            stt(P, "vector", o1[rb][:], R[rb][:, 1:513], cw[:, cc, 1:2], o1[rb][:], ALU.mult, ALU.add,
                                [rk, "cw", "o1_%d" % rb], ["o1_%d" % rb])
                            stt(P, "vector", ob[rb][:], R[rb][:, 2:514], cw[:, cc, 2:3], o1[rb][:], ALU.mult, ALU.add,
                                [rk, "cw", "o1_%d" % rb], ["ob%d" % rb])
                            cp(P, "gpsimd", carry[:, cc, :], R[rb][:, 512:514], [rk], ["carry%d" % cc])
                            if s == 8:
                                stt(P, "vector", ob[rb][:, 0:1], R[rb][:, 2:3], nfw[:, cc, 2:3], ob[rb][:, 0:1], ALU.mult, ALU.add,
                                    [rk, "nfw", "ob%d" % rb], ["ob%d" % rb])
                                stt(P, "vector", ob[rb][:, 1:2], R[rb][:, 1:2], nfw[:, cc, 0:1], ob[rb][:, 1:2], ALU.mult, ALU.add,
                                    [rk, "nfw", "ob%d" % rb], ["ob%d" % rb])
                            flush_pending()

                            def post(j=j, rb=rb, hb=hb, s=s, col0=col0):
                                for q in range(4):
                                    tr(P, psH[q // 2][:, (q % 2) * 4 + j, :], ob[rb][:, q * 128:(q + 1) * 128], ident[:], ["ob%d" % rb, "ident"],
                                       ["psH%d" % (q // 2)])
                                if j == WG // 128 - 1:
                                    for q in range(4):
                                        cp(P, ev_eng(), hst[hb][:, q, :], psH[q // 2][:, (q % 2) * 4:(q % 2) * 4 + 4, :].rearrange("p a b -> p (a b)"),
                                           ["psH%d" % (q // 2)], ["hst%d" % hb])
                                    r0 = 512 * s
                                    dst = u_hy[r0:r0 + 512, col0:col0 + WG].rearrange("(q p) c -> p q c", p=128)
                                    dma(P, "gpq", dst, hst[hb][:], ["hst%d" % hb], ["u_hy%d" % hb])
                            pending.append(post)
                elif col0 < 3 * CH + 2 * CN:
                    dstT = qT if col0 < 3 * CH + CN else kT
                    cbase = col0 - (3 * CH if col0 < 3 * CH + CN else 3 * CH + CN)
                    for j in range(WG // 128):
                        qb = j % 2
                        for half in range(TT1 // 512):
                            pm = mcount[0] % 4
                            mcount[0] += 1
                            for k in range(KD):
                                mm(P, psM[pm][:], wbuf[wb][:, k, j * 128:(j + 1) * 128], at[:, k, half * 512:(half + 1) * 512],
                                   k == 0, k == KD - 1, [wk, atk], ["psM%d" % pm])
                            flush_pending()
                            cp(P, ev_eng(), qst[qb][:, half * 512:(half + 1) * 512], psM[pm][:], ["psM%d" % pm], ["qst%d" % qb])
                        dma(P, "gpq", dstT[cbase + j * 128:cbase + (j + 1) * 128, t0:t0 + TT1], qst[qb][:], ["qst%d" % qb], ["qkT%d" % qb])
                else:
                    h0 = (col0 - 3 * CH - 2 * CN) // 64
                    for b in range(NB):
                        pm = mcount[0] % 4
                        mcount[0] += 1
                        vb = b % 2
                        for k in range(KD):
                            mm(P, psM[pm][:], at[:, k, b * 128:(b + 1) * 128], wbuf[wb][:, k, :], k == 0, k == KD - 1,
                               [wk, atk], ["psM%d" % pm])
                        cp(P, ev_eng(), vst[vb][:, :, 0:64], psM[pm][:].rearrange("p (h d) -> p h d", d=64), ["psM%d" % pm], ["vst%d" % vb])
                        dma(P, "gpq", vn1[t0 + b * 128:t0 + (b + 1) * 128, h0:h0 + 8, :], vst[vb][:], ["vst%d" % vb], ["vn1_%d" % vb])
        flush_pending()
        bg(len(bjobs))
        fl = S("fl", [128, NCC, 2], F32)
        flb = S("flb", [128, NCC], BF16)
        frow = S("frow", [1, NCC * 128], BF16)
        keys = ["carry%d" % i for i in range(NCC)]
        tt(P, "vector", fl[:], carry[:], cw[:, :, 0:2], ALU.mult, keys + ["cw", "fl"], ["fl"])
        tt(P, "vector", fl[:, :, 0], fl[:, :, 0], fl[:, :, 1], ALU.add, ["fl"], ["fl"])
        tt(P, "vector", flb[:], fl[:, :, 0], cb[:], ALU.add, ["fl", "cb"], ["flb"])
        for c0 in range(0, NCC, 4):
            pa = (c0 // 4) % 2
            for kk in range(4):
                tr(P, psA[pa][0:1, kk, :], flb[:, c0 + kk:c0 + kk + 1], ident[:], ["flb", "ident"], ["psA%d" % pa])
            cp(P, "vector", frow[:, c0 * 128:(c0 + 4) * 128], psA[pa][0:1, :, :].rearrange("p a b -> p (a b)"), ["psA%d" % pa], ["frow"])
        dma(P, "gpq", u_hy[T:T + 1, :], frow[:], ["frow"], ["u_hy"])
        P.emit()


def _phase_hy(nc, P, c, A):
    CH, T, CB, NCB = c.CH, c.T, c.CB, c.NCB
    u_hy, S1, S2, KH, y_hy, z1d = A["u_hy"], A["S1"], A["S2"], A["KH"], A["y_hy"], A["z1d"]
    NG = 4
    KGRP = 2
    LOOK = 2
    with contextlib.ExitStack() as st0:
        S0 = lambda n, sh, dt: st0.enter_context(nc.sbuf_tensor("hy_" + n, sh, dt))
        h3b = S0("h3b", [64, T], BF16)
        w4b = S0("w4b", [64, 4 * CH], BF16)
        with contextlib.ExitStack() as st:
            S = lambda n, sh, dt: st.enter_context(nc.sbuf_tensor("hy_" + n, sh, dt))
            PS = lambda n, sh, dt: st.enter_context(nc.psum_tensor("hy_" + n, sh, dt))
            zT = S("zT", [33, T], F32)
            dma(P, "sync", zT[:], A["c_zT"], (), ["zT"])
            w1 = S("w1", [33, 64], F32)
            w2 = S("w2", [64, 64], F32)
            w3 = S("w3", [64, 64], F32)
            dma(P, "sync", w1[:], A["pe_w1"], (), ["w1"])
            dma(P, "sync", w2[:], A["pe_w2"], (), ["w2"])
            dma(P, "sync", w3[:], A["pe_w3"], (), ["w3"])
            pv = S("pv", [64, 8], F32)
            for i, nm in enumerate(["pe_freq", "pe_b1", "pe_b2", "pe_b3"]):
                dma(P, "sync", pv[:, i:i + 1], A[nm].rearrange("(p o) -> p o", o=1), (), ["pv"], slow=True)
            ts(P, "vector", pv[:, 4:5], pv[:, 0:1], 1.0 / (2 * math.pi), None, ALU.mult, None, ["pv"], ["pv"])
            for i in range(3):
                tt(P, "vector", pv[:, 5 + i:6 + i], pv[:, 4:5], pv[:, 1 + i:2 + i], ALU.mult, ["pv"], ["pv"])
            hA = S("hA", [64, T], F32)
            hB = S("hB", [64, T], F32)
            w4f = S("w4f", [64, 4 * CH], F32)
            dma(P, "sync", w4f[:], A["pe_w4"], (), ["w4f"])
            cp(P, "vector", w4b[:], w4f[:], ["w4f"], ["w4b"])
            uu = [S("uu%d" % i, [64, 512], F32) for i in range(2)]
            ui = [S("ui%d" % i, [64, 512], I32) for i in range(2)]
            uf = [S("uf%d" % i, [64, 512], F32) for i in range(2)]
            psF = [PS("psF%d" % i, [64, 512], F32) for i in range(2)]
            layers = [(w1, "w1", zT, hA, 33), (w2, "w2", hA, hB, 64), (w3, "w3", hB, None, 64)]
            n = 0
            for li, (wl, wk, src, dst, K) in enumerate(layers):
                for ch in range(T // 512):
                    b = n % 2
                    n += 1
                    sl = slice(ch * 512, (ch + 1) * 512)
                    srck = "zT" if li == 0 else "h%d_%d" % (li - 1, ch)
                    mm(P, psF[b][:], wl[0:K, :], src[0:K, sl], True, True, [wk, srck], ["psF%d" % b])
                    ts(P, "vector", uu[b][:], psF[b][:], pv[:, 4:5], pv[:, 5 + li:6 + li], ALU.mult, ALU.add,
                       ["psF%d" % b, "pv"], ["uu%d" % b])
                    cp(P, "vector", ui[b][:], uu[b][:], ["uu%d" % b], ["ui%d" % b])
                    cp(P, "gpsimd", uf[b][:], ui[b][:], ["ui%d" % b], ["uf%d" % b])
                    tt(P, "gpsimd", uu[b][:], uu[b][:], uf[b][:], ALU.subtract, ["uu%d" % b, "uf%d" % b], ["uu%d" % b])
                    outap = h3b[:, sl] if li == 2 else dst[:, sl]
                    act(P, outap, uu[b][:], AF.Sin, ["uu%d" % b], ["h%d_%d" % (li, ch)], scale=2 * math.pi)
            P.emit()

        with contextlib.ExitStack() as st:
            S = lambda n, sh, dt: st.enter_context(nc.sbuf_tensor("hy_" + n, sh, dt))
            PS = lambda n, sh, dt: st.enter_context(nc.psum_tensor("hy_" + n, sh, dt))
            W = S("W", [128, 4, 128], BF16)
            dma(P, "sync", W[:].rearrange("p a b -> p (a b)"), A["c_W"], (), ["W"])
            Wre, Wim, nWim, nWre = W[:, 0, :], W[:, 1, :], W[:, 2, :], W[:, 3, :]
            tcol = S("tcol", [64, 128], F32)
            mf = S("mf", [64, 128], F32)
            mb = S("mb", [64, 128], F32)
            dma(P, "sync", tcol[:], A["c_tcol"], (), ["tcol"])
            dma(P, "sync", mf[:], A["c_mf"], (), ["mf"])
            dma(P, "sync", mb[:], A["c_mb"], (), ["mb"])
            nad = S("nad", [64, CH], F32)
            dma(P, "sync", nad[:], bass.AP(tensor=A["c_nad"].tensor, offset=0, ap=[[0, 64], [1, CH]]), (), ["nad"])
            hbias = S("hbias", [64, 2, CH], F32)
            dma(P, "sync", hbias[:], bass.AP(tensor=A["hy_bias"].tensor, offset=0, ap=[[0, 64], [CH, 2], [1, CH]]), (), ["hbias"])
            Eg = [S("Eg%d" % i, [64, NG, 128], BF16) for i in range(2)]
            Gg = [S("Gg%d" % i, [128, NG, 64], BF16) for i in range(2)]
            xin = [S("xin%d" % i, [64, NG, CB], BF16) for i in range(2)]
            x1s = [S("x1s%d" % i, [128, NG, CB], BF16) for i in range(2)]
            dec = [S("dec%d" % i, [64, CB], F32) for i in range(3)]
            hf = [S("hf%d" % i, [64, CB], BF16) for i in range(4)]
            xb1 = [S("xb1_%d" % i, [128, 2, KGRP, CB], BF16) for i in range(2)]
            xb2 = [S("xb2_%d" % i, [128, 2, KGRP, CB], BF16) for i in range(2)]
            khs = [S("khs%d" % i, [128, KGRP, 2, CB], BF16) for i in range(2)]
            yh = [S("yh%d" % i, [128, 2, CB], BF16) for i in range(3)]
            tmp = [S("tmp%d" % i, [128, CB], F32) for i in range(8)]
            vs = [S("vs%d" % i, [128, 2, KGRP, CB], BF16) for i in range(2)]
            vin = [S("vin%d" % i, [128, NG, CB], BF16) for i in range(2)]
            gv = [S("gv%d" % i, [64, NG, CB], BF16) for i in range(2)]
            gx = [S("gx%d" % i, [64, NG, CB], BF16) for i in range(2)]
            vbz = [S("vbz%d" % i, [64, CB], F32) for i in range(2)]
            zt = [S("zt%d" % i, [64, CB], F32) for i in range(2)]
            zo = [S("zo%d" % i, [64, NG, CB], BF16) for i in range(2)]
            pB = [PS("pB%d" % i, [128, CB], F32) for i in range(4)]
            pV = [PS("pV%d" % i, [128, CB], F32) for i in range(4)]
            cnt = {}

            def nxt(k, m):
                v = cnt.get(k, 0)
                cnt[k] = v + 1
                return v % m

            def ev_eng():
                return "scalar" if nxt("ev", 2) else "vector"

            Enat_v = A["c_Enat"].rearrange("p (a b) -> p a b", a=128, b=128)
            Edat_v = A["c_Edat"].rearrange("p (a b) -> p a b", a=128, b=128)
            G_v = A["c_G"].rearrange("p (a b) -> p a b", a=128, b=64)
            h3v = h3b[:].rearrange("p (a b) -> p b a", b=128)

            def tokview(ap2d):
                return ap2d.rearrange("(a b) c -> a b c", b=128)

            def stageA(slot, Ev, get_rhs):
                items = [(gi, j) for gi in range(128 // NG) for j in range(NG)]
                ebs, sbs = {}, {}
                queue = []
                for it in items + [None] * LOOK:
                    cur = None
                    if it is not None:
                        gi, j = it
                        if j == 0:
                            ebs[gi] = nxt("eg", 2)
                            dma(P, "sync", Eg[ebs[gi]][:], Ev[:, gi * NG:(gi + 1) * NG], (), ["Eg%d" % ebs[gi]])
                        rhs, rkeys = get_rhs(gi, j)
                        cur = (gi, j, rhs, rkeys)
                        queue.append(cur)
                    if len(queue) > LOOK or (it is None and queue):
                        gi, j, rhs, rkeys = queue.pop(0)
                        eb = ebs[gi]
                        if j == 0:
                            sbs[gi] = nxt("x1s", 2)
                        sb = sbs[gi]
                        p = nxt("pB", 4)
                        mm(P, pB[p][:], Eg[eb][:, j, :], rhs, True, True, ["Eg%d" % eb] + rkeys, ["pB%d" % p])
                        cp(P, "scalar" if nxt("evA", 2) else "vector", x1s[sb][:, j, :], pB[p][:], ["pB%d" % p], ["x1s%d_%d" % (sb, j)])
                        if j == NG - 1:
                            dst = S1[slot].rearrange("r k n c -> (r k) n c")[:, gi * NG:(gi + 1) * NG, :]
                            dma(P, "gpq", dst, x1s[sb][:], ["x1s%d_%d" % (sb, jj) for jj in range(NG)], ["S1_%d_%d" % (slot, gi % 2)])

            def data_rhs(src2d, skey):
                tv = tokview(src2d)
                state = {}

                def get(gi, j):
                    if j == 0:
                        b = nxt("xin", 2)
                        state["b"] = b
                        dma(P, "sync", xin[b][:], tv[:, gi * NG:(gi + 1) * NG, :], [skey], ["xin%d" % b])
                    b = state["b"]
                    return xin[b][:, j, :], ["xin%d" % b]
                return get

            def filt_rhs(od, c0):
                o, dr = od // 2, od % 2
                wcol = o * 2 * CH + dr * CH + c0
                msk, mk = (mf, "mf") if dr == 0 else (mb, "mb")

                def get(gi, j):
                    n2 = gi * NG + j
                    d = nxt("dec", 3)
                    act(P, dec[d][:], nad[:, c0:c0 + CB], AF.Exp, ["nad", "tcol"], ["dec%d" % d], scale=tcol[:, n2:n2 + 1])
                    p = nxt("pV", 4)
                    mm(P, pV[p][0:64, :], h3v[:, n2, :], w4b[:, wcol:wcol + CB], True, True, ["h3b", "w4b"], ["pV%d" % p])
                    hb_ = nxt("hf", 4)
                    stt(P, "vector", hf[hb_][:], pV[p][0:64, :], msk[:, n2:n2 + 1], dec[d][:], ALU.mult, ALU.mult,
                        ["pV%d" % p, mk, "dec%d" % d], ["hf%d" % hb_])
                    return hf[hb_][:], ["hf%d" % hb_]
                return get

            def load_xb(buf, bk, slot, k1a, k1b):
                for ri in range(2):
                    src = S1[slot, ri, k1a:k1b, :, :].rearrange("k n c -> n k c")
                    dma(P, "sync", buf[:, ri, 0:k1b - k1a, :], src, ["S1_%d_0" % slot, "S1_%d_1" % slot], [bk])

            def stageB_filter(o, cbi):
                for k1a in range(0, 64, KGRP):
                    k1b = min(64, k1a + KGRP)
                    b = nxt("xb", 2)
                    load_xb(xb1[b], "xb1_%d" % b, 2 * o, k1a, k1b)
                    load_xb(xb2[b], "xb2_%d" % b, 2 * o + 1, k1a, k1b)
                    kb = nxt("khs", 2)
                    rk = ["xb1_%d" % b, "xb2_%d" % b, "W"]
                    for kk in range(k1b - k1a):
                        fr, fi = xb1[b][:, 0, kk, :], xb1[b][:, 1, kk, :]
                        br, bi = xb2[b][:, 0, kk, :], xb2[b][:, 1, kk, :]
                        pr = nxt("pB", 4)
                        for n_, (w_, x_) in enumerate(((Wre, fr), (nWim, fi), (Wre, br), (nWim, bi))):
                            mm(P, pB[pr][:], w_, x_, n_ == 0, n_ == 3, rk, ["pB%d" % pr])
                        cp(P, ev_eng(), khs[kb][:, kk, 0, :], pB[pr][:], ["pB%d" % pr], ["khs%d" % kb])
                        pi = nxt("pB", 4)
                        for n_, (w_, x_) in enumerate(((Wim, fr), (Wre, fi), (nWim, br), (nWre, bi))):
                            mm(P, pB[pi][:], w_, x_, n_ == 0, n_ == 3, rk, ["pB%d" % pi])
                        cp(P, ev_eng(), khs[kb][:, kk, 1, :], pB[pi][:], ["pB%d" % pi], ["khs%d" % kb])
                    dma(P, "gpq", KH[o, cbi, :, k1a:k1b, :, :], khs[kb][:, 0:k1b - k1a, :, :], ["khs%d" % kb], ["KH%d" % o])

            def stageB_conv(slot, o, cbi):
                items = [(k1a, kk) for k1a in range(0, 64, KGRP) for kk in range(min(64, k1a + KGRP) - k1a)]
                grp = {}
                queue = []
                for it in items + [None] * LOOK:
                    cur = None
                    if it is not None:
                        k1a, kk = it
                        k1b = min(64, k1a + KGRP)
                        if kk == 0:
                            b = nxt("xb", 2)
                            load_xb(xb1[b], "xb1_%d" % b, slot, k1a, k1b)
                            kb = nxt("khs", 2)
                            dma(P, "sync", khs[kb][:, 0:k1b - k1a, :, :], KH[o, cbi, :, k1a:k1b, :, :], ["KH%d" % o], ["khs%d" % kb])
                            grp[k1a] = [b, kb, None]
                        b, kb, _ = grp[k1a]
                        rk = ["xb1_%d" % b, "W"]
                        xr, xi = xb1[b][:, 0, kk, :], xb1[b][:, 1, kk, :]
                        kr, ki = khs[kb][:, kk, 0, :], khs[kb][:, kk, 1, :]
                        pr = nxt("pB", 4)
                        mm(P, pB[pr][:], Wre, xr, True, False, rk, ["pB%d" % pr])
                        mm(P, pB[pr][:], nWim, xi, False, True, rk, ["pB%d" % pr])
                        pi = nxt("pB", 4)
                        mm(P, pB[pi][:], Wim, xr, True, False, rk, ["pB%d" % pi])
                        mm(P, pB[pi][:], Wre, xi, False, True, rk, ["pB%d" % pi])
                        t = [nxt("tmp", 8) for _ in range(4)]
                        kk_ = ["khs%d" % kb]
                        tt(P, "vector", tmp[t[0]][:], pB[pr][:], kr, ALU.mult, ["pB%d" % pr] + kk_, ["tmp%d" % t[0]])
                        tt(P, "vector", tmp[t[1]][:], pB[pi][:], ki, ALU.mult, ["pB%d" % pi] + kk_, ["tmp%d" % t[1]])
                        tt(P, "vector", tmp[t[2]][:], pB[pr][:], ki, ALU.mult, ["pB%d" % pr] + kk_, ["tmp%d" % t[2]])
                        tt(P, "vector", tmp[t[3]][:], pB[pi][:], kr, ALU.mult, ["pB%d" % pi] + kk_, ["tmp%d" % t[3]])
                        yb = nxt("yh", 3)
                        tt(P, "gpsimd", yh[yb][:, 0, :], tmp[t[0]][:], tmp[t[1]][:], ALU.subtract, ["tmp%d" % t[0], "tmp%d" % t[1]], ["yh%d" % yb])
                        tt(P, "gpsimd", yh[yb][:, 1, :], tmp[t[2]][:], tmp[t[3]][:], ALU.add, ["tmp%d" % t[2], "tmp%d" % t[3]], ["yh%d" % yb])
                        cur = (k1a, kk, yb)
                        queue.append(cur)
                    if len(queue) > LOOK or (it is None and queue):
                        k1a, kk, yb = queue.pop(0)
                        k1b = min(64, k1a + KGRP)
                        if kk == 0:
                            grp[k1a][2] = nxt("vs", 2)
                        vb = grp[k1a][2]
                        yr, yi = yh[yb][:, 0, :], yh[yb][:, 1, :]
                        yk = ["yh%d" % yb, "W"]
                        vr = nxt("pV", 4)
                        mm(P, pV[vr][:], Wre, yr, True, False, yk, ["pV%d" % vr])
                        mm(P, pV[vr][:], Wim, yi, False, True, yk, ["pV%d" % vr])
                        cp(P, "scalar", vs[vb][:, 0, kk, :], pV[vr][:], ["pV%d" % vr], ["vs%d" % vb])
                        vi = nxt("pV", 4)
                        mm(P, pV[vi][:], nWim, yr, True, False, yk, ["pV%d" % vi])
                        mm(P, pV[vi][:], Wre, yi, False, True, yk, ["pV%d" % vi])
                        cp(P, "scalar", vs[vb][:, 1, kk, :], pV[vi][:], ["pV%d" % vi], ["vs%d" % vb])
                        if kk == k1b - k1a - 1:
                            for ri in range(2):
                                dma(P, "gpq", S2[ri, :, k1a:k1b, :], vs[vb][:, ri, 0:k1b - k1a, :], ["vs%d" % vb], ["S2_%d" % ri])

            def stageAp(o, cbi):
                c0 = cbi * CB
                gate2d = u_hy[1:T + 1, (1 + o) * CH + c0:(1 + o) * CH + c0 + CB]
                zsrc2d, zkey = (u_hy[1:T + 1, c0:c0 + CB], "u_hy") if o == 0 else (z1d[:, c0:c0 + CB], "z1d")
                dst2d, dkey = (z1d[:, c0:c0 + CB], "z1d") if o == 0 else (y_hy[:, c0:c0 + CB], "y_hy")
                gview, zview, dview = tokview(gate2d), tokview(zsrc2d), tokview(dst2d)
                def ap_loads(gi):
                    gb = nxt("gg", 2)
                    dma(P, "sync", Gg[gb][:], G_v[:, gi * NG:(gi + 1) * NG], (), ["Gg%d" % gb])
                    vb = nxt("vin", 2)
                    for ri in range(2):
                        src = S2[ri, gi * NG:(gi + 1) * NG, :, :].rearrange("n k c -> k n c")
                        dma(P, "sync", vin[vb][ri * 64:(ri + 1) * 64, :, :], src, ["S2_%d" % ri], ["vin%d" % vb])
                    xb_ = nxt("gx", 2)
                    dma(P, "sync", gx[xb_][:], gview[:, gi * NG:(gi + 1) * NG, :], ["u_hy"], ["gx%d" % xb_])
                    dma(P, "sync", gv[xb_][:], zview[:, gi * NG:(gi + 1) * NG, :], [zkey], ["gv%d" % xb_])
                    return gb, vb, xb_

                nxt_ld = ap_loads(0)
                for gi in range(128 // NG):
                    gb, vb, xb_ = nxt_ld
                    if gi + 1 < 128 // NG:
                        nxt_ld = ap_loads(gi + 1)
                    ob_ = nxt("zo", 2)
                    for j in range(NG):
                        p = nxt("pV", 4)
                        mm(P, pV[p][0:64, :], Gg[gb][:, j, :], vin[vb][:, j, :], True, True, ["Gg%d" % gb, "vin%d" % vb], ["pV%d" % p])
                        q = nxt("vbz", 2)
                        tt(P, "gpsimd", vbz[q][:], gv[xb_][:, j, :], hbias[:, o, c0:c0 + CB], ALU.mult, ["gv%d" % xb_, "hbias"], ["vbz%d" % q])
                        tt(P, "vector", zt[q][:], pV[p][0:64, :], vbz[q][:], ALU.add, ["pV%d" % p, "vbz%d" % q], ["zt%d" % q])
                        tt(P, "vector", zo[ob_][:, j, :], zt[q][:], gx[xb_][:, j, :], ALU.mult, ["zt%d" % q, "gx%d" % xb_], ["zo%d" % ob_])
                    dma(P, "gpq", dview[:, gi * NG:(gi + 1) * NG, :], zo[ob_][:], ["zo%d" % ob_], [dkey])

            for cbi in range(NCB):
                c0 = cbi * CB
                for od in range(4):
                    stageA(od, Enat_v, filt_rhs(od, c0))
                for o in range(2):
                    stageB_filter(o, cbi)
                stageA(0, Edat_v, data_rhs(u_hy[1:T + 1, c0:c0 + CB], "u_hy"))
                stageB_conv(0, 0, cbi)
                stageAp(0, cbi)
                stageA(1, Edat_v, data_rhs(z1d[:, c0:c0 + CB], "z1d"))
                stageB_conv(1, 1, cbi)
                stageAp(1, cbi)
            P.emit()


def _phase_nat(nc, P, c, A):
    NH, CN, T = c.NH, c.CN, c.T
    NHP = NH // 2
    qT, kT, vn1, y_nat = A["qT"], A["kT"], A["vn1"], A["y_nat"]
    NTY = len(NAT_TYPES)
    with contextlib.ExitStack() as st:
        S = lambda n, sh, dt: st.enter_context(nc.sbuf_tensor("nat_" + n, sh, dt))
        PS = lambda n, sh, dt: st.enter_context(nc.psum_tensor("nat_" + n, sh, dt))
        ident = S("ident", [128, 128], BF16)
        dma(P, "sync", ident[:], A["c_ident"], (), ["ident"])
        msk = S("msk", [128, NTY, 896], BF16)
        dma(P, "sync", msk[:], A["c_mask"].rearrange("t p c -> p t c"), (), ["msk"])
        TT = S("TT", [128, NH, 960], BF16)
        TMint = S("TMint", [128, NH, 576], BF16)
        tst = [S("tst%d" % i, [128, 960], F32) for i in range(2)]
        for h in range(NH):
            b = h % 2
            dma(P, "sync", tst[b][:], A["nat_tt"][h], (), ["tst%d" % b])
            cp(P, "vector", TT[:, h, :], tst[b][:], ["tst%d" % b], ["TT%d" % h])
            tt(P, "gpsimd", TMint[:, h, :], TT[:, h, 192:768], msk[:, 0, 0:576], ALU.add, ["TT%d" % h, "msk"], ["TMint%d" % h])
        TMT = S("TMT", [128, NH, 640], BF16)
        ssb = [S("ssb%d" % i, [128, 640], F32) for i in range(3)]
        qt = [S("qt%d" % i, [128, NHP, 512], BF16) for i in range(2)]
        kt = [S("kt%d" % i, [128, NHP, 896], BF16) for i in range(2)]
        v1 = [S("v1_%d" % i, [128, 7, NH, 65], BF16) for i in range(2)]
        tmsp = [S("tmsp%d" % i, [128, 896], BF16) for i in range(2)]
        pt = [S("pt%d" % i, [128, 1024], BF16) for i in range(3)]
        ys = [S("ys%d" % i, [128, CN], BF16) for i in range(2)]
        rec = [S("rec%d" % i, [128, 4], F32) for i in range(2)]
        stS = contextlib.ExitStack()
        pSb = [stS.enter_context(nc.psum_tensor("nat_pSb0", [128, 1024], BF16))] * 2
        cnt = {}

        def nxt(k, m):
            v = cnt.get(k, 0)
            cnt[k] = v + 1
            return v % m

        qTv = qT.rearrange("(c p) t -> p c t", p=128)
        kTv = kT.rearrange("(c p) t -> p c t", p=128)
        mset(P, "vector", TMT[:].rearrange("p a b -> p (a b)"), 0.0, ["TMTall"])
        for h in range(NH):
            for kb in range(5):
                kp = 128 if kb < 4 else 64
                sb = 0
                tr(P, pSb[sb][0:kp, kb * 128:(kb + 1) * 128], TMint[:, h, kb * 128:kb * 128 + kp], ident[:], ["TMint%d" % h, "ident"], ["pSb%d" % sb])
            cp(P, "vector" if h % 2 else "scalar", TMT[:, h, 0:512], pSb[0][:, 0:512], ["pSb0"], ["TMT%d" % h, "TMTall"])
            cp(P, "vector" if h % 2 else "scalar", TMT[0:64, h, 512:640], pSb[0][0:64, 512:640], ["pSb0"], ["TMT%d" % h, "TMTall"])
        P.emit()
        stS.close()
        pS = [PS("pS%d" % i, [128, 1024], F32) for i in range(3)]
        pO = [PS("pO%d" % i, [128, 4, 65], F32) for i in range(2)]
        for m in range(T // 128):
            tname, ks, nr = nat_block(m)
            ti = NAT_TYPES.index(tname)
            Bs = ks - 2 * m + 7
            nfull = nr // 2
            nkb = (nr + 1) // 2
            if m % 4 == 0:
                qb = nxt("qt", 2)
                dma(P, "sync", qt[qb][:], qTv[:, :, m * 128:m * 128 + 512], (), ["qt%d" % qb])
            qcol = (m % 4) * 128
            kb_ = nxt("kt", 2)
            dma(P, "sync", kt[kb_][:, :, 0:nr * 64], kTv[:, :, ks * 64:(ks + nr) * 64], (), ["kt%d" % kb_])
            vb = nxt("v1", 2)
            t0 = ks * 64
            dma(P, "sync", v1[vb][:, 0:nfull, :, :], vn1[t0:t0 + nfull * 128].rearrange("(b p) h e -> p b h e", p=128), (), ["v1_%d" % vb])
            if nr % 2:
                dma(P, "sync", v1[vb][0:64, nfull, :, :], vn1[t0 + nfull * 128:t0 + nfull * 128 + 64], (), ["v1_%d" % vb])
            yb = nxt("ys", 2)
            pend = []
            obs = {}
            for h in range(NH):
                hp, hc = h % 2, h // 2
                if tname == "int":
                    TMh, tmk = TMint[:, h, :], "TMint%d" % h
                else:
                    tb = nxt("tmsp", 2)
                    tt(P, "vector", tmsp[tb][:, 0:nr * 64], TT[:, h, Bs * 64:(Bs + nr) * 64], msk[:, ti, 0:nr * 64], ALU.add,
                       ["TT%d" % h, "msk"], ["tmsp%d" % tb])
                    TMh, tmk = tmsp[tb], "tmsp%d" % tb
                sb = nxt("pS", 3)
                if tname == "int":
                    for kb in range(nkb):
                        kp = 128 if kb < nfull else 64
                        out = pS[sb][0:kp, kb * 128:(kb + 1) * 128]
                        mm(P, out, kt[kb_][hp * 64:(hp + 1) * 64, hc, kb * 128:kb * 128 + kp], qt[qb][hp * 64:(hp + 1) * 64, hc, qcol:qcol + 128],
                           True, True, ["kt%d" % kb_, "qt%d" % qb], ["pS%d" % sb])
                    tt(P, "vector", ssb[sb][:, 0:640], pS[sb][:, 0:640], TMT[:, h, 0:640], ALU.add, ["pS%d" % sb, "TMT%d" % h], ["ssb%d" % sb])
                    act(P, pt[sb][:, 0:640], ssb[sb][:, 0:640], AF.Exp, ["ssb%d" % sb], ["pt%d" % sb])
                else:
                    for kb in range(nkb):
                        kp = 128 if kb < nfull else 64
                        out = pS[sb][0:kp, kb * 128:(kb + 1) * 128]
                        mm(P, out, kt[kb_][hp * 64:(hp + 1) * 64, hc, kb * 128:kb * 128 + kp], qt[qb][hp * 64:(hp + 1) * 64, hc, qcol:qcol + 128],
                           True, False, ["kt%d" % kb_, "qt%d" % qb], ["pS%d" % sb])
                        mm(P, out, TMh[:, kb * 128:kb * 128 + kp], ident[:], False, True, [tmk, "ident"], ["pS%d" % sb])
                    act(P, pt[sb][:, 0:nfull * 128], pS[sb][:, 0:nfull * 128], AF.Exp, ["pS%d" % sb], ["pt%d" % sb])
                    if nr % 2:
                        act(P, pt[sb][0:64, nfull * 128:nkb * 128], pS[sb][0:64, nfull * 128:nkb * 128], AF.Exp, ["pS%d" % sb], ["pt%d" % sb])
                while len(pend) > 1:
                    pend.pop(0)()
                if h % 4 == 0:
                    obs[h // 4] = nxt("pO", 2)

                def pv(h=h, sb=sb):
                    ob = obs[h // 4]
                    for kb in range(nkb):
                        kp = 128 if kb < nfull else 64
                        mm(P, pO[ob][:, h % 4, :], pt[sb][0:kp, kb * 128:(kb + 1) * 128], v1[vb][0:kp, kb, h, :], kb == 0, kb == nkb - 1,
                           ["pt%d" % sb, "v1_%d" % vb], ["pO%d" % ob])
                    if h % 4 == 3:
                        rb = nxt("rec", 2)
                        P.op("vector", (lambda o_, i_: (lambda e: e.reciprocal(out=o_, in_=i_)))(rec[rb][:], pO[ob][:, :, 64]),
                             ["pO%d" % ob], ["rec%d" % rb])
                        for hh in range(4):
                            hd = h - 3 + hh
                            if hh % 2:
                                act(P, ys[yb][:, hd * 64:(hd + 1) * 64], pO[ob][:, hh, 0:64], AF.Copy, ["pO%d" % ob, "rec%d" % rb], ["ys%d" % yb],
                                    scale=rec[rb][:, hh:hh + 1])
                            else:
                                ts(P, "vector", ys[yb][:, hd * 64:(hd + 1) * 64], pO[ob][:, hh, 0:64], rec[rb][:, hh:hh + 1], None, ALU.mult, None,
                                   ["pO%d" % ob, "rec%d" % rb], ["ys%d" % yb])
                pend.append(pv)
            while pend:
                pend.pop(0)()
            dma(P, "gpq", y_nat[m * 128:(m + 1) * 128, :], ys[yb][:], ["ys%d" % yb], ["y_nat%d" % yb])
        P.emit()


def _phase_p3(nc, P, c, A):
    D, CH, CN, DFF, T, KD, KM, KF, KG, DC = c.D, c.CH, c.CN, c.DFF, c.T, c.KD, c.KM, c.KF, c.KG, c.DC
    MIXC = c.MIXC
    TT3 = 512
    NB = TT3 // 128
    WK = max(KM, KD, KG)
    x, y, y_hy, y_nat = A["x"], A["y"], A["y_hy"], A["y_nat"]
    Wb_out, Wb_up, Wb_down = A["Wb_out"], A["Wb_up"], A["Wb_down"]
    NSL = KF // KG
    FFS = KG * 128
    with contextlib.ExitStack() as st:
        S = lambda n, sh, dt: st.enter_context(nc.sbuf_tensor(n, sh, dt))
        PS = lambda n, sh, dt: st.enter_context(nc.psum_tensor(n, sh, dt))
        ident = S("ident", [128, 128], BF16)
        dma(P, "sync", ident[:], A["c_ident"], (), ["ident"])
        gfin = S("gfin", [128, D], F32)
        dma(P, "sync", gfin[:], bass.AP(tensor=A["norm_f_g"].tensor, offset=0, ap=[[0, 128], [1, D]]), (), ["gfin"])
        yin = [S("yin%d" % i, [128, MIXC], BF16) for i in range(2)]
        mixb = [S("mixb%d" % i, [128, MIXC], BF16) for i in range(2)]
        actT = S("actT", [128, KD, TT3], BF16)
        mixT = S("mixT", [128, KM, TT3], BF16)
        xres = [S("xres%d" % i, [128, D], F32) for i in range(NB)]
        mbb = [S("mbb%d" % i, [128, D], BF16) for i in range(2)]
        uT = [S("uT%d" % i, [128, KG, TT3], BF16) for i in range(2)]
        rl = [S("rl%d" % i, [128, TT3], F32) for i in range(2)]
        outb = [S("outb%d" % i, [128, D], F32) for i in range(2)]
        wbuf = [S("wbuf%d" % i, [128, WK, 512], BF16) for i in range(3)]
        stt_ = S("stats", [128, 24], F32)
        psA = [PS("psA%d" % i, [128, 4, 128], BF16) for i in range(2)]
        psM = [PS("psM%d" % i, [128, 512], F32) for i in range(6)]
        cnt = {}

        def nxt(k, m):
            v = cnt.get(k, 0)
            cnt[k] = v + 1
            return v % m

        def ev_eng():
            return "scalar" if nxt("ev", 2) else "vector"

        def rstd_of(src_ap, n, skeys, col, junk_ap, junk_key):
            k0, k1 = "st%d" % col, "st%d" % (col + 1)
            act(P, junk_ap, src_ap, AF.Square, skeys, [junk_key, k0], accum_out=stt_[:, col:col + 1])
            act(P, stt_[:, col + 1:col + 2], stt_[:, col:col + 1], AF.Sqrt, [k0], [k1], bias=EPS, scale=1.0 / n)
            P.op("vector", (lambda o_, i_: (lambda e: e.reciprocal(out=o_, in_=i_)))(stt_[:, col + 1:col + 2], stt_[:, col + 1:col + 2]),
                 [k1], [k1])
            return stt_[:, col + 1:col + 2], k1

        def transposes(src, skey, nk, b, dstT=None, dkey="actT"):
            if dstT is None:
                dstT = actT
            for k0 in range(0, nk, 4):
                pa = nxt("psA", 2)
                n_ = min(4, nk - k0)
                for kk in range(n_):
                    tr(P, psA[pa][:, kk, :], src[:, (k0 + kk) * 128:(k0 + kk + 1) * 128], ident[:], [skey, "ident"], ["psA%d" % pa])
                cp(P, ev_eng(), dstT[:, k0:k0 + n_, b * 128:(b + 1) * 128], psA[pa][:, 0:n_, :], ["psA%d" % pa], [dkey])

        def load_w(src3d, nk, ncol, skey):
            wb = nxt("wbuf", 3)
            dma(P, "sync", wbuf[wb][:, 0:nk, 0:ncol], src3d, [skey], ["wbuf%d" % wb])
            return wb

        def stepA1(ti, b):
            r0 = ti * TT3 + b * 128
            yb = b % 2
            dma(P, "sync", yin[yb][:, 0:CH], y_hy[r0:r0 + 128, :], (), ["yin%d" % yb])
            dma(P, "sync", yin[yb][:, CH:MIXC], y_nat[r0:r0 + 128, :], (), ["yin%d" % yb])
            r_h, kh_ = rstd_of(yin[yb][:, 0:CH], CH, ["yin%d" % yb], 12 + 4 * yb, mixb[yb][:, 0:CH], "mixb%d" % yb)
            r_n, kn_ = rstd_of(yin[yb][:, CH:MIXC], CN, ["yin%d" % yb], 14 + 4 * yb, mixb[yb][:, CH:MIXC], "mixb%d" % yb)
            ts(P, "vector", mixb[yb][:, 0:CH], yin[yb][:, 0:CH], r_h, None, ALU.mult, None, ["yin%d" % yb, kh_], ["mixb%d" % yb])
            act(P, mixb[yb][:, CH:MIXC], yin[yb][:, CH:MIXC], AF.Copy, ["yin%d" % yb, kn_], ["mixb%d" % yb], scale=r_n)

        def stepA2(ti, b):
            yb = b % 2
            transposes(mixb[yb], "mixb%d" % yb, KM, b, mixT, "mixT")

        for b in range(NB):
            stepA1(0, b)
            stepA2(0, b)
        NTL3 = T // TT3
        pre_wb = [None]
        for ti in range(NTL3):
            t0 = ti * TT3
            for b in range(NB):
                dma(P, "sync", xres[b][:], x[t0 + b * 128:t0 + (b + 1) * 128, :], (), ["xres%d" % b])
            for cc in range(D // DC):
                if cc == 0 and pre_wb[0] is not None:
                    wb = pre_wb[0]
                    pre_wb[0] = None
                else:
                    wb = load_w(Wb_out[:, cc * DC:(cc + 1) * DC].rearrange("(k p) c -> p k c", p=128), KM, DC, "Wb_out")
                for b in range(NB):
                    pm = nxt("psM", 6)
                    for k in range(KM):
                        mm(P, psM[pm][:, 0:DC], mixT[:, k, b * 128:(b + 1) * 128], wbuf[wb][:, k, 0:DC], k == 0, k == KM - 1,
                           ["mixT", "wbuf%d" % wb], ["psM%d" % pm])
                    sl = slice(cc * DC, (cc + 1) * DC)
                    tt(P, "vector", xres[b][:, sl], psM[pm][:, 0:DC], xres[b][:, sl], ALU.add, ["psM%d" % pm, "xres%d" % b], ["xres%d" % b])
            for b in range(NB):
                mbi = nxt("mbb", 2)
                r_m, km_ = rstd_of(xres[b][:], D, ["xres%d" % b], 4, mbb[mbi][:], "mbb%d" % mbi)
                if b % 2:
                    act(P, mbb[mbi][:], xres[b][:], AF.Copy, ["xres%d" % b, km_], ["mbb%d" % mbi], scale=r_m)
                else:
                    ts(P, "vector", mbb[mbi][:], xres[b][:], r_m, None, ALU.mult, None, ["xres%d" % b, km_], ["mbb%d" % mbi])
                transposes(mbb[mbi], "mbb%d" % mbi, KD, b)
            for s_ in range(NSL):
                ub = nxt("uT", 2)
                for sub in range(FFS // 512):
                    f0 = s_ * FFS + sub * 512
                    wb = load_w(Wb_up[:, f0:f0 + 512].rearrange("(k p) c -> p k c", p=128), KD, 512, "Wb_up")
                    for j in range(4):
                        pm = nxt("psM", 6)
                        for k in range(KD):
                            mm(P, psM[pm][:], wbuf[wb][:, k, j * 128:(j + 1) * 128], actT[:, k, :], k == 0, k == KD - 1,
                               ["actT", "wbuf%d" % wb], ["psM%d" % pm])
                        rb = nxt("rl", 2)
                        act(P, rl[rb][:], psM[pm][:], AF.Relu, ["psM%d" % pm], ["rl%d" % rb])
                        tt(P, "gpsimd", uT[ub][:, sub * 4 + j, :], rl[rb][:], rl[rb][:], ALU.mult, ["rl%d" % rb], ["uT%d_%d" % (ub, sub * 4 + j)])
                for cc in range(D // DC):
                    src = Wb_down[s_ * FFS:(s_ + 1) * FFS, cc * DC:(cc + 1) * DC].rearrange("(k p) c -> p k c", p=128)
                    wb = load_w(src, KG, DC, "Wb_down")
                    for b in range(NB):
                        pm = nxt("psM", 6)
                        for k in range(KG):
                            mm(P, psM[pm][:, 0:DC], uT[ub][:, k, b * 128:(b + 1) * 128], wbuf[wb][:, k, 0:DC], k == 0, k == KG - 1,
                               ["uT%d_%d" % (ub, k), "wbuf%d" % wb], ["psM%d" % pm])
                        sl = slice(cc * DC, (cc + 1) * DC)
                        tt(P, "vector", xres[b][:, sl], psM[pm][:, 0:DC], xres[b][:, sl], ALU.add, ["psM%d" % pm, "xres%d" % b], ["xres%d" % b])
                if ti + 1 < NTL3:
                    stp = [st_ for st_ in range(NB + 1) if (st_ * NSL) // (NB + 1) == s_] if NSL >= 2 else (list(range(NB + 1)) if s_ == 0 else [])
                    for st_ in stp:
                        if st_ >= 1:
                            stepA2(ti + 1, st_ - 1)
                        if st_ < NB:
                            stepA1(ti + 1, st_)
            if ti + 1 < NTL3:
                pre_wb[0] = load_w(Wb_out[:, 0:DC].rearrange("(k p) c -> p k c", p=128), KM, DC, "Wb_out")
            for b in range(NB):
                mbi = nxt("mbb", 2)
                r_f, kf_ = rstd_of(xres[b][:], D, ["xres%d" % b], 6 + 2 * (b % 2), mbb[mbi][:], "mbb%d" % mbi)
                ob_ = nxt("outb", 2)
                stt(P, "vector", outb[ob_][:], xres[b][:], r_f, gfin[:], ALU.mult, ALU.mult, ["xres%d" % b, kf_, "gfin"], ["outb%d" % ob_])
                dma(P, "gpq", y[t0 + b * 128:t0 + (b + 1) * 128, :], outb[ob_][:], ["outb%d" % ob_], ["y%d" % ob_])
        P.emit()


_CACHE = {}


def _consts(cfg, ctype):
    f32 = np.float32
    C = {}
    C["c_ident"] = np.eye(128, dtype=f32).astype(NPBF)
    ft = fft_tables(ctype)
    C["c_Enat"], C["c_Edat"], C["c_G"], C["c_W"] = ft["Enat"], ft["Edat"], ft["G"], ft["W"]
    fc = filter_consts(ctype)
    C["c_zT"], C["c_tcol"], C["c_mf"], C["c_mb"] = fc["zT"], fc["tcol"], fc["mf"], fc["mb"]
    maxd = math.log(1e-2) / 0.3
    mind = math.log(1e-2) / 1.5
    C["c_nad"] = (-np.abs(np.linspace(mind, maxd, cfg.CH, dtype=f32))).astype(f32)
    C["c_mask"] = nat_masks(ctype)
    C["c_flag"] = np.array([float(ctype)], f32)
    return C


def kernel(x_prompt, x_sample, norm_mix_g, w_in, hy_conv_w, hy_conv_b, hy_pe_w1, hy_pe_b1, hy_pe_w2, hy_pe_b2,
           hy_pe_w3, hy_pe_b3, hy_pe_freq, hy_pe_w4, hy_bias, nat_rpb, gnorm_hy, gnorm_nat, w_out, norm_mlp_g,
           w_up, w_down, norm_f_g):
    cfg = FULL
    f32 = np.float32
    A = lambda v: np.ascontiguousarray(np.asarray(v), dtype=f32)
    x_prompt, x_sample = A(x_prompt), A(x_sample)
    if "nc" not in _CACHE:
        _CACHE["nc"] = build_program(cfg)
        _CACHE["c"] = [_consts(cfg, 0), _consts(cfg, 1)]
    nc = _CACHE["nc"]
    idx, ok = nat_tt_index()
    rpb = A(nat_rpb)[0].reshape(cfg.NH, -1)
    ntt = np.where(ok[None], rpb[:, idx], f32(0.0)).astype(f32)
    shared = {
        "norm_mix_g": A(norm_mix_g)[0], "w_in": A(w_in)[0], "hy_conv_w": A(hy_conv_w)[0], "hy_conv_b": A(hy_conv_b)[0],
        "pe_w1": A(hy_pe_w1)[0], "pe_b1": A(hy_pe_b1)[0], "pe_w2": A(hy_pe_w2)[0], "pe_b2": A(hy_pe_b2)[0],
        "pe_w3": A(hy_pe_w3)[0], "pe_b3": A(hy_pe_b3)[0], "pe_freq": A(hy_pe_freq)[0], "pe_w4": A(hy_pe_w4)[0],
        "hy_bias": A(hy_bias)[0], "nat_tt": ntt,
        "gnorm": np.concatenate([A(gnorm_hy)[0], A(gnorm_nat)[0]]), "w_out": A(w_out)[0],
        "norm_mlp_g": A(norm_mlp_g)[0], "w_up": A(w_up)[0], "w_down": A(w_down)[0], "norm_f_g": A(norm_f_g),
    }
    in_maps = []
    for core in range(8):
        d = dict(shared)
        if core < 4:
            d["x"] = x_sample[core]
            d.update(_CACHE["c"][0])
        else:
            j = core - 4
            d["x"] = x_prompt[2 * j:2 * j + 2].reshape(cfg.T, cfg.D)
            d.update(_CACHE["c"][1])
        in_maps.append(d)
    res = run_bass_kernel_spmd(nc, in_maps, core_ids=list(range(8)))
    y_sample = np.stack([res.results[cidx]["y"] for cidx in range(4)], 0).astype(f32)
    y_prompt = np.concatenate([res.results[4 + j]["y"].reshape(2, 4096, cfg.D) for j in range(4)], 0).astype(f32)
    return (y_prompt, y_sample)
```
